# Optimizing a Trainium2 kernel written in Bass

```python
import math
import jax
import jax.numpy as jnp
from jax import lax
import numpy as np

D_MODEL = 2048
BATCH = 4
SEQ = 4096
DEPTH = 2

GRID_W = 64
CTX_LEN = 256
F32 = jnp.float32

N_BRANCH = 4
BRANCH_W = D_MODEL // N_BRANCH
Q_BLOCK = 128
ROPE_THETA = 10000.0
LN_EPS = 1e-6
RMS_EPS = 1e-6
DEEPNORM_ALPHA = (2 * DEPTH) ** 0.25
DEEPNORM_BETA = (8 * DEPTH) ** -0.25

MLA_HEADS = 4
MLA_Q_RANK = 448
MLA_KV_RANK = 128
MLA_NOPE = 128
MLA_ROPE = 64
MLA_V = BRANCH_W // MLA_HEADS
MLA_COLS = MLA_Q_RANK + MLA_KV_RANK + MLA_ROPE

GQA_HEADS = 4
GQA_KV_HEADS = 2
GQA_HD = BRANCH_W // GQA_HEADS
GQA_COLS = (GQA_HEADS + 2 * GQA_KV_HEADS) * GQA_HD

HY_W = BRANCH_W
HY_SHORT = 3
HY_EMB = 33
HY_BANDS = (HY_EMB - 1) // 2
HY_FFN = 64
HY_MIN_DECAY = math.log(1e-2) / 1.5
HY_MAX_DECAY = math.log(1e-2) / 0.3
HY_COLS = 3 * HY_W

MB_INNER = BRANCH_W
MB_HEADDIM = 64
MB_HEADS = MB_INNER // MB_HEADDIM
MB_GROUPS = 2
MB_STATE = 128
MB_CONV = 3
MB_CHUNK = 128
MB_CONV_CH = MB_INNER + 2 * MB_GROUPS * MB_STATE
MB_COLS = MB_INNER + MB_CONV_CH + 2 * MB_HEADS

IN_COLS = MLA_COLS + GQA_COLS + HY_COLS + MB_COLS
IN_SPLITS = [MLA_COLS, MLA_COLS + GQA_COLS, MLA_COLS + GQA_COLS + HY_COLS]

FFN_HIDDEN = ((8 * D_MODEL // 3 + 255) // 256) * 256

kernel_name = "hybrid_mla_gqa_hyena_ssd_dit_block"


def standardize(x):
    xf = x.astype(F32)
    mu = jnp.mean(xf, -1, keepdims=True)
    var = jnp.mean(jnp.square(xf - mu), -1, keepdims=True)
    return (xf - mu) * lax.rsqrt(var + LN_EPS)


def layer_norm(x, g, b):
    return (standardize(x) * g.astype(F32) + b.astype(F32)).astype(x.dtype)


def modulate(x, shift, scale):
    return (standardize(x) * (1.0 + scale.astype(F32)) + shift.astype(F32)).astype(x.dtype)


def rms_norm(x, g):
    xf = x.astype(F32)
    y = xf * lax.rsqrt(jnp.mean(jnp.square(xf), -1, keepdims=True) + RMS_EPS)
    return (y * g.astype(F32)).astype(x.dtype)


def axial_rope(n_tokens, rot_dim):
    rows = n_tokens // GRID_W
    row = jnp.repeat(jnp.arange(rows, dtype=F32), GRID_W)
    col = jnp.tile(jnp.arange(GRID_W, dtype=F32), rows)
    n_freq = rot_dim // 4
    inv_freq = ROPE_THETA ** (-jnp.arange(n_freq, dtype=F32) / n_freq)
    ang = jnp.concatenate([row[:, None] * inv_freq, col[:, None] * inv_freq], axis=-1)
    return jnp.cos(ang), jnp.sin(ang)


def apply_rope(x, cos, sin):
    xf = x.astype(F32).reshape(x.shape[:-1] + (x.shape[-1] // 2, 2))
    x0, x1 = xf[..., 0], xf[..., 1]
    c = cos[None, :, None, :]
    s = sin[None, :, None, :]
    out = jnp.stack([x0 * c - x1 * s, x0 * s + x1 * c], axis=-1)
    return out.reshape(x.shape).astype(x.dtype)


def block_attention(q, k, v, scale):
    b, lq = q.shape[:2]
    nb = lq // Q_BLOCK
    qb = q.reshape((b, nb, Q_BLOCK) + q.shape[2:]).swapaxes(0, 1)

    def one_block(qblk):
        s = jnp.einsum('bqhgd,bkhd->bhgqk', qblk, k, preferred_element_type=F32) * scale
        p = jax.nn.softmax(s, axis=-1).astype(v.dtype)
        return jnp.einsum('bhgqk,bkhd->bqhgd', p, v)

    out = lax.map(one_block, qb)
    return out.swapaxes(0, 1).reshape((b, lq) + out.shape[3:])


def depthwise_conv(u, w, b):
    k, ch = w.shape
    pad = k // 2
    y = lax.conv_general_dilated(u, w[:, None, :].astype(u.dtype), window_strides=(1,),
                                 padding=[(pad, pad)], dimension_numbers=('NWC', 'WIO', 'NWC'),
                                 feature_group_count=ch)
    return y + b.astype(u.dtype)


def mla_qkv(p, q_norm_g, kv_norm_g, w_uq, w_ukv, rope):
    b, n = p.shape[:2]
    c_q, c_kv, k_rot = jnp.split(p, [MLA_Q_RANK, MLA_Q_RANK + MLA_KV_RANK], axis=-1)
    q = (rms_norm(c_q, q_norm_g) @ w_uq).reshape(b, n, MLA_HEADS, MLA_NOPE + MLA_ROPE)
    kv = (rms_norm(c_kv, kv_norm_g) @ w_ukv).reshape(b, n, MLA_HEADS, MLA_NOPE + MLA_V)
    q_nope, q_rot = jnp.split(q, [MLA_NOPE], axis=-1)
    k_nope, v = jnp.split(kv, [MLA_NOPE], axis=-1)
    k_rot = k_rot[:, :, None, :]
    if rope is not None:
        q_rot = apply_rope(q_rot, *rope)
        k_rot = apply_rope(k_rot, *rope)
    q = jnp.concatenate([q_nope, q_rot], axis=-1)
    k = jnp.concatenate([k_nope, jnp.broadcast_to(k_rot, (b, n, MLA_HEADS, MLA_ROPE))], axis=-1)
    return q, k, v


def mla_mixer(p_ctx, p_lat, q_norm_g, kv_norm_g, w_uq, w_ukv, need_ctx):
    scale = (MLA_NOPE + MLA_ROPE) ** -0.5
    b, n = p_lat.shape[:2]
    q_c, k_c, v_c = mla_qkv(p_ctx, q_norm_g, kv_norm_g, w_uq, w_ukv, None)
    q_l, k_l, v_l = mla_qkv(p_lat, q_norm_g, kv_norm_g, w_uq, w_ukv, axial_rope(n, MLA_ROPE))
    k_all = jnp.concatenate([k_c, k_l], axis=1)
    v_all = jnp.concatenate([v_c, v_l], axis=1)
    o_lat = block_attention(q_l[:, :, :, None, :], k_all, v_all, scale).reshape(b, n, MLA_HEADS * MLA_V)
    o_ctx = None
    if need_ctx:
        o_ctx = block_attention(q_c[:, :, :, None, :], k_c, v_c, scale).reshape(b, p_ctx.shape[1], MLA_HEADS * MLA_V)
    return o_ctx, o_lat


def gqa_qkv(p, q_norm_g, k_norm_g, rope):
    b, n = p.shape[:2]
    q, k, v = jnp.split(p, [GQA_HEADS * GQA_HD, (GQA_HEADS + GQA_KV_HEADS) * GQA_HD], axis=-1)
    q = rms_norm(q.reshape(b, n, GQA_HEADS, GQA_HD), q_norm_g)
    k = rms_norm(k.reshape(b, n, GQA_KV_HEADS, GQA_HD), k_norm_g)
    v = v.reshape(b, n, GQA_KV_HEADS, GQA_HD)
    if rope is not None:
        q = apply_rope(q, *rope)
        k = apply_rope(k, *rope)
    return q.reshape(b, n, GQA_KV_HEADS, GQA_HEADS // GQA_KV_HEADS, GQA_HD), k, v


def gqa_mixer(p_ctx, p_lat, q_norm_g, k_norm_g, need_ctx):
    scale = GQA_HD ** -0.5
    b, n = p_lat.shape[:2]
    q_c, k_c, v_c = gqa_qkv(p_ctx, q_norm_g, k_norm_g, None)
    q_l, k_l, v_l = gqa_qkv(p_lat, q_norm_g, k_norm_g, axial_rope(n, GQA_HD))
    k_all = jnp.concatenate([k_c, k_l], axis=1)
    v_all = jnp.concatenate([v_c, v_l], axis=1)
    o_lat = block_attention(q_l, k_all, v_all, scale).reshape(b, n, GQA_HEADS * GQA_HD)
    o_ctx = None
    if need_ctx:
        o_ctx = block_attention(q_c, k_c, v_c, scale).reshape(b, p_ctx.shape[1], GQA_HEADS * GQA_HD)
    return o_ctx, o_lat


def hyena_filters(n, w1, b1, w2, b2, w3, freq):
    t = jnp.linspace(0.0, 1.0, n, dtype=F32)[:, None]
    omega = 2.0 * math.pi * jnp.arange(n, dtype=F32) / n
    bands = jnp.linspace(1e-4, HY_BANDS - 1, HY_BANDS, dtype=F32)
    ang = omega[:, None] * bands[None, :]
    feats = jnp.concatenate([t, jnp.cos(ang), -jnp.sin(ang)], axis=-1)
    fr = freq.astype(F32)
    hdn = jnp.sin(fr * (feats @ w1.astype(F32) + b1.astype(F32)))
    hdn = jnp.sin(fr * (hdn @ w2.astype(F32) + b2.astype(F32)))
    filt = hdn @ w3.astype(F32)
    deltas = jnp.abs(jnp.linspace(HY_MIN_DECAY, HY_MAX_DECAY, HY_W, dtype=F32))
    window = jnp.exp(-t * deltas[None, :])
    h_fwd, h_bwd = jnp.split(filt, 2, axis=-1)
    return h_fwd * window, h_bwd * window


def bidir_long_conv(u, h_fwd, h_bwd):
    n, ch = h_fwd.shape
    k2 = jnp.concatenate([h_fwd, jnp.zeros((1, ch), F32), h_bwd[1:][::-1]], axis=0)
    u_f = jnp.fft.rfft(u.astype(F32), n=2 * n, axis=1)
    k_f = jnp.fft.rfft(k2, n=2 * n, axis=0)
    return jnp.fft.irfft(u_f * k_f[None], n=2 * n, axis=1)[:, :n]


def hyena_seq(p, conv_w, conv_b, w1, b1, w2, b2, w3, freq, skip):
    u = depthwise_conv(p, conv_w, conv_b)
    x0, x1, v = jnp.split(u, 3, axis=-1)
    v = (v * x1).astype(F32)
    h_fwd, h_bwd = hyena_filters(p.shape[1], w1, b1, w2, b2, w3, freq)
    v = bidir_long_conv(v, h_fwd, h_bwd) + v * skip.astype(F32)
    return (x0.astype(F32) * v).astype(p.dtype)


def hyena_mixer(p_ctx, p_lat, conv_w, conv_b, w1, b1, w2, b2, w3, freq, skip, need_ctx):
    o_lat = hyena_seq(p_lat, conv_w, conv_b, w1, b1, w2, b2, w3, freq, skip)
    o_ctx = hyena_seq(p_ctx, conv_w, conv_b, w1, b1, w2, b2, w3, freq, skip) if need_ctx else None
    return o_ctx, o_lat


def ssd_scan(x, dt, a, bm, cm, init_state):
    b, n, h, p = x.shape
    nc = n // MB_CHUNK

    def chunk(t):
        return t.reshape((b, nc, MB_CHUNK) + t.shape[2:])

    xdt = chunk(x.astype(F32) * dt[..., None])
    da = chunk(dt * a)
    bc = chunk(bm.astype(F32))
    cc = chunk(cm.astype(F32))
    cum = jnp.cumsum(da, axis=2)
    lower = jnp.tril(jnp.ones((MB_CHUNK, MB_CHUNK), bool))
    seg = cum[:, :, :, None, :] - cum[:, :, None, :, :]
    decay_ls = jnp.exp(jnp.where(lower[None, None, :, :, None], seg, -jnp.inf))
    scores = jnp.einsum('bclhn,bcshn->bclsh', cc, bc) * decay_ls
    y_diag = jnp.einsum('bclsh,bcshp->bclhp', scores, xdt)
    decay_to_end = jnp.exp(cum[:, :, -1:, :] - cum)
    chunk_states = jnp.einsum('bclhn,bclh,bclhp->bchpn', bc, decay_to_end, xdt)
    chunk_decay = jnp.exp(cum[:, :, -1, :])

    def step(state, inp):
        st, dec = inp
        return state * dec[:, :, None, None] + st, state

    final, s_in = lax.scan(step, init_state.astype(F32),
                           (chunk_states.swapaxes(0, 1), chunk_decay.swapaxes(0, 1)))
    s_in = s_in.swapaxes(0, 1)
    y_off = jnp.einsum('bclhn,bchpn,bclh->bclhp', cc, s_in, jnp.exp(cum))
    return (y_diag + y_off).reshape(b, n, h, p), final


def mamba_prep(p, conv_w, conv_b, dt_bias):
    b, n = p.shape[:2]
    z, xbc, dt = jnp.split(p, [MB_INNER, MB_INNER + MB_CONV_CH], axis=-1)
    xbc = jax.nn.silu(depthwise_conv(xbc, conv_w, conv_b))
    xs, bm, cm = jnp.split(xbc, [MB_INNER, MB_INNER + MB_GROUPS * MB_STATE], axis=-1)
    rep = MB_HEADS // MB_GROUPS
    xs = xs.reshape(b, n, MB_HEADS, MB_HEADDIM)
    bm = jnp.repeat(bm.reshape(b, n, MB_GROUPS, MB_STATE), rep, axis=2)
    cm = jnp.repeat(cm.reshape(b, n, MB_GROUPS, MB_STATE), rep, axis=2)
    dt = jax.nn.softplus(dt.astype(F32).reshape(b, n, 2, MB_HEADS) + dt_bias.astype(F32))
    return z, xs, bm, cm, dt


def mamba_mixer(p_ctx, p_lat, conv_w, conv_b, a_log, dt_bias, d_skip, norm_g, need_ctx):
    a = -jnp.exp(a_log.astype(F32))
    zc, xc, bc, cc, dtc = mamba_prep(p_ctx, conv_w, conv_b, dt_bias)
    zl, xl, bl, cl, dtl = mamba_prep(p_lat, conv_w, conv_b, dt_bias)
    zero = jnp.zeros((p_lat.shape[0], MB_HEADS, MB_HEADDIM, MB_STATE), F32)

    def flip(t):
        return jnp.flip(t, axis=1)

    yc_f, sc_f = ssd_scan(xc, dtc[:, :, 0], a[0], bc, cc, zero)
    yl_f, _ = ssd_scan(xl, dtl[:, :, 0], a[0], bl, cl, sc_f)
    yc_b, sc_b = ssd_scan(flip(xc), flip(dtc[:, :, 1]), a[1], flip(bc), flip(cc), zero)
    yl_b, _ = ssd_scan(flip(xl), flip(dtl[:, :, 1]), a[1], flip(bl), flip(cl), sc_b)

    def finish(yf, yb, xs, z):
        y = yf + flip(yb) + xs.astype(F32) * d_skip.astype(F32)[:, None]
        y = y.reshape(z.shape).astype(z.dtype)
        return rms_norm(y * jax.nn.silu(z), norm_g)

    o_lat = finish(yl_f, yl_b, xl, zl)
    o_ctx = finish(yc_f, yc_b, xc, zc) if need_ctx else None
    return o_ctx, o_lat


def merge_branches(h, outs, w_gate, b_gate, w_branch, w_out):
    acc = None
    for i, o in enumerate(outs):
        term = jax.nn.sigmoid(h @ w_gate[i] + b_gate[i]) * (o @ w_branch[i])
        acc = term if acc is None else acc + term
    return acc @ w_out


def swiglu(h, w_in, w_out):
    up, gate = jnp.split(h @ w_in, 2, axis=-1)
    return (jax.nn.silu(gate) * up) @ w_out


def setup_inputs(seed: int = 0) -> dict:
    key = jax.random.key(seed)
    ks = iter(jax.random.split(key, 64))
    L, D = DEPTH, D_MODEL

    def nrm(shape, scale):
        return jax.random.normal(next(ks), shape, F32) * scale

    def gain(shape):
        return 1.0 + nrm(shape, 0.02)

    x = nrm((BATCH, SEQ, D), 1.0)
    c = nrm((BATCH, D), 1.0)
    ctx = nrm((BATCH, CTX_LEN, D), 1.0)
    c_ctx = nrm((D,), 1.0)
    w_ada = nrm((L, D, 6 * D), 0.5 * D ** -0.5)
    b_ada = nrm((L, 6 * D), 0.01)
    w_in = nrm((L, D, IN_COLS), D ** -0.5)
    mla_q_norm = gain((L, MLA_Q_RANK))
    mla_kv_norm = gain((L, MLA_KV_RANK))
    mla_w_uq = nrm((L, MLA_Q_RANK, MLA_HEADS * (MLA_NOPE + MLA_ROPE)), MLA_Q_RANK ** -0.5)
    mla_w_ukv = nrm((L, MLA_KV_RANK, MLA_HEADS * (MLA_NOPE + MLA_V)), MLA_KV_RANK ** -0.5)
    gqa_q_norm = gain((L, GQA_HD))
    gqa_k_norm = gain((L, GQA_HD))
    hy_conv_w = nrm((L, HY_SHORT, HY_COLS), HY_SHORT ** -0.5)
    hy_conv_b = nrm((L, HY_COLS), 0.02)
    hy_w1 = nrm((L, HY_EMB, HY_FFN), HY_EMB ** -0.5)
    hy_b1 = nrm((L, HY_FFN), 0.02)
    hy_w2 = nrm((L, HY_FFN, HY_FFN), HY_FFN ** -0.5)
    hy_b2 = nrm((L, HY_FFN), 0.02)
    hy_w3 = nrm((L, HY_FFN, 2 * HY_W), 0.02)
    hy_freq = gain((L, HY_FFN))
    hy_skip = nrm((L, HY_W), 1.0)
    mb_conv_w = nrm((L, MB_CONV, MB_CONV_CH), MB_CONV ** -0.5)
    mb_conv_b = nrm((L, MB_CONV_CH), 0.02)
    mb_a_log = jnp.log(jax.random.uniform(next(ks), (L, 2, MB_HEADS), F32, 1.0, 16.0))
    dt0 = jnp.exp(jax.random.uniform(next(ks), (L, 2, MB_HEADS), F32, math.log(1e-3), math.log(1e-1)))
    mb_dt_bias = dt0 + jnp.log(-jnp.expm1(-dt0))
    mb_d = gain((L, MB_HEADS))
    mb_norm = gain((L, MB_INNER))
    w_mgate = nrm((L, N_BRANCH, D, D), D ** -0.5)
    b_mgate = nrm((L, N_BRANCH, D), 0.02)
    w_branch = nrm((L, N_BRANCH, BRANCH_W, D), BRANCH_W ** -0.5)
    w_out = nrm((L, D, D), DEEPNORM_BETA * D ** -0.5)
    ln1_g = gain((L, D))
    ln1_b = nrm((L, D), 0.02)
    w_ffn_in = nrm((L, D, 2 * FFN_HIDDEN), D ** -0.5)
    w_ffn_out = nrm((L, FFN_HIDDEN, D), DEEPNORM_BETA * FFN_HIDDEN ** -0.5)
    ln2_g = gain((L, D))
    ln2_b = nrm((L, D), 0.02)
    return {"x": x, "c": c, "ctx": ctx, "c_ctx": c_ctx, "w_ada": w_ada, "b_ada": b_ada, "w_in": w_in,
            "mla_q_norm": mla_q_norm, "mla_kv_norm": mla_kv_norm, "mla_w_uq": mla_w_uq, "mla_w_ukv": mla_w_ukv,
            "gqa_q_norm": gqa_q_norm, "gqa_k_norm": gqa_k_norm,
            "hy_conv_w": hy_conv_w, "hy_conv_b": hy_conv_b, "hy_w1": hy_w1, "hy_b1": hy_b1, "hy_w2": hy_w2,
            "hy_b2": hy_b2, "hy_w3": hy_w3, "hy_freq": hy_freq, "hy_skip": hy_skip,
            "mb_conv_w": mb_conv_w, "mb_conv_b": mb_conv_b, "mb_a_log": mb_a_log, "mb_dt_bias": mb_dt_bias,
            "mb_d": mb_d, "mb_norm": mb_norm,
            "w_mgate": w_mgate, "b_mgate": b_mgate, "w_branch": w_branch, "w_out": w_out,
            "ln1_g": ln1_g, "ln1_b": ln1_b, "w_ffn_in": w_ffn_in, "w_ffn_out": w_ffn_out,
            "ln2_g": ln2_g, "ln2_b": ln2_b}


def reference(x, c, ctx, c_ctx, w_ada, b_ada, w_in, mla_q_norm, mla_kv_norm, mla_w_uq, mla_w_ukv,
              gqa_q_norm, gqa_k_norm, hy_conv_w, hy_conv_b, hy_w1, hy_b1, hy_w2, hy_b2, hy_w3, hy_freq,
              hy_skip, mb_conv_w, mb_conv_b, mb_a_log, mb_dt_bias, mb_d, mb_norm, w_mgate, b_mgate,
              w_branch, w_out, ln1_g, ln1_b, w_ffn_in, w_ffn_out, ln2_g, ln2_b):
    z = ctx
    for l in range(DEPTH):
        need_ctx = l < DEPTH - 1
        mod = jax.nn.silu(c) @ w_ada[l] + b_ada[l]
        mod_c = jax.nn.silu(c_ctx) @ w_ada[l] + b_ada[l]
        sh1, sc1, g1, sh2, sc2, g2 = jnp.split(mod[:, None, :], 6, axis=-1)
        csh1, csc1, cg1, csh2, csc2, cg2 = jnp.split(mod_c, 6, axis=-1)

        h = modulate(x, sh1, sc1)
        hc = modulate(z, csh1, csc1)
        pa, pb, pcv, pd = jnp.split(h @ w_in[l], IN_SPLITS, axis=-1)
        pa_c, pb_c, pcv_c, pd_c = jnp.split(hc @ w_in[l], IN_SPLITS, axis=-1)

        oa_c, oa = mla_mixer(pa_c, pa, mla_q_norm[l], mla_kv_norm[l], mla_w_uq[l], mla_w_ukv[l], need_ctx)
        ob_c, ob = gqa_mixer(pb_c, pb, gqa_q_norm[l], gqa_k_norm[l], need_ctx)
        oc_c, oc = hyena_mixer(pcv_c, pcv, hy_conv_w[l], hy_conv_b[l], hy_w1[l], hy_b1[l], hy_w2[l],
                               hy_b2[l], hy_w3[l], hy_freq[l], hy_skip[l], need_ctx)
        od_c, od = mamba_mixer(pd_c, pd, mb_conv_w[l], mb_conv_b[l], mb_a_log[l], mb_dt_bias[l],
                               mb_d[l], mb_norm[l], need_ctx)

        y = merge_branches(h, (oa, ob, oc, od), w_mgate[l], b_mgate[l], w_branch[l], w_out[l])
        x_next = layer_norm(DEEPNORM_ALPHA * x + g1 * y, ln1_g[l], ln1_b[l])
        f = swiglu(modulate(x_next, sh2, sc2), w_ffn_in[l], w_ffn_out[l])
        x_next = layer_norm(DEEPNORM_ALPHA * x_next + g2 * f, ln2_g[l], ln2_b[l])

        if need_ctx:
            yc = merge_branches(hc, (oa_c, ob_c, oc_c, od_c), w_mgate[l], b_mgate[l], w_branch[l], w_out[l])
            z = layer_norm(DEEPNORM_ALPHA * z + cg1 * yc, ln1_g[l], ln1_b[l])
            fc = swiglu(modulate(z, csh2, csc2), w_ffn_in[l], w_ffn_out[l])
            z = layer_norm(DEEPNORM_ALPHA * z + cg2 * fc, ln2_g[l], ln2_b[l])
        x = x_next
    return x
```

```python
import numpy as np
import concourse.bass as bass
import concourse.mybir as mybir

F32 = mybir.dt.float32
BF16 = mybir.dt.bfloat16
AF = mybir.ActivationFunctionType
ALU = mybir.AluOpType
AX = mybir.AxisListType

N_DMA_SEMS = 4


class Buf:
    __slots__ = ("name", "w", "r")

    def __init__(self, name):
        self.name = name
        self.w = {}
        self.r = {}


class Prog:
    def __init__(self, nc):
        self.nc = nc
        self.eng_handles = {"pe": nc.tensor, "dve": nc.vector, "act": nc.scalar,
                            "pool": nc.gpsimd, "sp": nc.sync}
        self.ops = {k: [] for k in self.eng_handles}
        self.seq = {k: 0 for k in self.eng_handles}
        self.known = {k: {} for k in self.eng_handles}
        self.sem_names = []
        for k in self.eng_handles:
            self.sem_names.append(("c", k))
        self.dma_rr = {}
        self.dma_cnt = {}
        self.dma_last = {}
        for q in ("sp", "pool", "act"):
            self.dma_rr[q] = 0
            for j in range(N_DMA_SEMS):
                key = ("d", q, j)
                self.sem_names.append(key)
                self.dma_cnt[key] = 0
                self.dma_last[key] = None
        self.sems = {}
        self.n_wait = 0
        self.n_ops = 0

    def _need(self, eng, reads, writes):
        deps = {}

        def add(d, war=False):
            for key, (val, snap) in d.items():
                if key == ("c", eng) and (war or eng == "pe"):
                    continue
                if key not in deps or deps[key][0] < val:
                    deps[key] = (val, snap)
        for b in reads:
            add(b.w)
        for b in writes:
            add(b.w)
            add(b.r, war=True)
        kn = self.known[eng]
        waits = []
        for key, (val, snap) in deps.items():
            if kn.get(key, 0) >= val:
                continue
            waits.append((key, val))
            kn[key] = val
            for k2, v2 in snap.items():
                if kn.get(k2, 0) < v2:
                    kn[k2] = v2
        return waits

    def _record(self, eng, key, val, reads, writes, partial):
        ev = (val, dict(self.known[eng]))
        for b in reads:
            old = b.r.get(key)
            if old is None or old[0] < val:
                b.r[key] = ev
        for b in writes:
            if partial:
                b.w[key] = ev
            else:
                b.w = {key: ev}
                b.r = {}

    def op(self, eng, fn, reads=(), writes=(), partial=False):
        waits = self._need(eng, reads, writes)
        self.seq[eng] += 1
        val = self.seq[eng]
        key = ("c", eng)
        self.ops[eng].append((waits, fn, key, 1))
        self._record(eng, key, val, reads, writes, partial)
        self.n_ops += 1
        self.n_wait += len(waits)

    def I(self, eng, name, reads=(), writes=(), partial=False, **kw):
        self.op(eng, (lambda e, name=name, kw=kw: getattr(e, name)(**kw)), reads=reads, writes=writes, partial=partial)

    def dma(self, q, out, in_, reads=(), writes=(), partial=True, **kw):
        j = self.dma_rr[q]
        self.dma_rr[q] = (j + 1) % N_DMA_SEMS
        key = ("d", q, j)
        waits = self._need(q, reads, writes)
        prev = self.dma_cnt[key]
        if prev > 0 and self.known[q].get(key, 0) < prev:
            waits.append((key, prev))
            self.known[q][key] = prev
        val = prev + 16
        self.dma_cnt[key] = val

        def fn(e, out=out, in_=in_, kw=kw):
            return e.dma_start(out=out, in_=in_, **kw)
        self.ops[q].append((waits, fn, key, 16))
        self._record(q, key, val, reads, writes, partial)
        self.n_ops += 1
        self.n_wait += len(waits)

    def finish(self, bufs, eng="sp"):
        waits = self._need(eng, bufs, ())
        self.ops[eng].append((waits, None, None, 0))

    def emit(self):
        nc = self.nc
        from contextlib import ExitStack
        with ExitStack() as st:
            used = set()
            for e, lst in self.ops.items():
                for waits, fn, key, inc in lst:
                    if key is not None:
                        used.add(key)
                    for k, _ in waits:
                        used.add(k)
            for key in self.sem_names:
                if key in used:
                    self.sems[key] = st.enter_context(nc.semaphore("s_" + "_".join(map(str, key))))
            block = st.enter_context(nc.Block())
            for ename in ("sp", "act", "pe", "dve", "pool"):
                lst = self.ops[ename]
                if not lst:
                    continue

                def body(e, lst=lst):
                    for waits, fn, key, inc in lst:
                        for k, v in waits:
                            e.wait_ge(self.sems[k], v)
                        if fn is not None:
                            fn(e).then_inc(self.sems[key], inc)
                reg = {"sp": block.sync, "act": block.scalar, "pe": block.tensor,
                       "dve": block.vector, "pool": block.gpsimd}[ename]
                reg(body)


class Ring:
    def __init__(self, nc, name, shape, dtype, n, psum=False):
        self.items = []
        for i in range(n):
            if psum:
                t = nc.alloc_psum_tensor(f"{name}{i}", list(shape), dtype)
            else:
                t = nc.alloc_sbuf_tensor(f"{name}{i}", list(shape), dtype)
            self.items.append((t.ap(), Buf(f"{name}{i}")))
        self.i = 0

    def next(self):
        it = self.items[self.i]
        self.i = (self.i + 1) % len(self.items)
        return it


import math
import numpy as np
import ml_dtypes
import concourse.bass as bass
import concourse.mybir as mybir

D = 2048
NLAT = 4096
NCTX = 256
NTOK = NLAT + NCTX
OWN_LAT = 2048
OWN_CTX = 128
NOWN = OWN_LAT + OWN_CTX
FFH = 5632
EPS = 1e-6
ALPHA = (2 * 2) ** 0.25
C_A, C_B, C_C, C_DZ, C_DX, C_DT, C_END = 0, 640, 1664, 3200, 3712, 4736, 4752
TMW = 2192
FMW = 2560


class Arena:
    def __init__(self, nc, limit=200 * 1024):
        self.nc = nc
        self.off = 0
        self.limit = limit
        self.n = 0
        self.base = nc.alloc_sbuf_tensor("arena", [128, limit // 4], F32).ap()

    def sb(self, shape, dtype, name=None):
        esz = mybir.dt.size(dtype)
        nel = int(np.prod(shape[1:]))
        nbytes = (nel * esz + 63) // 64 * 64
        assert self.off + nbytes <= self.limit, f"SBUF arena overflow {self.off}+{nbytes}"
        a = self.base[0:shape[0], self.off // 4:(self.off + nbytes) // 4]
        if dtype != F32:
            a = a.bitcast(dtype)
        a = a[:, 0:nel]
        if len(shape) == 3:
            a = a.rearrange("p (a b) -> p a b", a=shape[1])
        elif len(shape) == 4:
            a = a.rearrange("p (a b c) -> p a b c", a=shape[1], b=shape[2])
        self.off += nbytes
        return a

    def mark(self):
        return self.off

    def reset(self, m=0):
        self.off = m


class Ctx:
    pass


def sbuf_ring(K, name, shape, dtype, n):
    return [(K.A.sb(shape, dtype, f"{name}{i}"), Buf(f"{name}{i}")) for i in range(n)]


class RR:
    def __init__(self, items):
        self.items = items
        self.i = 0

    def next(self):
        it = self.items[self.i]
        self.i = (self.i + 1) % len(self.items)
        return it


def barrier(K):
    P = K.P
    evs = {}
    for e in P.eng_handles:
        if P.seq[e] > 0:
            evs[("c", e)] = (P.seq[e], {})
    for key, cnt in P.dma_cnt.items():
        if cnt > 0:
            evs[key] = (cnt, {})
    b = Buf("barrier")
    b.w = evs
    for e in ("sp", "act", "pe", "dve", "pool"):
        waits = P._need(e, [b], ())
        if waits:
            P.ops[e].append((waits, None, None, 0))


def evac_copy(K, i, out, in_, reads, writes, partial=False):
    if i % 2 == 0:
        K.P.op("dve", lambda e: e.tensor_copy(out=out, in_=in_), reads=reads, writes=writes, partial=partial)
    else:
        K.P.op("act", lambda e: e.copy(out=out, in_=in_), reads=reads, writes=writes, partial=partial)


def copy_on(K, eng, out, in_, reads, writes, partial=True):
    if eng == "act":
        K.P.op("act", lambda e: e.copy(out=out, in_=in_), reads=reads, writes=writes, partial=partial)
    else:
        K.P.op("dve", lambda e: e.tensor_copy(out=out, in_=in_), reads=reads, writes=writes, partial=partial)


def ln_stats(K, x_ap, bx, width, tmp):
    P = K.P
    st, b_st = tmp["st"].next()
    mv, b_mv = tmp["mv"].next()
    nch = width // 512
    for c in range(nch):
        P.op("dve", lambda e, c=c: e.bn_stats(out=st[:, c, :], in_=x_ap[:, c * 512:(c + 1) * 512]),
             reads=[bx], writes=[b_st], partial=(c > 0))
    P.op("dve", lambda e: e.bn_aggr(out=mv[:, 0:2], in_=st[:, 0:nch, :]), reads=[b_st], writes=[b_mv])
    P.op("dve", lambda e: e.tensor_scalar_add(out=mv[:, 3:4], in0=mv[:, 1:2], scalar1=EPS), reads=[b_mv], writes=[b_mv])
    P.op("act", lambda e: e.sqrt(out=mv[:, 3:4], in_=mv[:, 3:4]), reads=[b_mv], writes=[b_mv])
    P.op("dve", lambda e: e.reciprocal(out=mv[:, 2:3], in_=mv[:, 3:4]), reads=[b_mv], writes=[b_mv])
    return mv, b_mv


def load_modT(K, mod_d, r, name):
    P, A = K.P, K.A
    raw = A.sb([96, 128], F32, name + "raw")
    b_raw = Buf(name + "raw")
    P.dma("sp", raw, mod_d[r:r + 1, :].rearrange("o (j p) -> (o j) p", p=128), writes=[b_raw], partial=False)
    ps, b_ps = K.ps_misc.next()
    P.op("pe", lambda e: e.transpose(out=ps[:, 0:96], in_=raw, identity=K.ident_f[0:96, 0:96]),
         reads=[b_raw, K.b_const], writes=[b_ps])
    mt = A.sb([128, 96], F32, name)
    b_mt = Buf(name)
    P.op("dve", lambda e: e.tensor_copy(out=mt, in_=ps[:, 0:96]), reads=[b_ps], writes=[b_mt])
    P.op("dve", lambda e: e.tensor_scalar_add(out=mt[:, 16:32], in0=mt[:, 16:32], scalar1=1.0), reads=[b_mt], writes=[b_mt])
    P.op("dve", lambda e: e.tensor_scalar_add(out=mt[:, 64:80], in0=mt[:, 64:80], scalar1=1.0), reads=[b_mt], writes=[b_mt])
    return mt, b_mt


def build_hT_tile(K, x_rows, modT, b_modT, soff, hT_dst, b_hT, tmp):
    P = K.P
    xt, b_xt = tmp["xt"].next()
    P.dma("sp", xt, x_rows, writes=[b_xt], partial=False)
    hT_from_sbuf(K, xt, b_xt, modT, b_modT, soff, hT_dst, b_hT, tmp)


def hT_from_sbuf(K, xt, b_xt, modT, b_modT, soff, hT_dst, b_hT, tmp):
    P = K.P
    mv, b_mv = ln_stats(K, xt, b_xt, D, tmp)
    xs, b_xs = tmp["xs"].next()
    P.op("dve", lambda e: e.tensor_scalar(out=xs, in0=xt, scalar1=mv[:, 0:1], scalar2=mv[:, 2:3],
                                          op0=ALU.subtract, op1=ALU.mult), reads=[b_xt, b_mv], writes=[b_xs])
    for half in range(2):
        ps, b_ps = K.ps_tp.next()
        psb = ps.bitcast(BF16).rearrange("p (k t) -> p k t", k=8)
        for k in range(8):
            kk = half * 8 + k
            P.op("pe", lambda e, k=k, kk=kk, psb=psb: e.transpose(out=psb[:, k, :], in_=xs[:, kk * 128:(kk + 1) * 128], identity=K.ident),
                 reads=[b_xs, K.b_const], writes=[b_ps], partial=(k > 0))
        t1, b_t1 = tmp["t1"].next()
        sc_b = modT[:, soff + 16 + half * 8: soff + 16 + half * 8 + 8].unsqueeze(2).broadcast_to([128, 8, 128])
        sh_b = modT[:, soff + half * 8: soff + half * 8 + 8].unsqueeze(2).broadcast_to([128, 8, 128])
        P.op("dve", lambda e, psb=psb, t1=t1, sc_b=sc_b: e.tensor_tensor(out=t1, in0=psb, in1=sc_b, op=ALU.mult),
             reads=[b_ps, b_modT], writes=[b_t1])
        P.op("pool", lambda e, t1=t1, sh_b=sh_b, half=half: e.tensor_tensor(out=hT_dst[:, half * 8:(half + 1) * 8, :], in0=t1, in1=sh_b, op=ALU.add),
             reads=[b_t1, b_modT], writes=[b_hT], partial=True)


def load_w_chunk(K, ring, w_ap, k_rows, c0, ncols):
    wb, b_wb = ring.next()
    kc = k_rows // 128
    K.P.dma("pool", wb[:, 0:kc, 0:ncols], w_ap[0:kc * 128, c0:c0 + ncols].rearrange("(k p) n -> p k n", p=128),
            writes=[b_wb], partial=False)
    return wb, b_wb


def phase_A(K, c2_d, w_ada, b_ada, mod_d):
    P, A = K.P, K.A
    m0 = A.mark()
    craw = A.sb([32, 128], F32, "craw")
    b_craw = Buf("craw")
    P.dma("sp", craw, c2_d.rearrange("r (k p) -> (r k) p", p=128), writes=[b_craw], partial=False)
    csil = A.sb([32, 128], F32, "csil")
    b_csil = Buf("csil")
    P.op("act", lambda e: e.activation(out=csil, in_=craw, func=AF.Silu), reads=[b_craw], writes=[b_csil])
    ps, b_ps = K.ps_misc.next()
    P.op("pe", lambda e: e.transpose(out=ps[:, 0:32], in_=csil, identity=K.ident_f[0:32, 0:32]),
         reads=[b_csil, K.b_const], writes=[b_ps])
    cT = A.sb([128, 32], F32, "cT")
    b_cT = Buf("cT")
    P.op("dve", lambda e: e.tensor_copy(out=cT, in_=ps[:, 0:32]), reads=[b_ps], writes=[b_cT])
    bias = A.sb([2, 12288], F32, "bada")
    b_bias = Buf("bada")
    P.dma("sp", bias, b_ada.broadcast_to([2, 12288]) if False else b_ada.partition_broadcast(2), writes=[b_bias], partial=False)
    modsb = A.sb([2, 12288], F32, "modsb")
    b_modsb = Buf("modsb")
    wr = RR(sbuf_ring(K, "wada", [128, 16, 512], F32, 2))
    for n in range(24):
        wch, b_wch = wr.next()
        P.dma("sp" if n % 2 == 0 else "act", wch, w_ada[:, n * 512:(n + 1) * 512].rearrange("(k p) n -> p k n", p=128),
              writes=[b_wch], partial=False)
        po, b_po = K.ps_acc.next()
        for k in range(16):
            P.op("pe", lambda e, k=k, po=po, wch=wch: e.matmul(po[0:2, :], lhsT=cT[:, k::16], rhs=wch[:, k, :], start=(k == 0), stop=(k == 15)),
                 reads=[b_cT, b_wch], writes=[b_po], partial=(k > 0))
        P.op("dve", lambda e, n=n, po=po: e.tensor_tensor(out=modsb[:, n * 512:(n + 1) * 512], in0=po[0:2, :], in1=bias[:, n * 512:(n + 1) * 512], op=ALU.add),
             reads=[b_po, b_bias], writes=[b_modsb], partial=True)
    P.dma("sp", mod_d, modsb, reads=[b_modsb], writes=[K.b_mod_d], partial=False)
    barrier(K)
    A.reset(m0)


def phase_B(K, x_lat, x_ctx, mod_d, w_in, pTM, pFM, hT_d):
    P, A = K.P, K.A
    m0 = A.mark()
    modT_l, b_ml = load_modT(K, mod_d, 0, "modTl")
    modT_c, b_mc = load_modT(K, mod_d, 1, "modTc")
    tmp = {
        "st": RR(sbuf_ring(K, "bst", [128, 4, 6], F32, 2)),
        "mv": RR(sbuf_ring(K, "bmv", [128, 4], F32, 2)),
        "xt": RR(sbuf_ring(K, "xt", [128, D], F32, 2)),
        "xs": RR(sbuf_ring(K, "xs", [128, D], BF16, 2)),
        "t1": RR(sbuf_ring(K, "t1", [128, 8, 128], F32, 2)),
    }
    hT = A.sb([128, 16, 2048], BF16, "hT")
    wring = RR(sbuf_ring(K, "wch", [128, 16, 512], BF16, 2))
    ostg = RR(sbuf_ring(K, "ostg", [128, 512], F32, 4))
    groups = [("lat", 0, 16), ("lat", 16, 16), ("ctx", 0, 2)]
    tm_chunks = [(C_A, 512, 0), (C_A + 512, 512, 512), (C_A + 1024, 512, 1024), (C_A + 1536, 128, 1536),
                 (C_DZ, 512, 1664), (C_DT, 16, 2176)]
    fm_chunks = [(C_C, 512, 0), (C_C + 512, 512, 512), (C_C + 1024, 512, 1024), (C_DX, 512, 1536), (C_DX + 512, 512, 2048)]
    ev = 0
    for kind, t0, nt in groups:
        b_hT = [Buf(f"hT{t}") for t in range(nt)]
        for t in range(nt):
            if kind == "lat":
                rows = x_lat[(t0 + t) * 128:(t0 + t + 1) * 128, :]
                mt, bm = modT_l, b_ml
            else:
                rows = x_ctx[(t0 + t) * 128:(t0 + t + 1) * 128, :]
                mt, bm = modT_c, b_mc
            build_hT_tile(K, rows, mt, bm, 0, hT[:, :, t * 128:(t + 1) * 128], b_hT[t], tmp)
        tok0 = (t0 * 128) if kind == "lat" else (NLAT + t0 * 128)
        ntok = nt * 128
        if kind == "lat" and t0 == 0:
            P.dma("sp", hT_d[:, :, 0:OWN_LAT], hT[:, :, 0:OWN_LAT], reads=b_hT, writes=[K.b_hT_d])
        if kind == "ctx":
            P.dma("sp", hT_d[:, :, OWN_LAT:OWN_LAT + OWN_CTX], hT[:, :, 0:OWN_CTX], reads=b_hT, writes=[K.b_hT_d])
        for (c0, ncols, dcol) in tm_chunks:
            wb, b_wb = load_w_chunk(K, wring, w_in, D, c0, ncols)
            for t in range(nt):
                po, b_po = K.ps_acc.next()
                for k in range(16):
                    P.op("pe", lambda e, k=k, po=po, wb=wb, t=t, ncols=ncols: e.matmul(po[:, 0:ncols], lhsT=hT[:, k, t * 128:(t + 1) * 128], rhs=wb[:, k, 0:ncols],
                                                                                 start=(k == 0), stop=(k == 15)),
                         reads=[b_hT[t], b_wb], writes=[b_po], partial=(k > 0))
                so, b_so = ostg.next()
                evac_copy(K, ev, so[:, 0:ncols], po[:, 0:ncols], [b_po], [b_so]); ev += 1
                P.dma("sp", pTM[tok0 + t * 128: tok0 + (t + 1) * 128, dcol:dcol + ncols], so[:, 0:ncols], reads=[b_so], writes=[K.b_pTM])
        for (c0, ncols, drow) in fm_chunks:
            wb, b_wb = load_w_chunk(K, wring, w_in, D, c0, ncols)
            for j in range(ncols // 128):
                for s0 in range(0, ntok, 512):
                    sl = min(512, ntok - s0)
                    po, b_po = K.ps_acc.next()
                    tiles = list(range(s0 // 128, (s0 + sl) // 128))
                    for k in range(16):
                        P.op("pe", lambda e, k=k, po=po, wb=wb, j=j, s0=s0, sl=sl: e.matmul(po[:, 0:sl], lhsT=wb[:, k, j * 128:(j + 1) * 128], rhs=hT[:, k, s0:s0 + sl],
                                                                                   start=(k == 0), stop=(k == 15)),
                             reads=[b_hT[t] for t in tiles] + [b_wb], writes=[b_po], partial=(k > 0))
                    so, b_so = ostg.next()
                    evac_copy(K, ev, so[:, 0:sl], po[:, 0:sl], [b_po], [b_so]); ev += 1
                    P.dma("sp", pFM[drow + j * 128: drow + (j + 1) * 128, tok0 + s0: tok0 + s0 + sl], so[:, 0:sl], reads=[b_so], writes=[K.b_pFM])
    barrier(K)
    A.reset(m0)


def make_ctx(nc):
    K = Ctx()
    K.nc = nc
    K.P = Prog(nc)
    K.A = Arena(nc)
    banks = []
    for i in range(8):
        t = nc.alloc_psum_tensor(f"psb{i}", [128, 512], F32)
        banks.append((t.ap(), Buf(f"psb{i}")))
    K.banks = banks
    K.ps_acc = RR(banks[0:4])
    K.ps_tp = RR(banks[4:6])
    K.ps_misc = RR(banks[6:8])
    return K


def load_consts(K, consts):
    P, A = K.P, K.A
    K.b_const = Buf("const")
    K.ident = A.sb([128, 128], BF16, "ident")
    K.ident_f = A.sb([128, 128], F32, "identf")
    P.dma("sp", K.ident, consts["ident_bf"], writes=[K.b_const])
    P.dma("sp", K.ident_f, consts["ident_f"], writes=[K.b_const])


def rms_rstd(K, ss, b_ss, n, width):
    P = K.P
    P.op("dve", lambda e: e.tensor_scalar(out=ss[:, 0:n], in0=ss[:, 0:n], scalar1=1.0 / width, scalar2=EPS, op0=ALU.mult, op1=ALU.add),
         reads=[b_ss], writes=[b_ss])
    P.op("act", lambda e: e.sqrt(out=ss[:, 0:n], in_=ss[:, 0:n]), reads=[b_ss], writes=[b_ss])
    P.op("dve", lambda e: e.reciprocal(out=ss[:, 0:n], in_=ss[:, 0:n]), reads=[b_ss], writes=[b_ss])


def rope_apply(K, src, dst, cs, b_src, b_dst, b_cs, nh, npair, tmpr):
    P = K.P
    s4 = src.rearrange("p h (i two) -> p h i two", two=2)
    d4 = dst.rearrange("p h (i two) -> p h i two", two=2)
    x0, x1 = s4[:, :, :, 0], s4[:, :, :, 1]
    cb = cs[:, 0:1, :].broadcast_to([128, nh, npair])
    sb_ = cs[:, 1:2, :].broadcast_to([128, nh, npair])
    (ta, b_ta) = tmpr.next()
    (tb, b_tb) = tmpr.next()
    ta = ta[:, 0:nh, 0:npair]; tb = tb[:, 0:nh, 0:npair]
    P.op("dve", lambda e: e.tensor_tensor(out=ta, in0=x0, in1=cb, op=ALU.mult), reads=[b_src, b_cs], writes=[b_ta])
    P.op("pool", lambda e: e.tensor_tensor(out=tb, in0=x1, in1=sb_, op=ALU.mult), reads=[b_src, b_cs], writes=[b_tb])
    P.op("dve", lambda e: e.tensor_tensor(out=d4[:, :, :, 0], in0=ta, in1=tb, op=ALU.subtract), reads=[b_ta, b_tb], writes=[b_dst], partial=True)
    (tc, b_tc) = tmpr.next()
    (td, b_td) = tmpr.next()
    tc = tc[:, 0:nh, 0:npair]; td = td[:, 0:nh, 0:npair]
    P.op("dve", lambda e: e.tensor_tensor(out=tc, in0=x0, in1=sb_, op=ALU.mult), reads=[b_src, b_cs], writes=[b_tc])
    P.op("pool", lambda e: e.tensor_tensor(out=td, in0=x1, in1=cb, op=ALU.mult), reads=[b_src, b_cs], writes=[b_td])
    P.op("dve", lambda e: e.tensor_tensor(out=d4[:, :, :, 1], in0=tc, in1=td, op=ALU.add), reads=[b_tc, b_td], writes=[b_dst], partial=True)


def attn_core(K, heads, V_all, b_V, dv, scale, o_tm, b_otm, need_ctx):
    P, A = K.P, K.A
    pt_ring = RR(sbuf_ring(K, "pt", [128, 512], BF16, 3))
    rc_ring = RR(sbuf_ring(K, "rc", [128, 1], F32, 4))
    obanks = K.banks[4:8]
    qgroups = [(q0, 512, list(range(34))) for q0 in range(0, OWN_LAT, 512)]
    if need_ctx:
        qgroups.append((OWN_LAT, 128, [32, 33]))
    for h, (kvi, parts) in enumerate(heads):
        for (q0, nq, kbs) in qgroups:
            nqs = nq // 128
            for kb in kbs:
                ps, b_ps = K.ps_acc.next()
                for pi, (kT, qT, bk, bq) in enumerate(parts):
                    P.op("pe", lambda e, ps=ps, kT=kT, qT=qT, kb=kb, pi=pi, q0=q0, nq=nq, last=(pi == len(parts) - 1): e.matmul(
                        ps[:, 0:nq], lhsT=kT[:, kb * 128:(kb + 1) * 128], rhs=qT[:, q0:q0 + nq], start=(pi == 0), stop=last),
                         reads=[bk, bq], writes=[b_ps], partial=(pi > 0))
                pt, b_pt = pt_ring.next()
                P.op("act", lambda e, ps=ps, pt=pt, nq=nq: e.activation(out=pt[:, 0:nq], in_=ps[:, 0:nq], func=AF.Exp, scale=scale),
                     reads=[b_ps], writes=[b_pt])
                for qs in range(nqs):
                    ob, b_ob = obanks[qs]
                    P.op("pe", lambda e, ob=ob, pt=pt, qs=qs, kb=kb, kvi=kvi, st=(kb == kbs[0]), sp=(kb == kbs[-1]): e.matmul(
                        ob[:, 0:dv + 1], lhsT=pt[:, qs * 128:(qs + 1) * 128], rhs=V_all[:, kb, kvi, 0:dv + 1], start=st, stop=sp),
                         reads=[b_pt, b_V], writes=[b_ob], partial=(kb != kbs[0]))
            for qs in range(nqs):
                ob, b_ob = obanks[qs]
                rc, b_rc = rc_ring.next()
                ot = q0 // 128 + qs
                P.op("dve", lambda e, rc=rc, ob=ob: e.reciprocal(out=rc, in_=ob[:, dv:dv + 1]), reads=[b_ob], writes=[b_rc])
                P.op("dve", lambda e, rc=rc, ob=ob, ot=ot, h=h: e.tensor_scalar_mul(out=o_tm[:, ot, h * dv:(h + 1) * dv], in0=ob[:, 0:dv], scalar1=rc[:, 0:1]),
                     reads=[b_ob, b_rc], writes=[b_otm], partial=True)


def store_oT(K, o_tm, b_otm, oT_d, branch, n_own_tiles):
    P = K.P
    stg = RR(sbuf_ring(K, "ostg", [128, 4, 128], BF16, 3))
    for ot in range(n_own_tiles):
        ps, b_ps = K.ps_tp.next()
        psb = ps.bitcast(BF16).rearrange("p (k t) -> p k t", k=8)
        for h in range(4):
            P.op("pe", lambda e, psb=psb, h=h, ot=ot: e.transpose(out=psb[:, h, :], in_=o_tm[:, ot, h * 128:(h + 1) * 128], identity=K.ident),
                 reads=[b_otm, K.b_const], writes=[b_ps], partial=(h > 0))
        so, b_so = stg.next()
        evac_copy(K, ot, so, psb[:, 0:4, :], [b_ps], [b_so])
        P.dma("sp", oT_d[branch * 512:(branch + 1) * 512, ot * 128:(ot + 1) * 128].rearrange("(h p) t -> p h t", p=128), so,
              reads=[b_so], writes=[K.b_oT_d])


def own_tiles(need_ctx):
    lst = [(t, t) for t in range(16)]
    if need_ctx:
        lst.append((32, 16))
    return lst


def phase_mla(K, pTM, q_norm, kv_norm, w_uq, w_ukv, ropeA, oT_d, need_ctx):
    P, A = K.P, K.A
    m0 = A.mark()
    NOWN = OWN_LAT + OWN_CTX
    knT = A.sb([128, 4, NTOK], BF16, "knT"); b_knT = Buf("knT")
    krT = A.sb([64, NTOK], BF16, "krT"); b_krT = Buf("krT")
    V_all = A.sb([128, 34, 4, 136], BF16, "Vall"); b_V = Buf("Vall")
    qnT = A.sb([128, 4, NOWN], BF16, "qnT"); b_qnT = Buf("qnT")
    qrT = A.sb([64, 4, NOWN], BF16, "qrT"); b_qrT = Buf("qrT")
    o_tm = A.sb([128, 17, 512], BF16, "otm"); b_otm = Buf("otm")
    m1 = A.mark()
    ckvT = A.sb([128, NTOK], BF16, "ckvT"); b_ckvT = Buf("ckvT")
    cqT = A.sb([128, 4, NOWN], BF16, "cqT"); b_cqT = Buf("cqT")
    wuq = A.sb([128, 4, 768], BF16, "wuq"); b_wuq = Buf("wuq")
    wukv = A.sb([128, 1024], BF16, "wukv"); b_wukv = Buf("wukv")
    gq = A.sb([128, 448], F32, "gq"); gkv = A.sb([128, 128], F32, "gkv"); b_g = Buf("gains")
    P.dma("pool", wuq[:, 0:3, :], w_uq[0:384, :].rearrange("(k p) n -> p k n", p=128), writes=[b_wuq])
    P.dma("pool", wuq[0:64, 3, :], w_uq[384:448, :], writes=[b_wuq])
    P.dma("pool", wukv, w_ukv, writes=[b_wukv])
    P.dma("sp", gq, q_norm.partition_broadcast(128), writes=[b_g])
    P.dma("sp", gkv, kv_norm.partition_broadcast(128), writes=[b_g])
    P.op("pool", lambda e: e.memset(V_all[:, :, :, 128:129], 1.0), writes=[b_V], partial=True)
    pa_r = RR(sbuf_ring(K, "pa", [128, 640], F32, 2))
    cs_r = RR(sbuf_ring(K, "csA", [128, 2, 32], F32, 2))
    sq_r = RR(sbuf_ring(K, "sq", [128, 448], F32, 2))
    ss_r = RR(sbuf_ring(K, "ss", [128, 2], F32, 4))
    nb_r = RR(sbuf_ring(K, "nb", [128, 640], BF16, 2))
    tmpr = RR(sbuf_ring(K, "rt", [128, 4, 32], F32, 8))
    owned = dict(own_tiles(need_ctx))
    for t in range(34):
        is_lat = t < 32
        pa, b_pa = pa_r.next()
        P.dma("sp", pa, pTM[t * 128:(t + 1) * 128, 0:640], writes=[b_pa], partial=False)
        nb, b_nb = nb_r.next()
        if is_lat:
            cs, b_cs = cs_r.next()
            P.dma("sp", cs, ropeA[t * 128:(t + 1) * 128], writes=[b_cs], partial=False)
        sq, b_sq = sq_r.next()
        ss, b_ss = ss_r.next()
        P.op("pool", lambda e, sq=sq, pa=pa: e.tensor_tensor(out=sq[:, 0:128], in0=pa[:, 448:576], in1=pa[:, 448:576], op=ALU.mult), reads=[b_pa], writes=[b_sq])
        P.op("dve", lambda e, sq=sq, ss=ss: e.reduce_sum(out=ss[:, 0:1], in_=sq[:, 0:128], axis=AX.X), reads=[b_sq], writes=[b_ss])
        rms_rstd(K, ss, b_ss, 1, 128)
        P.op("dve", lambda e, nb=nb, pa=pa, ss=ss: e.scalar_tensor_tensor(out=nb[:, 448:576], in0=pa[:, 448:576], scalar=ss[:, 0:1], in1=gkv, op0=ALU.mult, op1=ALU.mult),
             reads=[b_pa, b_ss, b_g], writes=[b_nb], partial=True)
        if is_lat:
            rope_apply(K, pa[:, 576:640].rearrange("p (h d) -> p h d", h=1), nb[:, 576:640].rearrange("p (h d) -> p h d", h=1), cs, b_pa, b_nb, b_cs, 1, 32, tmpr)
        else:
            P.op("dve", lambda e, nb=nb, pa=pa: e.tensor_copy(out=nb[:, 576:640], in_=pa[:, 576:640]), reads=[b_pa], writes=[b_nb], partial=True)
        ps, b_ps = K.ps_tp.next()
        psb = ps.bitcast(BF16).rearrange("p (k t) -> p k t", k=8)
        P.op("pe", lambda e, psb=psb, nb=nb: e.transpose(out=psb[:, 0, :], in_=nb[:, 448:576], identity=K.ident), reads=[b_nb, K.b_const], writes=[b_ps])
        P.op("pe", lambda e, psb=psb, nb=nb: e.transpose(out=psb[0:64, 1, :], in_=nb[:, 576:640], identity=K.ident), reads=[b_nb, K.b_const], writes=[b_ps], partial=True)
        own = t in owned
        if own:
            ot = owned[t]
            sq2, b_sq2 = sq_r.next()
            ss2, b_ss2 = ss_r.next()
            P.op("pool", lambda e, sq2=sq2, pa=pa: e.tensor_tensor(out=sq2, in0=pa[:, 0:448], in1=pa[:, 0:448], op=ALU.mult), reads=[b_pa], writes=[b_sq2])
            P.op("dve", lambda e, sq2=sq2, ss2=ss2: e.reduce_sum(out=ss2[:, 0:1], in_=sq2, axis=AX.X), reads=[b_sq2], writes=[b_ss2])
            rms_rstd(K, ss2, b_ss2, 1, 448)
            P.op("dve", lambda e, nb=nb, pa=pa, ss2=ss2: e.scalar_tensor_tensor(out=nb[:, 0:448], in0=pa[:, 0:448], scalar=ss2[:, 0:1], in1=gq, op0=ALU.mult, op1=ALU.mult),
                 reads=[b_pa, b_ss2, b_g], writes=[b_nb], partial=True)
            for kc in range(4):
                kp = 128 if kc < 3 else 64
                P.op("pe", lambda e, psb=psb, nb=nb, kc=kc, kp=kp: e.transpose(out=psb[0:kp, 2 + kc, :], in_=nb[:, kc * 128:kc * 128 + kp], identity=K.ident),
                     reads=[b_nb, K.b_const], writes=[b_ps], partial=True)
        ce = "act" if t % 2 == 0 else "dve"
        copy_on(K, ce, ckvT[:, t * 128:(t + 1) * 128], psb[:, 0, :], [b_ps], [b_ckvT])
        copy_on(K, ce, krT[:, t * 128:(t + 1) * 128], psb[0:64, 1, :], [b_ps], [b_krT])
        if own:
            copy_on(K, ce, cqT[:, 0:3, ot * 128:(ot + 1) * 128], psb[:, 2:5, :], [b_ps], [b_cqT])
            copy_on(K, ce, cqT[0:64, 3, ot * 128:(ot + 1) * 128], psb[0:64, 5, :], [b_ps], [b_cqT])
    ev = 0
    for h in range(4):
        for s0 in range(0, NTOK, 512):
            sl = min(512, NTOK - s0)
            po, b_po = K.ps_acc.next()
            P.op("pe", lambda e, po=po, h=h, s0=s0, sl=sl: e.matmul(po[:, 0:sl], lhsT=wukv[:, h * 256:h * 256 + 128], rhs=ckvT[:, s0:s0 + sl], start=True, stop=True),
                 reads=[b_wukv, b_ckvT], writes=[b_po])
            evac_copy(K, ev, knT[:, h, s0:s0 + sl], po[:, 0:sl], [b_po], [b_knT], partial=True); ev += 1
    wv = wukv.rearrange("p (h c) -> p h c", h=4)[:, :, 128:256]
    for kt in range(34):
        po, b_po = K.ps_acc.next()
        pov = po.rearrange("p (h c) -> p h c", h=4)
        P.op("pe", lambda e, pov=pov, kt=kt: e.matmul(pov, lhsT=ckvT[:, kt * 128:(kt + 1) * 128], rhs=wv, start=True, stop=True),
             reads=[b_wukv, b_ckvT], writes=[b_po])
        evac_copy(K, ev, V_all[:, kt, :, 0:128], pov, [b_po], [b_V], partial=True); ev += 1
    qf_r = RR(sbuf_ring(K, "qf", [128, 4, 192], F32, 2))
    qb_r = RR(sbuf_ring(K, "qb", [128, 4, 192], BF16, 2))
    for (t, ot) in own_tiles(need_ctx):
        qf, b_qf = qf_r.next()
        qff = qf.rearrange("p h d -> p (h d)")
        for half in range(2):
            po, b_po = K.ps_acc.next()
            for kc in range(4):
                kp = 128 if kc < 3 else 64
                P.op("pe", lambda e, po=po, kc=kc, kp=kp, ot=ot, half=half: e.matmul(po[:, 0:384], lhsT=cqT[0:kp, kc, ot * 128:(ot + 1) * 128], rhs=wuq[0:kp, kc, half * 384:(half + 1) * 384],
                                                                           start=(kc == 0), stop=(kc == 3)),
                     reads=[b_cqT, b_wuq], writes=[b_po], partial=(kc > 0))
            evac_copy(K, half, qff[:, half * 384:(half + 1) * 384], po[:, 0:384], [b_po], [b_qf], partial=(half > 0))
        qb, b_qb = qb_r.next()
        P.op("act", lambda e, qb=qb, qf=qf: e.copy(out=qb[:, :, 0:128], in_=qf[:, :, 0:128]), reads=[b_qf], writes=[b_qb])
        if t < 32:
            cs, b_cs = cs_r.next()
            P.dma("sp", cs, ropeA[t * 128:(t + 1) * 128], writes=[b_cs], partial=False)
            rope_apply(K, qf[:, :, 128:192], qb[:, :, 128:192], cs, b_qf, b_qb, b_cs, 4, 32, tmpr)
        else:
            P.op("dve", lambda e, qb=qb, qf=qf: e.tensor_copy(out=qb[:, :, 128:192], in_=qf[:, :, 128:192]), reads=[b_qf], writes=[b_qb], partial=True)
        ps, b_ps = K.ps_tp.next()
        psb = ps.bitcast(BF16).rearrange("p (k t) -> p k t", k=8)
        for h in range(4):
            P.op("pe", lambda e, psb=psb, qb=qb, h=h: e.transpose(out=psb[:, h, :], in_=qb[:, h, 0:128], identity=K.ident), reads=[b_qb, K.b_const], writes=[b_ps], partial=(h > 0))
            P.op("pe", lambda e, psb=psb, qb=qb, h=h: e.transpose(out=psb[0:64, 4 + h, :], in_=qb[:, h, 128:192], identity=K.ident), reads=[b_qb, K.b_const], writes=[b_ps], partial=True)
        ce = "act" if ot % 2 == 0 else "dve"
        copy_on(K, ce, qnT[:, :, ot * 128:(ot + 1) * 128], psb[:, 0:4, :], [b_ps], [b_qnT])
        copy_on(K, ce, qrT[:, :, ot * 128:(ot + 1) * 128], psb[0:64, 4:8, :], [b_ps], [b_qrT])
    barrier(K)
    A.reset(m1)
    heads = [(h, [(knT[:, h, :], qnT[:, h, :], b_knT, b_qnT), (krT, qrT[:, h, :], b_krT, b_qrT)]) for h in range(4)]
    attn_core(K, heads, V_all, b_V, 128, 192 ** -0.5, o_tm, b_otm, need_ctx)
    store_oT(K, o_tm, b_otm, oT_d, 0, 17 if need_ctx else 16)
    barrier(K)
    A.reset(m0)


def phase_gqa(K, pTM, q_norm, k_norm, ropeB, oT_d, need_ctx):
    P, A = K.P, K.A
    m0 = A.mark()
    NOWN = OWN_LAT + OWN_CTX
    kT = A.sb([128, 2, NTOK], BF16, "gkT"); b_kT = Buf("gkT")
    V_all = A.sb([128, 34, 2, 136], BF16, "gV"); b_V = Buf("gV")
    qT = A.sb([128, 4, NOWN], BF16, "gqT"); b_qT = Buf("gqT")
    o_tm = A.sb([128, 17, 512], BF16, "gotm"); b_otm = Buf("gotm")
    m1 = A.mark()
    gn = A.sb([128, 6, 128], F32, "ggain"); b_g = Buf("ggain")
    for h in range(4):
        P.dma("sp", gn[:, h, :], q_norm.partition_broadcast(128), writes=[b_g])
    for h in range(2):
        P.dma("sp", gn[:, 4 + h, :], k_norm.partition_broadcast(128), writes=[b_g])
    P.op("pool", lambda e: e.memset(V_all[:, :, :, 128:129], 1.0), writes=[b_V], partial=True)
    pb_r = RR(sbuf_ring(K, "pb", [128, 1024], F32, 2))
    cs_r = RR(sbuf_ring(K, "csB", [128, 2, 64], F32, 2))
    sq_r = RR(sbuf_ring(K, "gsq", [128, 6, 128], F32, 2))
    xn_r = RR(sbuf_ring(K, "gxn", [128, 6, 128], F32, 2))
    ss_r = RR(sbuf_ring(K, "gss", [128, 6], F32, 3))
    nb_r = RR(sbuf_ring(K, "gnb", [128, 6, 128], BF16, 2))
    tmpr = RR(sbuf_ring(K, "grt", [128, 6, 64], F32, 8))
    owned = dict(own_tiles(need_ctx))
    import os
    tl = [int(v) for v in os.environ.get("GQA_T", ",".join(map(str, range(34)))).split(",")]
    for t in tl:
        is_lat = t < 32
        own = t in owned
        pb, b_pb = pb_r.next()
        P.dma("sp", pb, pTM[t * 128:(t + 1) * 128, 640:1664], writes=[b_pb], partial=False)
        h0, nh = (0, 6) if own else (4, 2)
        x = pb[:, h0 * 128:(h0 + nh) * 128].rearrange("p (h d) -> p h d", h=nh)
        sq, b_sq = sq_r.next()
        ss, b_ss = ss_r.next()
        xn, b_xn = xn_r.next()
        nb, b_nb = nb_r.next()
        P.op("pool", lambda e, sq=sq, x=x, nh=nh: e.tensor_tensor(out=sq[:, 0:nh, :], in0=x, in1=x, op=ALU.mult), reads=[b_pb], writes=[b_sq])
        P.op("dve", lambda e, sq=sq, ss=ss, nh=nh: e.reduce_sum(out=ss[:, 0:nh], in_=sq[:, 0:nh, :], axis=AX.X), reads=[b_sq], writes=[b_ss])
        rms_rstd(K, ss, b_ss, nh, 128)
        P.op("dve", lambda e, xn=xn, x=x, ss=ss, nh=nh: e.tensor_tensor(out=xn[:, 0:nh, :], in0=x, in1=ss[:, 0:nh].unsqueeze(2).broadcast_to([128, nh, 128]), op=ALU.mult),
             reads=[b_pb, b_ss], writes=[b_xn])
        if is_lat:
            P.op("pool", lambda e, xn=xn, nh=nh, h0=h0: e.tensor_tensor(out=xn[:, 0:nh, :], in0=xn[:, 0:nh, :], in1=gn[:, h0:h0 + nh, :], op=ALU.mult),
                 reads=[b_xn, b_g], writes=[b_xn])
            cs, b_cs = cs_r.next()
            P.dma("sp", cs, ropeB[t * 128:(t + 1) * 128], writes=[b_cs], partial=False)
            rope_apply(K, xn[:, 0:nh, :], nb[:, 0:nh, :], cs, b_xn, b_nb, b_cs, nh, 64, tmpr)
        else:
            P.op("pool", lambda e, xn=xn, nb=nb, nh=nh, h0=h0: e.tensor_tensor(out=nb[:, 0:nh, :], in0=xn[:, 0:nh, :], in1=gn[:, h0:h0 + nh, :], op=ALU.mult),
                 reads=[b_xn, b_g], writes=[b_nb])
        ps, b_ps = K.ps_tp.next()
        psb = ps.bitcast(BF16).rearrange("p (k t) -> p k t", k=8)
        for j in range(nh):
            P.op("pe", lambda e, psb=psb, nb=nb, j=j: e.transpose(out=psb[:, j, :], in_=nb[:, j, :], identity=K.ident), reads=[b_nb, K.b_const], writes=[b_ps], partial=(j > 0))
        ce = "act" if t % 2 == 0 else "dve"
        if own:
            ot = owned[t]
            copy_on(K, ce, qT[:, :, ot * 128:(ot + 1) * 128], psb[:, 0:4, :], [b_ps], [b_qT])
            copy_on(K, ce, kT[:, :, t * 128:(t + 1) * 128], psb[:, 4:6, :], [b_ps], [b_kT])
        else:
            copy_on(K, ce, kT[:, :, t * 128:(t + 1) * 128], psb[:, 0:2, :], [b_ps], [b_kT])
        P.op("act", lambda e, pb=pb, t=t: e.copy(out=V_all[:, t, :, 0:128], in_=pb[:, 768:1024].rearrange("p (h d) -> p h d", h=2)), reads=[b_pb], writes=[b_V], partial=True)
    barrier(K)
    A.reset(m1)
    heads = [(h // 2, [(kT[:, h // 2, :], qT[:, h, :], b_kT, b_qT)]) for h in range(4)]
    import os
    if getattr(K, "dbg", None):
        P.dma("sp", K.dbg["qT"], qT, reads=[b_qT], writes=[K.b_oT_d])
        P.dma("sp", K.dbg["kT"], kT, reads=[b_kT], writes=[K.b_oT_d])
        P.dma("sp", K.dbg["V"], V_all, reads=[b_V], writes=[K.b_oT_d])
    if os.environ.get("BISECT") == "prep":
        barrier(K); A.reset(m0); return
    attn_core(K, heads, V_all, b_V, 128, 128 ** -0.5, o_tm, b_otm, need_ctx)
    if os.environ.get("BISECT") == "core":
        barrier(K); A.reset(m0); return
    store_oT(K, o_tm, b_otm, oT_d, 1, 17 if need_ctx else 16)
    barrier(K)
    A.reset(m0)


def rope_tables(half):
    pos = np.arange(NLAT)
    if half == 1:
        pos = NLAT - 1 - pos
    row = (pos // 64).astype(np.float32)
    col = (pos % 64).astype(np.float32)
    out = []
    for rot in (64, 128):
        nf = rot // 4
        inv = (10000.0 ** (-np.arange(nf, dtype=np.float32) / nf)).astype(np.float32)
        ang = np.concatenate([row[:, None] * inv, col[:, None] * inv], -1).astype(np.float32)
        out.append(np.stack([np.cos(ang), np.sin(ang)], 1).astype(np.float32))
    return out


def load_bcast(K, src_row, width, name):
    t = K.A.sb([128, width], F32, name)
    b = Buf(name)
    K.P.dma("sp", t, src_row.partition_broadcast(128), writes=[b], partial=False)
    return t, b


def ln_affine_tile(K, r_rows_dram, g_b, b_b, bgb, tmp, out_tile, b_out):
    P = K.P
    xt, b_xt = tmp["xt"].next()
    P.dma("sp", xt, r_rows_dram, writes=[b_xt], partial=False)
    mv, b_mv = ln_stats(K, xt, b_xt, D, tmp)
    P.op("dve", lambda e: e.tensor_scalar(out=xt, in0=xt, scalar1=mv[:, 0:1], scalar2=mv[:, 2:3], op0=ALU.subtract, op1=ALU.mult),
         reads=[b_xt, b_mv], writes=[b_xt])
    P.op("pool", lambda e: e.tensor_tensor(out=xt, in0=xt, in1=g_b, op=ALU.mult), reads=[b_xt, bgb], writes=[b_xt])
    P.op("dve", lambda e: e.tensor_tensor(out=out_tile, in0=xt, in1=b_b, op=ALU.add), reads=[b_xt, bgb], writes=[b_out])


def phase_D(K, x_lat, x_ctx, hT_d, oT_d, mod_d, w_mgate, b_mgate, w_branch, w_out, ln1_g, ln1_b,
            w_ffn_in, w_ffn_out, ln2_g, ln2_b, r_d, xmid_d, out_lat, out_ctx, need_ctx):
    P, A = K.P, K.A
    m0 = A.mark()
    n_own = 17 if need_ctx else 16

    def xrows(ot, c0=0, c1=D):
        return x_lat[ot * 128:(ot + 1) * 128, c0:c1] if ot < 16 else x_ctx[0:128, c0:c1]

    def outrows(ot):
        return out_lat[ot * 128:(ot + 1) * 128, :] if ot < 16 else out_ctx[0:128, :]

    modT_l, b_ml = load_modT(K, mod_d, 0, "DmodTl")
    modT_c, b_mc = load_modT(K, mod_d, 1, "DmodTc")
    braw = A.sb([64, 128], F32, "bgraw"); b_braw = Buf("bgraw")
    P.dma("sp", braw, b_mgate.rearrange("i (k p) -> (i k) p", p=128), writes=[b_braw], partial=False)
    ps, b_ps = K.ps_misc.next()
    P.op("pe", lambda e: e.transpose(out=ps[:, 0:64], in_=braw, identity=K.ident_f[0:64, 0:64]), reads=[b_braw, K.b_const], writes=[b_ps])
    bgT = A.sb([128, 64], F32, "bgT"); b_bgT = Buf("bgT")
    P.op("dve", lambda e: e.tensor_copy(out=bgT, in_=ps[:, 0:64]), reads=[b_ps], writes=[b_bgT])
    m_grp = A.mark()
    groups = [list(range(0, 8)), list(range(8, n_own))]
    oT_v = oT_d.rearrange("(c p) t -> p c t", p=128)
    def do_group(tiles):
        A.reset(m_grp)
        g0 = tiles[0] * 128
        ng = len(tiles) * 128
        slabs = [(s0, min(512, ng - s0)) for s0 in range(0, ng, 512)]
        accT = A.sb([128, 16, ng], BF16, "accT"); b_accT = Buf("accT")
        m_s1 = A.mark()
        hTg = A.sb([128, 16, ng], BF16, "hTg"); b_hTg = Buf("hTg")
        oTg = A.sb([128, 16, ng], BF16, "oTg"); b_oTg = Buf("oTg")
        P.dma("sp", hTg, hT_d[:, :, g0:g0 + ng], writes=[b_hTg], partial=False)
        P.dma("act", oTg, oT_v[:, :, g0:g0 + ng], writes=[b_oTg], partial=False)
        wg_r = RR(sbuf_ring(K, "wg", [128, 16, 512], BF16, 2))
        wb_r = RR(sbuf_ring(K, "wbr", [128, 4, 512], BF16, 2))
        acc = A.sb([128, 4, ng], F32, "acc"); b_acc = Buf("acc")
        sg_r = RR(sbuf_ring(K, "sig", [128, 512], F32, 2))
        tm_r = RR(sbuf_ring(K, "term", [128, 512], F32, 2))
        for jc in range(4):
            for i in range(4):
                wg, b_wg = load_w_chunk(K, wg_r, w_mgate[i], D, jc * 512, 512)
                wb, b_wb = load_w_chunk(K, wb_r, w_branch[i], 512, jc * 512, 512)
                for jj in range(4):
                    j = jc * 4 + jj
                    for (s0, sl) in slabs:
                        pg, b_pg = K.ps_acc.next()
                        for k in range(16):
                            P.op("pe", lambda e, pg=pg, wg=wg, k=k, jj=jj, s0=s0, sl=sl: e.matmul(pg[:, 0:sl], lhsT=wg[:, k, jj * 128:(jj + 1) * 128], rhs=hTg[:, k, s0:s0 + sl],
                                                                                        start=(k == 0), stop=(k == 15)),
                                 reads=[b_wg, b_hTg], writes=[b_pg], partial=(k > 0))
                        pb, b_pb = K.ps_acc.next()
                        for k in range(4):
                            P.op("pe", lambda e, pb=pb, wb=wb, k=k, jj=jj, s0=s0, sl=sl, i=i: e.matmul(pb[:, 0:sl], lhsT=wb[:, k, jj * 128:(jj + 1) * 128], rhs=oTg[:, i * 4 + k, s0:s0 + sl],
                                                                                             start=(k == 0), stop=(k == 3)),
                                 reads=[b_wb, b_oTg], writes=[b_pb], partial=(k > 0))
                        sg, b_sg = sg_r.next()
                        P.op("act", lambda e, sg=sg, pg=pg, sl=sl, i=i, j=j: e.activation(out=sg[:, 0:sl], in_=pg[:, 0:sl], func=AF.Sigmoid, bias=bgT[:, i * 16 + j:i * 16 + j + 1]),
                             reads=[b_pg, b_bgT], writes=[b_sg])
                        if i == 0:
                            P.op("dve", lambda e, sg=sg, pb=pb, jj=jj, s0=s0, sl=sl: e.tensor_tensor(out=acc[:, jj, s0:s0 + sl], in0=sg[:, 0:sl], in1=pb[:, 0:sl], op=ALU.mult),
                                 reads=[b_sg, b_pb], writes=[b_acc], partial=True)
                        else:
                            tm, b_tm = tm_r.next()
                            P.op("dve", lambda e, tm=tm, sg=sg, pb=pb, sl=sl: e.tensor_tensor(out=tm[:, 0:sl], in0=sg[:, 0:sl], in1=pb[:, 0:sl], op=ALU.mult),
                                 reads=[b_sg, b_pb], writes=[b_tm])
                            P.op("pool", lambda e, tm=tm, jj=jj, s0=s0, sl=sl: e.tensor_tensor(out=acc[:, jj, s0:s0 + sl], in0=acc[:, jj, s0:s0 + sl], in1=tm[:, 0:sl], op=ALU.add),
                                 reads=[b_tm, b_acc], writes=[b_acc], partial=True)
            P.op("act", lambda e, jc=jc: e.copy(out=accT[:, jc * 4:(jc + 1) * 4, :], in_=acc), reads=[b_acc], writes=[b_accT], partial=True)
        barrier(K)
        A.reset(m_s1)
        g1l, b_g1l = load_bcast(K, mod_d[0:1, 2 * D:3 * D], D, "g1l")
        g1c, b_g1c = load_bcast(K, mod_d[1:2, 2 * D:3 * D], D, "g1c")
        wo_r = RR(sbuf_ring(K, "wo", [128, 16, 512], BF16, 2))
        xc_r = RR(sbuf_ring(K, "xc", [128, 512], F32, 3))
        rr_r = RR(sbuf_ring(K, "rr", [128, 512], F32, 3))
        for nch in range(4):
            wo, b_wo = load_w_chunk(K, wo_r, w_out, D, nch * 512, 512)
            for ti, ot in enumerate(tiles):
                po, b_po = K.ps_acc.next()
                for k in range(16):
                    P.op("pe", lambda e, po=po, wo=wo, k=k, ti=ti: e.matmul(po, lhsT=accT[:, k, ti * 128:(ti + 1) * 128], rhs=wo[:, k, :], start=(k == 0), stop=(k == 15)),
                         reads=[b_accT, b_wo], writes=[b_po], partial=(k > 0))
                xc, b_xc = xc_r.next()
                P.dma("act", xc, xrows(ot, nch * 512, (nch + 1) * 512), writes=[b_xc], partial=False)
                rr, b_rr = rr_r.next()
                gb, bgb = (g1l, b_g1l) if ot < 16 else (g1c, b_g1c)
                P.op("dve", lambda e, rr=rr, po=po, gb=gb, nch=nch: e.tensor_tensor(out=rr, in0=po, in1=gb[:, nch * 512:(nch + 1) * 512], op=ALU.mult),
                     reads=[b_po, bgb], writes=[b_rr])
                P.op("dve", lambda e, rr=rr, xc=xc: e.scalar_tensor_tensor(out=rr, in0=xc, scalar=ALPHA, in1=rr, op0=ALU.mult, op1=ALU.add),
                     reads=[b_xc, b_rr], writes=[b_rr])
                P.dma("sp", r_d[ot * 128:(ot + 1) * 128, nch * 512:(nch + 1) * 512], rr, reads=[b_rr], writes=[K.b_r_d])
        barrier(K)
        A.reset(m_grp)
        hT2 = A.sb([128, 16, ng], BF16, "hT2")
        b_hT2 = [Buf(f"hT2_{i}") for i in range(len(tiles))]
        m_s3b = A.mark()
        l1g, b_l1 = load_bcast(K, ln1_g, D, "l1g")
        l1b, _ = load_bcast(K, ln1_b, D, "l1b")
        b_l1b = _
        tmp = {
            "st": RR(sbuf_ring(K, "Dst", [128, 4, 6], F32, 2)), "mv": RR(sbuf_ring(K, "Dmv", [128, 4], F32, 2)),
            "xt": RR(sbuf_ring(K, "Dxt", [128, D], F32, 2)), "xs": RR(sbuf_ring(K, "Dxs", [128, D], BF16, 2)),
            "t1": RR(sbuf_ring(K, "Dt1", [128, 8, 128], F32, 2)),
        }
        xm_r = RR(sbuf_ring(K, "xm", [128, D], F32, 2))
        for ti, ot in enumerate(tiles):
            xm, b_xm = xm_r.next()
            xt, b_xt = tmp["xt"].next()
            P.dma("sp", xt, r_d[ot * 128:(ot + 1) * 128, :], writes=[b_xt], partial=False)
            mv, b_mv = ln_stats(K, xt, b_xt, D, tmp)
            P.op("dve", lambda e, xt=xt, mv=mv: e.tensor_scalar(out=xt, in0=xt, scalar1=mv[:, 0:1], scalar2=mv[:, 2:3], op0=ALU.subtract, op1=ALU.mult),
                 reads=[b_xt, b_mv], writes=[b_xt])
            P.op("pool", lambda e, xt=xt: e.tensor_tensor(out=xt, in0=xt, in1=l1g, op=ALU.mult), reads=[b_xt, b_l1], writes=[b_xt])
            P.op("dve", lambda e, xt=xt, xm=xm: e.tensor_tensor(out=xm, in0=xt, in1=l1b, op=ALU.add), reads=[b_xt, b_l1b], writes=[b_xm])
            P.dma("act", xmid_d[ot * 128:(ot + 1) * 128, :], xm, reads=[b_xm], writes=[K.b_xmid_d])
            mt, bm = (modT_l, b_ml) if ot < 16 else (modT_c, b_mc)
            hT_from_sbuf(K, xm, b_xm, mt, bm, 48, hT2[:, :, ti * 128:(ti + 1) * 128], b_hT2[ti], tmp)
        barrier(K)
        A.reset(m_s3b)
        aT = A.sb([128, 44, ng], BF16, "aT"); b_aT = Buf("aT")
        m_s3 = A.mark()
        wu_r = RR(sbuf_ring(K, "wu", [128, 16, 128], BF16, 2))
        wgt_r = RR(sbuf_ring(K, "wgt", [128, 16, 128], BF16, 2))
        sl_r = RR(sbuf_ring(K, "silu", [128, 512], F32, 2))
        for j in range(44):
            wu, b_wu = load_w_chunk(K, wu_r, w_ffn_in, D, j * 128, 128)
            wgt, b_wgt = load_w_chunk(K, wgt_r, w_ffn_in, D, FFH + j * 128, 128)
            for (s0, sl) in slabs:
                tl = list(range(s0 // 128, (s0 + sl) // 128))
                pu, b_pu = K.ps_acc.next()
                pgt, b_pgt = K.ps_acc.next()
                for k in range(16):
                    P.op("pe", lambda e, pu=pu, wu=wu, k=k, s0=s0, sl=sl: e.matmul(pu[:, 0:sl], lhsT=wu[:, k, :], rhs=hT2[:, k, s0:s0 + sl], start=(k == 0), stop=(k == 15)),
                         reads=[b_wu] + [b_hT2[t] for t in tl], writes=[b_pu], partial=(k > 0))
                for k in range(16):
                    P.op("pe", lambda e, pgt=pgt, wgt=wgt, k=k, s0=s0, sl=sl: e.matmul(pgt[:, 0:sl], lhsT=wgt[:, k, :], rhs=hT2[:, k, s0:s0 + sl], start=(k == 0), stop=(k == 15)),
                         reads=[b_wgt] + [b_hT2[t] for t in tl], writes=[b_pgt], partial=(k > 0))
                sv, b_sv = sl_r.next()
                P.op("act", lambda e, sv=sv, pgt=pgt, sl=sl: e.activation(out=sv[:, 0:sl], in_=pgt[:, 0:sl], func=AF.Silu), reads=[b_pgt], writes=[b_sv])
                P.op("dve", lambda e, sv=sv, pu=pu, j=j, s0=s0, sl=sl: e.tensor_tensor(out=aT[:, j, s0:s0 + sl], in0=sv[:, 0:sl], in1=pu[:, 0:sl], op=ALU.mult),
                     reads=[b_sv, b_pu], writes=[b_aT], partial=True)
        barrier(K)
        A.reset(m_s3)
        g2l, b_g2l = load_bcast(K, mod_d[0:1, 5 * D:6 * D], D, "g2l")
        g2c, b_g2c = load_bcast(K, mod_d[1:2, 5 * D:6 * D], D, "g2c")
        CW = 128
        w2_r = RR(sbuf_ring(K, "w2", [128, 44, CW], BF16, 2))
        xc2_r = RR(sbuf_ring(K, "xc2", [128, CW], F32, 3))
        rr2_r = RR(sbuf_ring(K, "rr2", [128, CW], F32, 3))
        for nch in range(D // CW):
            w2, b_w2 = load_w_chunk(K, w2_r, w_ffn_out, FFH, nch * CW, CW)
            for ti, ot in enumerate(tiles):
                po, b_po = K.ps_acc.next()
                for k in range(44):
                    P.op("pe", lambda e, po=po, w2=w2, k=k, ti=ti: e.matmul(po[:, 0:128], lhsT=aT[:, k, ti * 128:(ti + 1) * 128], rhs=w2[:, k, :], start=(k == 0), stop=(k == 43)),
                         reads=[b_aT, b_w2], writes=[b_po], partial=(k > 0))
                xc, b_xc = xc2_r.next()
                P.dma("act", xc, xmid_d[ot * 128:(ot + 1) * 128, nch * 128:(nch + 1) * 128], reads=[K.b_xmid_d], writes=[b_xc], partial=False)
                rr, b_rr = rr2_r.next()
                gb, bgb = (g2l, b_g2l) if ot < 16 else (g2c, b_g2c)
                P.op("dve", lambda e, rr=rr, po=po, gb=gb, nch=nch: e.tensor_tensor(out=rr, in0=po[:, 0:128], in1=gb[:, nch * 128:(nch + 1) * 128], op=ALU.mult),
                     reads=[b_po, bgb], writes=[b_rr])
                P.op("dve", lambda e, rr=rr, xc=xc: e.scalar_tensor_tensor(out=rr, in0=xc, scalar=ALPHA, in1=rr, op0=ALU.mult, op1=ALU.add),
                     reads=[b_xc, b_rr], writes=[b_rr])
                P.dma("sp", r_d[ot * 128:(ot + 1) * 128, nch * 128:(nch + 1) * 128], rr, reads=[b_rr], writes=[K.b_r_d])
        barrier(K)
        A.reset(m_grp)
        l2g, b_l2g = load_bcast(K, ln2_g, D, "l2g")
        l2b, b_l2b = load_bcast(K, ln2_b, D, "l2b")
        tmp = {"st": RR(sbuf_ring(K, "Est", [128, 4, 6], F32, 2)), "mv": RR(sbuf_ring(K, "Emv", [128, 4], F32, 2)),
               "xt": RR(sbuf_ring(K, "Ext", [128, D], F32, 3))}
        for ti, ot in enumerate(tiles):
            xt, b_xt = tmp["xt"].next()
            P.dma("sp", xt, r_d[ot * 128:(ot + 1) * 128, :], reads=[K.b_r_d], writes=[b_xt], partial=False)
            mv, b_mv = ln_stats(K, xt, b_xt, D, tmp)
            P.op("dve", lambda e, xt=xt, mv=mv: e.tensor_scalar(out=xt, in0=xt, scalar1=mv[:, 0:1], scalar2=mv[:, 2:3], op0=ALU.subtract, op1=ALU.mult),
                 reads=[b_xt, b_mv], writes=[b_xt])
            P.op("pool", lambda e, xt=xt: e.tensor_tensor(out=xt, in0=xt, in1=l2g, op=ALU.mult), reads=[b_xt, b_l2g], writes=[b_xt])
            P.op("dve", lambda e, xt=xt: e.tensor_tensor(out=xt, in0=xt, in1=l2b, op=ALU.add), reads=[b_xt, b_l2b], writes=[b_xt])
            P.dma("act", outrows(ot), xt, reads=[b_xt], writes=[K.b_out])
        barrier(K)
    for tiles in groups:
        do_group(tiles)
    A.reset(m0)


def mamba_consts():
    s = np.arange(128)[:, None]
    l = np.arange(128)[None, :]
    tri = np.stack([(s <= l), (s >= l), np.ones((128, 128), bool)]).astype(np.float32)
    l5 = np.arange(512)[None, :]
    masks = np.zeros((2, 4, 128, 512), np.float32)
    for j in range(4):
        masks[0, j] = (j * 128 + s) <= l5
        masks[1, j] = (j * 128 + s) >= l5
    return tri, masks.astype(ml_dtypes.bfloat16)


def store_oT_tile(K, y_bf, b_y, oT_d, branch, ot, stg):
    P = K.P
    ps, b_ps = K.ps_tp.next()
    psb = ps.bitcast(BF16).rearrange("p (k t) -> p k t", k=8)
    for h in range(4):
        P.I("pe", "transpose", reads=[b_y, K.b_const], writes=[b_ps], partial=(h > 0),
            out=psb[:, h, :], in_=y_bf[:, h * 128:(h + 1) * 128], identity=K.ident)
    so, b_so = stg.next()
    P.I("dve", "tensor_copy", reads=[b_ps], writes=[b_so], out=so, in_=psb[:, 0:4, :])
    P.dma("sp", oT_d[branch * 512:(branch + 1) * 512, ot * 128:(ot + 1) * 128].rearrange("(h p) t -> p h t", p=128), so,
          reads=[b_so], writes=[K.b_oT_d])


def phase_mamba(K, pTM, pFM, conv_w, conv_b, a_log, dt_bias, d_skip, norm_g, tri_d, masks_d, cumT_d, oT_d, need_ctx):
    P, A = K.P, K.A
    m0 = A.mark()
    BT = A.sb([128, 2, NTOK], BF16, "mBT"); b_BT = Buf("mBT")
    CT = A.sb([128, 2, NOWN], BF16, "mCT"); b_CT = Buf("mCT")
    xdt = A.sb([128, 2, 34, 512], BF16, "mxdt"); b_xdt = Buf("mxdt")
    xtm = A.sb([128, 17, 512], BF16, "mxtm"); b_xtm = Buf("mxtm")
    negcum = A.sb([128, 34, 16], F32, "mnegcum"); b_nc = Buf("mnegcum")
    m1 = A.mark()
    tri = A.sb([128, 3, 128], F32, "mtri"); b_tri = Buf("mtri")
    P.dma("sp", tri, tri_d.rearrange("a p l -> p a l"), writes=[b_tri], partial=False)
    craw = A.sb([32, 128], F32, "mcraw"); b_craw = Buf("mcraw")
    P.dma("sp", craw[0:24, :], conv_w.rearrange("k (c p) -> (k c) p", p=128), writes=[b_craw])
    P.dma("sp", craw[24:32, :], conv_b.rearrange("o (c p) -> (o c) p", p=128), writes=[b_craw])
    ps, b_ps = K.ps_misc.next()
    P.I("pe", "transpose", reads=[b_craw, K.b_const], writes=[b_ps], out=ps[:, 0:32], in_=craw, identity=K.ident_f[0:32, 0:32])
    cw = A.sb([128, 32], F32, "mcw"); b_cw = Buf("mcw")
    P.I("dve", "tensor_copy", reads=[b_ps], writes=[b_cw], out=cw, in_=ps[:, 0:32])
    alog, b_alog = load_bcast(K, a_log, 16, "malog")
    dtb, b_dtb = load_bcast(K, dt_bias, 16, "mdtb")
    P.I("act", "activation", reads=[b_alog], writes=[b_alog], out=alog, in_=alog, func=AF.Exp)
    P.I("dve", "tensor_scalar_mul", reads=[b_alog], writes=[b_alog], out=alog, in0=alog, scalar1=-1.0)
    dt = A.sb([128, 34, 16], F32, "mdt"); b_dt = Buf("mdt")
    da = A.sb([128, 34, 16], F32, "mda"); b_da = Buf("mda")
    P.dma("sp", dt, pTM[:, 2176:2192].rearrange("(t p) c -> p t c", p=128), reads=[K.b_pTM], writes=[b_dt], partial=False, allow_slow_non_contiguous=False)
    P.I("dve", "tensor_tensor", reads=[b_dt, b_dtb], writes=[b_dt], out=dt, in0=dt, in1=dtb.unsqueeze(1).broadcast_to([128, 34, 16]), op=ALU.add)
    P.I("act", "activation", reads=[b_dt], writes=[b_dt], out=dt, in_=dt, func=AF.Exp)
    P.I("act", "activation", reads=[b_dt], writes=[b_dt], out=dt, in_=dt, func=AF.Ln, bias=1.0)
    P.I("dve", "tensor_tensor", reads=[b_dt, b_alog], writes=[b_da], out=da, in0=dt, in1=alog.unsqueeze(1).broadcast_to([128, 34, 16]), op=ALU.mult)
    ntot_r = RR(sbuf_ring(K, "mntot", [128, 8], F32, 3))
    orders = [[32, 33] + list(range(32)), [33, 32] + list(range(31, -1, -1))]
    for d in range(2):
        ntot, b_nt = ntot_r.next()
        P.I("pool", "memset", writes=[b_nt], ap=ntot, constant=0.0)
        for T in orders[d]:
            pc, b_pc = K.ps_misc.next()
            P.I("pe", "matmul", reads=[b_tri, b_da], writes=[b_pc], out=pc[:, 0:8], lhsT=tri[:, d, :], rhs=da[:, T, d * 8:(d + 1) * 8], start=True, stop=True)
            P.I("pe", "matmul", reads=[b_tri, b_da], writes=[b_pc], partial=True, out=pc[:, 8:16], lhsT=tri[:, 2, :], rhs=da[:, T, d * 8:(d + 1) * 8], start=True, stop=True)
            P.I("dve", "scalar_tensor_tensor", reads=[b_pc, b_nt], writes=[b_nc], partial=True,
                out=negcum[:, T, d * 8:(d + 1) * 8], in0=pc[:, 0:8], scalar=-1.0, in1=ntot, op0=ALU.mult, op1=ALU.add)
            ntot2, b_nt2 = ntot_r.next()
            P.I("dve", "scalar_tensor_tensor", reads=[b_pc, b_nt], writes=[b_nt2],
                out=ntot2, in0=pc[:, 8:16], scalar=-1.0, in1=ntot, op0=ALU.mult, op1=ALU.add)
            ntot, b_nt = ntot2, b_nt2
    m2 = A.mark()
    cumT = A.sb([16, NTOK], F32, "mcumT"); b_cumT = Buf("mcumT")
    for T in range(34):
        pt_, b_pt_ = K.ps_misc.next()
        P.I("pe", "transpose", reads=[b_nc, K.b_const], writes=[b_pt_], out=pt_[0:16, 0:128], in_=negcum[:, T, :], identity=K.ident_f)
        P.I("dve", "tensor_scalar_mul", reads=[b_pt_], writes=[b_cumT], partial=True, out=cumT[:, T * 128:(T + 1) * 128], in0=pt_[0:16, 0:128], scalar1=-1.0)
    P.dma("sp", cumT_d, cumT, reads=[b_cumT], writes=[K.b_cumT_d], partial=False)
    barrier(K)
    A.reset(m2)
    pf_r = RR(sbuf_ring(K, "mpf", [128, NTOK], F32, 2))
    u = A.sb([128, NTOK], F32, "mu"); b_u = Buf("mu")
    xTb_r = RR(sbuf_ring(K, "mxTb", [128, NTOK], BF16, 2))
    segs = [(0, NLAT), (NLAT, NTOK)]
    for c in range(8):
        pf, b_pf = pf_r.next()
        P.dma("sp" if c % 2 == 0 else "act", pf, pFM[1536 + c * 128:1536 + (c + 1) * 128, :], reads=[K.b_pFM], writes=[b_pf], partial=False)
        P.I("act", "activation", reads=[b_pf, b_cw], writes=[b_u], out=u, in_=pf, func=AF.Identity, scale=cw[:, 8 + c:9 + c], bias=cw[:, 24 + c:25 + c])
        for (a, b) in segs:
            P.I("dve", "scalar_tensor_tensor", reads=[b_pf, b_cw, b_u], writes=[b_u], partial=True,
                out=u[:, a + 1:b], in0=pf[:, a:b - 1], scalar=cw[:, c:c + 1], in1=u[:, a + 1:b], op0=ALU.mult, op1=ALU.add)
            P.I("dve", "scalar_tensor_tensor", reads=[b_pf, b_cw, b_u], writes=[b_u], partial=True,
                out=u[:, a:b - 1], in0=pf[:, a + 1:b], scalar=cw[:, 16 + c:17 + c], in1=u[:, a:b - 1], op0=ALU.mult, op1=ALU.add)
        if c < 4:
            xTb, b_xTb = xTb_r.next()
            P.I("act", "activation", reads=[b_u], writes=[b_xTb], out=xTb, in_=u, func=AF.Silu)
            for T0 in range(0, 34, 8):
                nT = min(8, 34 - T0)
                ps2, b_ps2 = K.ps_tp.next()
                psb = ps2.bitcast(BF16).rearrange("p (k t) -> p k t", k=8)
                for i in range(nT):
                    P.I("pe", "transpose", reads=[b_xTb, K.b_const], writes=[b_ps2], partial=(i > 0),
                        out=psb[:, i, :], in_=xTb[:, (T0 + i) * 128:(T0 + i + 1) * 128], identity=K.ident)
                src = psb[:, 0:nT, :].rearrange("p t (h e) -> p t h e", h=2)
                for d in range(2):
                    P.I("dve", "tensor_tensor", reads=[b_ps2, b_dt], writes=[b_xdt], partial=True,
                        out=xdt[:, d, T0:T0 + nT, c * 128:(c + 1) * 128].rearrange("p t (h e) -> p t h e", h=2), in0=src,
                        in1=dt[:, T0:T0 + nT, d * 8 + 2 * c:d * 8 + 2 * c + 2].unsqueeze(3).broadcast_to([128, nT, 2, 64]), op=ALU.mult)
                if T0 < 16:
                    P.I("dve", "tensor_copy", reads=[b_ps2], writes=[b_xtm], partial=True, out=xtm[:, T0:T0 + 8, c * 128:(c + 1) * 128], in_=psb[:, 0:8, :])
                if T0 == 32:
                    P.I("dve", "tensor_copy", reads=[b_ps2], writes=[b_xtm], partial=True, out=xtm[:, 16, c * 128:(c + 1) * 128], in_=psb[:, 0, :])
        elif c < 6:
            P.I("act", "activation", reads=[b_u], writes=[b_BT], partial=True, out=BT[:, c - 4, :], in_=u, func=AF.Silu)
        else:
            P.I("act", "activation", reads=[b_u], writes=[b_CT], partial=True, out=CT[:, c - 6, 0:OWN_LAT], in_=u[:, 0:OWN_LAT], func=AF.Silu)
            P.I("act", "activation", reads=[b_u], writes=[b_CT], partial=True, out=CT[:, c - 6, OWN_LAT:NOWN], in_=u[:, NLAT:NLAT + OWN_CTX], func=AF.Silu)
    barrier(K)
    A.reset(m1)
    ysum = A.sb([128, 17, 512], F32, "mysum"); b_ys = Buf("mysum")
    masks = A.sb([128, 2, 4, 512], BF16, "mmask"); b_mk = Buf("mmask")
    P.dma("sp", masks, masks_d.rearrange("d j p l -> p d j l"), writes=[b_mk], partial=False)
    crow_r = RR(sbuf_ring(K, "mcrow", [128, NOWN], F32, 2))
    dec_r = RR(sbuf_ring(K, "mdec", [128, 512], F32, 2))
    pt_r = RR(sbuf_ring(K, "mpt", [128, 512], BF16, 3))
    obanks = K.banks[4:8]
    qgroups = []
    for i in range(4):
        qgroups.append((i * 512, 512, i * 4, False))
    if need_ctx:
        qgroups.append((OWN_LAT, 128, 32, True))
    for d in range(2):
        for h in range(8):
            g = h // 4
            col = d * 8 + h
            crow, b_crow = crow_r.next()
            P.dma("sp", crow[:, 0:OWN_LAT], cumT_d[col:col + 1, 0:OWN_LAT].partition_broadcast(128), reads=[K.b_cumT_d], writes=[b_crow], partial=False)
            P.dma("act", crow[:, OWN_LAT:NOWN], cumT_d[col:col + 1, NLAT:NLAT + OWN_CTX].partition_broadcast(128), reads=[K.b_cumT_d], writes=[b_crow])
            for (q0, nq, T0, is_ctx) in qgroups:
                if not is_ctx:
                    if d == 0:
                        blocks = [(32, None), (33, None)] + [(T, None) for T in range(T0)] + [(T0 + j, j) for j in range(4)]
                    else:
                        blocks = [(32, None), (33, None)] + [(T, None) for T in range(31, T0 + 3, -1)] + [(T0 + j, j) for j in range(4)]
                else:
                    blocks = [(32, 0)] if d == 0 else [(33, None), (32, 0)]
                nqs = nq // 128
                for bi, (kb, dj) in enumerate(blocks):
                    pc, b_pc = K.ps_acc.next()
                    P.I("pe", "matmul", reads=[b_BT, b_CT], writes=[b_pc], out=pc[:, 0:nq], lhsT=BT[:, g, kb * 128:(kb + 1) * 128], rhs=CT[:, g, q0:q0 + nq], start=True, stop=True)
                    dec, b_dec = dec_r.next()
                    if dj is None:
                        P.I("act", "activation", reads=[b_crow, b_nc], writes=[b_dec], out=dec[:, 0:nq], in_=crow[:, q0:q0 + nq], func=AF.Exp, bias=negcum[:, kb, col:col + 1])
                    else:
                        P.I("dve", "tensor_scalar", reads=[b_crow, b_nc], writes=[b_dec], out=dec[:, 0:nq], in0=crow[:, q0:q0 + nq], scalar1=negcum[:, kb, col:col + 1], scalar2=0.0,
                            op0=ALU.add, op1=ALU.min)
                        P.I("act", "activation", reads=[b_dec], writes=[b_dec], out=dec[:, 0:nq], in_=dec[:, 0:nq], func=AF.Exp)
                    pt, b_pt = pt_r.next()
                    P.I("dve", "tensor_tensor", reads=[b_pc, b_dec], writes=[b_pt], out=pt[:, 0:nq], in0=pc[:, 0:nq], in1=dec[:, 0:nq], op=ALU.mult)
                    if dj is not None:
                        P.I("pool", "tensor_tensor", reads=[b_pt, b_mk], writes=[b_pt], out=pt[:, 0:nq], in0=pt[:, 0:nq], in1=masks[:, d, dj, 0:nq], op=ALU.mult)
                    for qs in range(nqs):
                        ob, b_ob = obanks[qs]
                        P.I("pe", "matmul", reads=[b_pt, b_xdt], writes=[b_ob], partial=(bi > 0), out=ob[:, 0:64], lhsT=pt[:, qs * 128:(qs + 1) * 128],
                            rhs=xdt[:, d, kb, h * 64:(h + 1) * 64], start=(bi == 0), stop=(bi == len(blocks) - 1))
                for qs in range(nqs):
                    ob, b_ob = obanks[qs]
                    ot = q0 // 128 + qs
                    if d == 0:
                        P.I("dve", "tensor_copy", reads=[b_ob], writes=[b_ys], partial=True, out=ysum[:, ot, h * 64:(h + 1) * 64], in_=ob[:, 0:64])
                    else:
                        P.I("dve", "tensor_tensor", reads=[b_ob, b_ys], writes=[b_ys], partial=True, out=ysum[:, ot, h * 64:(h + 1) * 64], in0=ob[:, 0:64],
                            in1=ysum[:, ot, h * 64:(h + 1) * 64], op=ALU.add)
    dsk, b_dsk = load_bcast(K, d_skip, 8, "mdsk")
    ng, b_ng = load_bcast(K, norm_g, 512, "mng")
    z_r = RR(sbuf_ring(K, "mz", [128, 512], F32, 2))
    t_r = RR(sbuf_ring(K, "mt", [128, 512], F32, 2))
    ss_r = RR(sbuf_ring(K, "mss", [128, 2], F32, 3))
    yb_r = RR(sbuf_ring(K, "myb", [128, 512], BF16, 2))
    stg = RR(sbuf_ring(K, "mstg", [128, 4, 128], BF16, 2))
    for (T, ot) in own_tiles(need_ctx):
        z, b_z = z_r.next()
        P.dma("sp", z, pTM[T * 128:(T + 1) * 128, 1664:2176], reads=[K.b_pTM], writes=[b_z], partial=False)
        P.I("act", "activation", reads=[b_z], writes=[b_z], out=z, in_=z, func=AF.Silu)
        t, b_t = t_r.next()
        P.I("dve", "tensor_tensor", reads=[b_xtm, b_dsk], writes=[b_t], out=t.rearrange("p (h e) -> p h e", h=8), in0=xtm[:, ot, :].rearrange("p (h e) -> p h e", h=8),
            in1=dsk.unsqueeze(2).broadcast_to([128, 8, 64]), op=ALU.mult)
        P.I("pool", "tensor_tensor", reads=[b_t, b_ys], writes=[b_t], out=t, in0=t, in1=ysum[:, ot, :], op=ALU.add)
        P.I("dve", "tensor_tensor", reads=[b_t, b_z], writes=[b_t], out=t, in0=t, in1=z, op=ALU.mult)
        P.I("pool", "tensor_tensor", reads=[b_t], writes=[b_z], out=z, in0=t, in1=t, op=ALU.mult)
        ss, b_ss = ss_r.next()
        P.I("dve", "reduce_sum", reads=[b_z], writes=[b_ss], out=ss[:, 0:1], in_=z, axis=AX.X)
        rms_rstd(K, ss, b_ss, 1, 512)
        yb, b_yb = yb_r.next()
        P.I("dve", "scalar_tensor_tensor", reads=[b_t, b_ss, b_ng], writes=[b_yb], out=yb, in0=t, scalar=ss[:, 0:1], in1=ng, op0=ALU.mult, op1=ALU.mult)
        store_oT_tile(K, yb, b_yb, oT_d, 3, ot, stg)
    barrier(K)
    A.reset(m0)


def hyena_consts(n, n_own):
    N = 2 * n
    nb = n + 1
    F = (nb + 127) // 128 * 128
    s = np.arange(n, dtype=np.float64)
    f = np.arange(F, dtype=np.float64)
    valid = (f < nb)
    th = 2 * np.pi * np.outer(s, f) / N
    CT = np.cos(th) * valid[None, :]
    ST = np.sin(th) * valid[None, :]
    t = np.arange(n_own, dtype=np.float64)
    w = np.where((f == 0) | (f == n), 1.0, 2.0) * valid
    th2 = 2 * np.pi * np.outer(f, t) / N
    Ci = (w[:, None] * np.cos(th2)) / N
    Si = (w[:, None] * np.sin(th2)) / N
    tt = np.linspace(0.0, 1.0, n, dtype=np.float32)[:, None]
    omega = (2.0 * math.pi * np.arange(n, dtype=np.float32) / n).astype(np.float32)
    bands = np.linspace(1e-4, 15, 16, dtype=np.float32)
    ang = omega[:, None] * bands[None, :]
    feats = np.concatenate([tt, np.cos(ang), -np.sin(ang)], -1).astype(np.float32)
    negt = (-tt[:, 0]).astype(np.float32).reshape(n // 128, 128).T.copy()
    bf = ml_dtypes.bfloat16
    return {"CT": CT.astype(bf), "ST": ST.astype(bf), "Ci": Ci.astype(bf), "Si": Si.astype(bf),
            "featsT": np.ascontiguousarray(feats.T), "negt": negt}


def hy_sin(K, dst, src_ps, rows, w, fcol, fbcol, b_src, b_dst, b_par, tmp):
    P = K.P
    a, b_a = tmp.next()
    s2, b_s2 = tmp.next()
    s4, b_s4 = tmp.next()
    a, s2, s4 = a[0:rows, 0:w], s2[0:rows, 0:w], s4[0:rows, 0:w]
    P.I("dve", "tensor_scalar", reads=[b_src, b_par], writes=[b_a], out=a, in0=src_ps, scalar1=fcol, scalar2=fbcol, op0=ALU.mult, op1=ALU.add)
    P.I("act", "activation", reads=[b_a], writes=[b_s2], out=s2, in_=a, func=AF.Sin, scale=0.5)
    P.I("act", "activation", reads=[b_a], writes=[b_s4], out=s4, in_=a, func=AF.Sin, scale=0.25)
    P.I("dve", "tensor_tensor", reads=[b_s4], writes=[b_s4], out=s4, in0=s4, in1=s4, op=ALU.mult)
    P.I("dve", "tensor_scalar", reads=[b_s4], writes=[b_s4], out=s4, in0=s4, scalar1=-2.0, scalar2=1.0, op0=ALU.mult, op1=ALU.add)
    P.I("dve", "scalar_tensor_tensor", reads=[b_s2, b_s4], writes=[b_dst], partial=True, out=dst, in0=s2, scalar=2.0, in1=s4, op0=ALU.mult, op1=ALU.mult)


def phase_hyena(K, pFM, conv_w, conv_b, w1, b1, w2, b2, w3, freq, skip, sgn_d, deltas_d, geoms, oT_d):
    P, A = K.P, K.A
    m0 = A.mark()
    craw = A.sb([52, 128], F32, "hcraw"); b_craw = Buf("hcraw")
    P.dma("sp", craw[0:36, :], conv_w.rearrange("k (c p) -> (k c) p", p=128), writes=[b_craw])
    P.dma("sp", craw[36:48, :], conv_b.rearrange("o (c p) -> (o c) p", p=128), writes=[b_craw])
    P.dma("sp", craw[48:52, :], skip.rearrange("o (c p) -> (o c) p", p=128), writes=[b_craw])
    ps, b_ps = K.ps_misc.next()
    P.I("pe", "transpose", reads=[b_craw, K.b_const], writes=[b_ps], out=ps[:, 0:52], in_=craw, identity=K.ident_f[0:52, 0:52])
    cw = A.sb([128, 52], F32, "hcw"); b_cw = Buf("hcw")
    P.I("dve", "tensor_copy", reads=[b_ps], writes=[b_cw], out=cw, in_=ps[:, 0:52])
    w1s = A.sb([33, 64], F32, "hw1"); w2s = A.sb([64, 64], F32, "hw2"); w3s = A.sb([64, 1024], F32, "hw3"); b_w = Buf("hw")
    P.dma("sp", w1s, w1, writes=[b_w]); P.dma("sp", w2s, w2, writes=[b_w]); P.dma("sp", w3s, w3, writes=[b_w])
    par = A.sb([64, 8], F32, "hpar"); b_par = Buf("hpar")
    P.dma("sp", par[:, 0:1], freq.rearrange("o k -> k o"), writes=[b_par], allow_slow_non_contiguous=True)
    P.dma("sp", par[:, 1:2], b1.rearrange("o k -> k o"), writes=[b_par], allow_slow_non_contiguous=True)
    P.dma("sp", par[:, 2:3], b2.rearrange("o k -> k o"), writes=[b_par], allow_slow_non_contiguous=True)
    P.I("dve", "tensor_tensor", reads=[b_par], writes=[b_par], partial=True, out=par[:, 3:4], in0=par[:, 0:1], in1=par[:, 1:2], op=ALU.mult)
    P.I("dve", "tensor_tensor", reads=[b_par], writes=[b_par], partial=True, out=par[:, 4:5], in0=par[:, 0:1], in1=par[:, 2:3], op=ALU.mult)
    deltab, b_dl = load_bcast(K, deltas_d, 512, "hdelta")
    sgn, b_sgn = load_bcast(K, sgn_d, 1, "hsgn")
    m_g = A.mark()
    for G in geoms:
        A.reset(m_g)
        hyena_geom(K, G, pFM, cw, b_cw, w1s, w2s, w3s, b_w, par, b_par, deltab, b_dl, sgn, b_sgn, oT_d)
    barrier(K)
    A.reset(m0)


def hyena_geom(K, G, pFM, cw, b_cw, w1s, w2s, w3s, b_w, par, b_par, deltab, b_dl, sgn, b_sgn, oT_d):
    P, A = K.P, K.A
    n, col0, own_off, n_own = G["n"], G["col0"], G["own_off"], G["n_own"]
    nsc = n // 128
    F = G["CT"].shape[1]
    nfb = F // 128
    negt = A.sb([128, nsc], F32, "hnegt"); b_negt = Buf("hnegt")
    P.dma("sp", negt, G["negt"], writes=[b_negt], partial=False)
    hdn2T = A.sb([64, n], F32, "hhdn2"); b_h2 = Buf("hhdn2")
    m_a = A.mark()
    featsT = A.sb([33, n], F32, "hfeat"); b_ft = Buf("hfeat")
    P.dma("sp", featsT, G["featsT"], writes=[b_ft], partial=False)
    hdn1T = A.sb([64, n], F32, "hhdn1"); b_h1 = Buf("hhdn1")
    tmp = RR(sbuf_ring(K, "hsin", [64, 512], F32, 6))
    for s0 in range(0, n, 512):
        sl = min(512, n - s0)
        pz, b_pz = K.ps_misc.next()
        P.I("pe", "matmul", reads=[b_w, b_ft], writes=[b_pz], out=pz[0:64, 0:sl], lhsT=w1s, rhs=featsT[:, s0:s0 + sl], start=True, stop=True)
        hy_sin(K, hdn1T[:, s0:s0 + sl], pz[0:64, 0:sl], 64, sl, par[:, 0:1], par[:, 3:4], b_pz, b_h1, b_par, tmp)
    for s0 in range(0, n, 512):
        sl = min(512, n - s0)
        pz, b_pz = K.ps_misc.next()
        P.I("pe", "matmul", reads=[b_w, b_h1], writes=[b_pz], out=pz[0:64, 0:sl], lhsT=w2s, rhs=hdn1T[:, s0:s0 + sl], start=True, stop=True)
        hy_sin(K, hdn2T[:, s0:s0 + sl], pz[0:64, 0:sl], 64, sl, par[:, 0:1], par[:, 4:5], b_pz, b_h2, b_par, tmp)
    barrier(K)
    A.reset(m_a)
    m_h = A.mark()
    for hc in range(2):
        A.reset(m_h)
        hyena_half(K, G, hc, pFM, cw, b_cw, w3s, b_w, deltab, b_dl, sgn, b_sgn, negt, b_negt, hdn2T, b_h2, oT_d)


def hyena_half(K, G, hc, pFM, cw, b_cw, w3s, b_w, deltab, b_dl, sgn, b_sgn, negt, b_negt, hdn2T, b_h2, oT_d):
    P, A = K.P, K.A
    n, col0, own_off, n_own = G["n"], G["col0"], G["own_off"], G["n_own"]
    nsc = n // 128
    F = G["CT"].shape[1]
    nfb = F // 128
    sfilt = A.sb([128, nsc, 256], BF16, "hsf"); b_sf = Buf("hsf")
    dfilt = A.sb([128, nsc, 256], BF16, "hdf"); b_df = Buf("hdf")
    vv_tm = A.sb([128, nsc, 256], BF16, "hvv"); b_vv = Buf("hvv")
    vv_own = A.sb([128, 2, n_own], BF16, "hvvo"); b_vvo = Buf("hvvo")
    x0_own = A.sb([128, 2, n_own], BF16, "hx0o"); b_x0o = Buf("hx0o")
    Yre = A.sb([128, nfb, 256], BF16, "hYre"); b_Yre = Buf("hYre")
    Yim = A.sb([128, nfb, 256], BF16, "hYim"); b_Yim = Buf("hYim")
    m_t = A.mark()
    win_r = RR(sbuf_ring(K, "hwin", [128, 256], F32, 2))
    hw_r = RR(sbuf_ring(K, "hhw", [128, 2, 256], F32, 2))
    w3v = w3s.rearrange("k (d c) -> k d c", d=2)[:, :, hc * 256:(hc + 1) * 256]
    for T in range(nsc):
        pf_, b_pf_ = K.ps_acc.next()
        pfv = pf_.rearrange("p (d c) -> p d c", d=2)
        P.I("pe", "matmul", reads=[b_w, b_h2], writes=[b_pf_], out=pfv, lhsT=hdn2T[:, T * 128:(T + 1) * 128], rhs=w3v, start=True, stop=True)
        win, b_win = win_r.next()
        P.I("act", "activation", reads=[b_dl, b_negt], writes=[b_win], out=win, in_=deltab[:, hc * 256:(hc + 1) * 256], func=AF.Exp, scale=negt[:, T:T + 1])
        hw, b_hw = hw_r.next()
        P.I("dve", "tensor_tensor", reads=[b_pf_, b_win], writes=[b_hw], out=hw, in0=pfv, in1=win.unsqueeze(1).broadcast_to([128, 2, 256]), op=ALU.mult)
        if T == 0:
            P.I("pool", "memset", reads=[], writes=[b_hw], partial=True, ap=hw[0:1, 1, :], constant=0.0)
        P.I("pool", "tensor_tensor", reads=[b_hw], writes=[b_sf], partial=True, out=sfilt[:, T, :], in0=hw[:, 0, :], in1=hw[:, 1, :], op=ALU.add)
        P.I("dve", "tensor_tensor", reads=[b_hw], writes=[b_df], partial=True, out=dfilt[:, T, :], in0=hw[:, 0, :], in1=hw[:, 1, :], op=ALU.subtract)
    barrier(K)
    A.reset(m_t)
    pf_r = RR(sbuf_ring(K, "hpf", [128, n], F32, 2))
    u_r = RR(sbuf_ring(K, "hu", [128, n], F32, 2))
    vvb = A.sb([128, n], BF16, "hvvb"); b_vvb = Buf("hvvb")

    def conv_chunk(c):
        pf, b_pf = pf_r.next()
        u, b_u = u_r.next()
        P.dma("sp" if c % 2 == 0 else "act", pf, pFM[c * 128:(c + 1) * 128, col0:col0 + n], reads=[K.b_pFM], writes=[b_pf], partial=False)
        P.I("act", "activation", reads=[b_pf, b_cw], writes=[b_u], out=u, in_=pf, func=AF.Identity, scale=cw[:, 12 + c:13 + c], bias=cw[:, 36 + c:37 + c])
        P.I("dve", "scalar_tensor_tensor", reads=[b_pf, b_cw, b_u], writes=[b_u], partial=True,
            out=u[:, 1:n], in0=pf[:, 0:n - 1], scalar=cw[:, c:c + 1], in1=u[:, 1:n], op0=ALU.mult, op1=ALU.add)
        P.I("dve", "scalar_tensor_tensor", reads=[b_pf, b_cw, b_u], writes=[b_u], partial=True,
            out=u[:, 0:n - 1], in0=pf[:, 1:n], scalar=cw[:, 24 + c:25 + c], in1=u[:, 0:n - 1], op0=ALU.mult, op1=ALU.add)
        return u, b_u
    for j in range(2):
        cc = 2 * hc + j
        ux, b_ux = conv_chunk(4 + cc)
        uv, b_uv = conv_chunk(8 + cc)
        P.I("pool", "tensor_tensor", reads=[b_ux, b_uv], writes=[b_uv], out=uv, in0=uv, in1=ux, op=ALU.mult)
        P.I("act", "copy", reads=[b_uv], writes=[b_vvb], out=vvb, in_=uv)
        P.I("dve", "tensor_copy", reads=[b_uv], writes=[b_vvo], partial=True, out=vv_own[:, j, :], in_=uv[:, 0:n_own])
        for T0 in range(0, nsc, 8):
            nT = min(8, nsc - T0)
            ps2, b_ps2 = K.ps_tp.next()
            psb = ps2.bitcast(BF16).rearrange("p (k t) -> p k t", k=8)
            for i in range(nT):
                P.I("pe", "transpose", reads=[b_vvb, K.b_const], writes=[b_ps2], partial=(i > 0),
                    out=psb[:, i, :], in_=vvb[:, (T0 + i) * 128:(T0 + i + 1) * 128], identity=K.ident)
            P.I("dve", "tensor_copy", reads=[b_ps2], writes=[b_vv], partial=True, out=vv_tm[:, T0:T0 + nT, j * 128:(j + 1) * 128], in_=psb[:, 0:nT, :])
        u0, b_u0 = conv_chunk(cc)
        P.I("act", "copy", reads=[b_u0], writes=[b_x0o], partial=True, out=x0_own[:, j, :], in_=u0[:, 0:n_own])
    barrier(K)
    A.reset(m_t)
    cst_r = RR(sbuf_ring(K, "hcst", [128, 2, nsc, 128], BF16, 3))
    ks_r = RR(sbuf_ring(K, "hks", [128, 2, 256], F32, 2))
    tt_r = RR(sbuf_ring(K, "htt", [128, 4, 256], F32, 2))
    for fb in range(nfb):
        cst, b_cst = cst_r.next()
        P.dma("sp", cst[:, 0, :, :], G["CT"][:, fb * 128:(fb + 1) * 128].rearrange("(k p) f -> p k f", p=128), writes=[b_cst], partial=False)
        P.dma("act", cst[:, 1, :, :], G["ST"][:, fb * 128:(fb + 1) * 128].rearrange("(k p) f -> p k f", p=128), writes=[b_cst])
        pa, b_pa = K.ps_acc.next(); pb, b_pb = K.ps_acc.next(); pck, b_pck = K.ps_acc.next(); pdk, b_pdk = K.ps_acc.next()
        for (po, b_po, ci, rhs, b_rhs) in ((pa, b_pa, 0, vv_tm, b_vv), (pb, b_pb, 1, vv_tm, b_vv), (pck, b_pck, 0, sfilt, b_sf), (pdk, b_pdk, 1, dfilt, b_df)):
            for k in range(nsc):
                P.I("pe", "matmul", reads=[b_cst, b_rhs], writes=[b_po], partial=(k > 0), out=po[:, 0:256], lhsT=cst[:, ci, k, :], rhs=rhs[:, k, :],
                    start=(k == 0), stop=(k == nsc - 1))
        ks, b_ks = ks_r.next()
        P.I("act", "copy", reads=[b_pck], writes=[b_ks], partial=True, out=ks[:, 0, :], in_=pck[:, 0:256])
        P.I("act", "activation", reads=[b_pdk, b_sgn], writes=[b_ks], partial=True, out=ks[:, 1, :], in_=pdk[:, 0:256], func=AF.Copy, scale=sgn[:, 0:1])
        tt, b_tt = tt_r.next()
        P.I("dve", "tensor_tensor", reads=[b_pa, b_ks], writes=[b_tt], partial=True, out=tt[:, 0, :], in0=pa[:, 0:256], in1=ks[:, 0, :], op=ALU.mult)
        P.I("dve", "tensor_tensor", reads=[b_pb, b_ks], writes=[b_tt], partial=True, out=tt[:, 1, :], in0=pb[:, 0:256], in1=ks[:, 1, :], op=ALU.mult)
        P.I("dve", "tensor_tensor", reads=[b_pa, b_ks], writes=[b_tt], partial=True, out=tt[:, 2, :], in0=pa[:, 0:256], in1=ks[:, 1, :], op=ALU.mult)
        P.I("dve", "tensor_tensor", reads=[b_pb, b_ks], writes=[b_tt], partial=True, out=tt[:, 3, :], in0=pb[:, 0:256], in1=ks[:, 0, :], op=ALU.mult)
        P.I("pool", "tensor_tensor", reads=[b_tt], writes=[b_Yre], partial=True, out=Yre[:, fb, :], in0=tt[:, 0, :], in1=tt[:, 1, :], op=ALU.subtract)
        P.I("pool", "tensor_tensor", reads=[b_tt], writes=[b_Yim], partial=True, out=Yim[:, fb, :], in0=tt[:, 2, :], in1=tt[:, 3, :], op=ALU.add)
    barrier(K)
    A.reset(m_t)
    FP = 11
    ic_r = RR(sbuf_ring(K, "hic", [128, 2, FP, 512], BF16, 3))
    yt_r = RR(sbuf_ring(K, "hyt", [128, 512], F32, 2))
    yo_r = RR(sbuf_ring(K, "hyo", [128, 512], BF16, 3))
    pieces = [(f0, min(FP, nfb - f0)) for f0 in range(0, nfb, FP)]
    for t0 in range(0, n_own, 512):
        tl = min(512, n_own - t0)
        accs = [K.ps_acc.next(), K.ps_acc.next()]
        for pi_, (f0, nf) in enumerate(pieces):
            ic, b_ic = ic_r.next()
            P.dma("sp", ic[:, 0, 0:nf, 0:tl], G["Ci"][f0 * 128:(f0 + nf) * 128, t0:t0 + tl].rearrange("(k p) t -> p k t", p=128), writes=[b_ic], partial=False)
            P.dma("act", ic[:, 1, 0:nf, 0:tl], G["Si"][f0 * 128:(f0 + nf) * 128, t0:t0 + tl].rearrange("(k p) t -> p k t", p=128), writes=[b_ic])
            for j in range(2):
                acc, b_acc = accs[j]
                for k in range(nf):
                    first = (pi_ == 0 and k == 0)
                    last = (pi_ == len(pieces) - 1 and k == nf - 1)
                    P.I("pe", "matmul", reads=[b_Yre, b_ic], writes=[b_acc], partial=(not first), out=acc[:, 0:tl], lhsT=Yre[:, f0 + k, j * 128:(j + 1) * 128],
                        rhs=ic[:, 0, k, 0:tl], start=first, stop=False)
                    P.I("pe", "matmul", reads=[b_Yim, b_ic], writes=[b_acc], partial=True, out=acc[:, 0:tl], lhsT=Yim[:, f0 + k, j * 128:(j + 1) * 128],
                        rhs=ic[:, 1, k, 0:tl], start=False, stop=last)
        for j in range(2):
            acc, b_acc = accs[j]
            cc = 2 * hc + j
            yt, b_yt = yt_r.next()
            P.I("dve", "scalar_tensor_tensor", reads=[b_vvo, b_cw, b_acc], writes=[b_yt], out=yt[:, 0:tl], in0=vv_own[:, j, t0:t0 + tl], scalar=cw[:, 48 + cc:49 + cc],
                in1=acc[:, 0:tl], op0=ALU.mult, op1=ALU.add)
            yo, b_yo = yo_r.next()
            P.I("pool", "tensor_tensor", reads=[b_yt, b_x0o], writes=[b_yo], out=yo[:, 0:tl], in0=yt[:, 0:tl], in1=x0_own[:, j, t0:t0 + tl], op=ALU.mult)
            P.dma("sp", oT_d[1024 + cc * 128:1024 + (cc + 1) * 128, own_off + t0:own_off + t0 + tl], yo[:, 0:tl], reads=[b_yo], writes=[K.b_oT_d])
    barrier(K)


from concourse.bass_utils import run_bass_kernel_spmd

NOWN = OWN_LAT + OWN_CTX


def build_layer(need_ctx):
    nc = bass.Bass("TRN2", target_bir_lowering=False)

    def dt(n, s, d=F32, kind="ExternalInput"):
        return nc.dram_tensor(n, s, d, kind=kind).ap()
    x_lat = dt("x_lat", [NLAT, D]); x_ctx = dt("x_ctx", [NCTX, D]); c2 = dt("c2", [2, D])
    w_ada = dt("w_ada", [D, 6 * D]); b_ada = dt("b_ada", [1, 6 * D]); w_in = dt("w_in", [D, C_END])
    consts = {"ident_bf": dt("ident_bf", [128, 128], BF16), "ident_f": dt("ident_f", [128, 128])}
    ropeA = dt("ropeA", [NLAT, 2, 32]); ropeB = dt("ropeB", [NLAT, 2, 64])
    qn = dt("mla_q_norm", [1, 448]); kvn = dt("mla_kv_norm", [1, 128]); wuq = dt("mla_w_uq", [448, 768]); wukv = dt("mla_w_ukv", [128, 1024])
    gq = dt("gqa_q_norm", [1, 128]); gk = dt("gqa_k_norm", [1, 128])
    w_mgate = dt("w_mgate", [4, D, D]); b_mgate = dt("b_mgate", [4, D]); w_branch = dt("w_branch", [4, 512, D]); w_out = dt("w_out", [D, D])
    ln1_g = dt("ln1_g", [1, D]); ln1_b = dt("ln1_b", [1, D]); ln2_g = dt("ln2_g", [1, D]); ln2_b = dt("ln2_b", [1, D])
    w_ffn_in = dt("w_ffn_in", [D, 2 * FFH]); w_ffn_out = dt("w_ffn_out", [FFH, D])
    hy_conv_w = dt("hy_conv_w", [3, 1536]); hy_conv_b = dt("hy_conv_b", [1, 1536]); hy_w1 = dt("hy_w1", [33, 64]); hy_b1 = dt("hy_b1", [1, 64])
    hy_w2 = dt("hy_w2", [64, 64]); hy_b2 = dt("hy_b2", [1, 64]); hy_w3 = dt("hy_w3", [64, 1024]); hy_freq = dt("hy_freq", [1, 64]); hy_skip = dt("hy_skip", [1, 512])
    hy_sgn = dt("hy_sgn", [1, 1]); hy_deltas = dt("hy_deltas", [1, 512])
    geoms = []
    for nm, n, col0, own_off, n_own in (("L", 4096, 0, 0, 2048), ("C", 256, 4096, 2048, 128)):
        if nm == "C" and not need_ctx:
            continue
        Fp = (n + 1 + 127) // 128 * 128
        G = {"n": n, "col0": col0, "own_off": own_off, "n_own": n_own,
             "CT": dt(f"hy{nm}_CT", [n, Fp], BF16), "ST": dt(f"hy{nm}_ST", [n, Fp], BF16),
             "Ci": dt(f"hy{nm}_Ci", [Fp, n_own], BF16), "Si": dt(f"hy{nm}_Si", [Fp, n_own], BF16),
             "featsT": dt(f"hy{nm}_featsT", [33, n]), "negt": dt(f"hy{nm}_negt", [128, n // 128])}
        geoms.append(G)
    mb_conv_w = dt("mb_conv_w", [3, 1024]); mb_conv_b = dt("mb_conv_b", [1, 1024]); mb_a_log = dt("mb_a_log", [1, 16]); mb_dt_bias = dt("mb_dt_bias", [1, 16])
    mb_d = dt("mb_d", [1, 8]); mb_norm = dt("mb_norm", [1, 512]); mb_tri = dt("mb_tri", [3, 128, 128]); mb_masks = dt("mb_masks", [2, 4, 128, 512], BF16)
    cumT_d = dt("cumT_d", [16, NTOK], F32, "Internal")
    mod_d = dt("mod_d", [2, 6 * D], F32, "Internal")
    pTM = dt("pTM", [NTOK, TMW], F32, "Internal")
    pFM = dt("pFM", [FMW, NTOK], F32, "Internal")
    hT_d = dt("hT_d", [128, 16, NOWN], BF16, "Internal")
    oT_d = dt("oT_d", [2048, NOWN], BF16, "Internal")
    r_d = dt("r_d", [NOWN, D], F32, "Internal")
    xmid_d = dt("xmid_d", [NOWN, D], F32, "Internal")
    out_lat = dt("out_lat", [OWN_LAT, D], F32, "ExternalOutput")
    out_ctx = dt("out_ctx", [OWN_CTX, D], F32, "ExternalOutput")
    K = make_ctx(nc)
    K.b_mod_d, K.b_pTM, K.b_pFM, K.b_hT_d = Buf("mod_d"), Buf("pTM"), Buf("pFM"), Buf("hT_d")
    K.b_oT_d, K.b_r_d, K.b_xmid_d, K.b_out = Buf("oT_d"), Buf("r_d"), Buf("xmid_d"), Buf("out")
    K.dbg = None
    load_consts(K, consts)
    K.b_cumT_d = Buf("cumT_d")
    if not need_ctx:
        K.P.dma("sp", out_ctx, x_ctx[0:128, :], writes=[K.b_out])
    phase_A(K, c2, w_ada, b_ada, mod_d)
    phase_B(K, x_lat, x_ctx, mod_d, w_in, pTM, pFM, hT_d)
    phase_mla(K, pTM, qn, kvn, wuq, wukv, ropeA, oT_d, need_ctx)
    phase_gqa(K, pTM, gq, gk, ropeB, oT_d, need_ctx)
    phase_hyena(K, pFM, hy_conv_w, hy_conv_b, hy_w1, hy_b1, hy_w2, hy_b2, hy_w3, hy_freq, hy_skip, hy_sgn, hy_deltas,
                geoms if need_ctx else geoms[:1], oT_d)
    phase_mamba(K, pTM, pFM, mb_conv_w, mb_conv_b, mb_a_log, mb_dt_bias, mb_d, mb_norm, mb_tri, mb_masks, cumT_d, oT_d, need_ctx)
    phase_D(K, x_lat, x_ctx, hT_d, oT_d, mod_d, w_mgate, b_mgate, w_branch, w_out, ln1_g, ln1_b, w_ffn_in, w_ffn_out, ln2_g, ln2_b,
            r_d, xmid_d, out_lat, out_ctx, need_ctx)
    K.P.finish([K.b_out], "sp")
    K.P.emit()
    return nc


def kernel(**inp):
    f32 = np.float32
    x = np.asarray(inp["x"], f32)
    z = np.asarray(inp["ctx"], f32)
    c = np.asarray(inp["c"], f32)
    c_ctx = np.asarray(inp["c_ctx"], f32)
    ident_bf = np.eye(128, dtype=ml_dtypes.bfloat16)
    ident_f = np.eye(128, dtype=f32)
    ropes = [rope_tables(0), rope_tables(1)]
    hyc = {"L": hyena_consts(4096, 2048), "C": hyena_consts(256, 128)}
    tri, masks = mamba_consts()
    deltas = np.abs(np.linspace(math.log(1e-2) / 1.5, math.log(1e-2) / 0.3, 512, dtype=np.float32))[None]
    B = x.shape[0]
    for l in range(2):
        need_ctx = l < 1
        nc = build_layer(need_ctx)
        in_maps = []
        for core in range(8):
            b, half = core // 2, core % 2
            xl = x[b] if half == 0 else x[b][::-1]
            xc = z[b] if half == 0 else z[b][::-1]
            g = lambda k: np.ascontiguousarray(np.asarray(inp[k][l], f32))
            w_in_c = g("w_in")
            hy_cw, mb_cw, alog, dtb = g("hy_conv_w"), g("mb_conv_w"), g("mb_a_log"), g("mb_dt_bias")
            if half == 1:
                w_in_c = np.concatenate([w_in_c[:, :C_DT], w_in_c[:, C_DT + 8:C_DT + 16], w_in_c[:, C_DT:C_DT + 8]], 1)
                hy_cw, mb_cw, alog, dtb = hy_cw[::-1], mb_cw[::-1], alog[::-1], dtb[::-1]
            extra = {"hy_conv_w": np.ascontiguousarray(hy_cw), "hy_conv_b": g("hy_conv_b")[None], "hy_w1": g("hy_w1"), "hy_b1": g("hy_b1")[None],
                     "hy_w2": g("hy_w2"), "hy_b2": g("hy_b2")[None], "hy_w3": g("hy_w3"), "hy_freq": g("hy_freq")[None], "hy_skip": g("hy_skip")[None],
                     "hy_sgn": np.full((1, 1), 1.0 if half == 0 else -1.0, f32), "hy_deltas": deltas,
                     "mb_conv_w": np.ascontiguousarray(mb_cw), "mb_conv_b": g("mb_conv_b")[None], "mb_a_log": np.ascontiguousarray(alog).reshape(1, 16),
                     "mb_dt_bias": np.ascontiguousarray(dtb).reshape(1, 16), "mb_d": g("mb_d")[None], "mb_norm": g("mb_norm")[None],
                     "mb_tri": tri, "mb_masks": masks}
            for nm in (("L", "C") if need_ctx else ("L",)):
                for k_, v_ in hyc[nm].items():
                    extra[f"hy{nm}_{k_}"] = v_
            in_maps.append({
                "x_lat": np.ascontiguousarray(xl), "x_ctx": np.ascontiguousarray(xc), "c2": np.stack([c[b], c_ctx]),
                "w_ada": g("w_ada"), "b_ada": g("b_ada")[None], "w_in": np.ascontiguousarray(w_in_c),
                "ident_bf": ident_bf, "ident_f": ident_f, "ropeA": ropes[half][0], "ropeB": ropes[half][1],
                "mla_q_norm": g("mla_q_norm")[None], "mla_kv_norm": g("mla_kv_norm")[None], "mla_w_uq": g("mla_w_uq"), "mla_w_ukv": g("mla_w_ukv"),
                "gqa_q_norm": g("gqa_q_norm")[None], "gqa_k_norm": g("gqa_k_norm")[None],
                "w_mgate": g("w_mgate"), "b_mgate": g("b_mgate"), "w_branch": g("w_branch"), "w_out": g("w_out"),
                "ln1_g": g("ln1_g")[None], "ln1_b": g("ln1_b")[None], "ln2_g": g("ln2_g")[None], "ln2_b": g("ln2_b")[None],
                "w_ffn_in": g("w_ffn_in"), "w_ffn_out": g("w_ffn_out"),
            })
            in_maps[-1].update(extra)
        res = run_bass_kernel_spmd(nc, in_maps, core_ids=list(range(8)))
        x_new = np.empty_like(x)
        z_new = np.empty_like(z)
        for core in range(8):
            b, half = core // 2, core % 2
            ol = res.results[core]["out_lat"]
            oc = res.results[core]["out_ctx"]
            if half == 0:
                x_new[b, :OWN_LAT] = ol
                z_new[b, :OWN_CTX] = oc
            else:
                x_new[b, OWN_LAT:] = ol[::-1]
                z_new[b, OWN_CTX:] = oc[::-1]
        x, z = x_new, z_new
    return x.astype(np.float32)
```

```python
import numpy as np
import concourse.bass as bass
import concourse.mybir as mybir

F32 = mybir.dt.float32
BF16 = mybir.dt.bfloat16
AF = mybir.ActivationFunctionType
ALU = mybir.AluOpType
AX = mybir.AxisListType

N_DMA_SEMS = 8


class Buf:
    __slots__ = ("name", "w", "r")

    def __init__(self, name):
        self.name = name
        self.w = {}
        self.r = {}


class Prog:
    _n_prog = 0

    def __init__(self, nc):
        self.nc = nc
        self.pid = Prog._n_prog
        Prog._n_prog += 1
        self.eng_handles = {"pe": nc.tensor, "dve": nc.vector, "act": nc.scalar,
                            "pool": nc.gpsimd, "sp": nc.sync}
        self.ops = {k: [] for k in self.eng_handles}
        self.seq = {k: 0 for k in self.eng_handles}
        self.known = {k: {} for k in self.eng_handles}
        self.sem_names = []
        for k in self.eng_handles:
            self.sem_names.append(("c", k))
        self.dma_rr = {}
        self.dma_cnt = {}
        self.dma_last = {}
        for q in ("sp", "pool", "act"):
            self.dma_rr[q] = 0
            for j in range(N_DMA_SEMS):
                key = ("d", q, j)
                self.sem_names.append(key)
                self.dma_cnt[key] = 0
                self.dma_last[key] = None
        self.sems = {}
        self.n_wait = 0
        self.n_ops = 0

    def _need(self, eng, reads, writes):
        deps = {}

        def add(d, war=False):
            for key, (val, snap) in d.items():
                if key == ("c", eng) and (war or eng == "pe"):
                    continue
                if key not in deps or deps[key][0] < val:
                    deps[key] = (val, snap)
        for b in reads:
            add(b.w)
        for b in writes:
            add(b.w)
            add(b.r, war=True)
        kn = self.known[eng]
        waits = []
        for key, (val, snap) in deps.items():
            if kn.get(key, 0) >= val:
                continue
            waits.append((key, val))
            kn[key] = val
            for k2, v2 in snap.items():
                if kn.get(k2, 0) < v2:
                    kn[k2] = v2
        return waits

    def _record(self, eng, key, val, reads, writes, partial):
        ev = (val, dict(self.known[eng]))
        for b in reads:
            old = b.r.get(key)
            if old is None or old[0] < val:
                b.r[key] = ev
        for b in writes:
            if partial:
                b.w[key] = ev
            else:
                b.w = {key: ev}
                b.r = {}

    def op(self, eng, fn, reads=(), writes=(), partial=False):
        waits = self._need(eng, reads, writes)
        self.seq[eng] += 1
        val = self.seq[eng]
        key = ("c", eng)
        self.ops[eng].append((waits, fn, key, 1))
        self._record(eng, key, val, reads, writes, partial)
        self.n_ops += 1
        self.n_wait += len(waits)

    def I(self, eng, name, reads=(), writes=(), partial=False, **kw):
        self.op(eng, (lambda e, name=name, kw=kw: getattr(e, name)(**kw)), reads=reads, writes=writes, partial=partial)

    def dma(self, q, out, in_, reads=(), writes=(), partial=True, **kw):
        j = self.dma_rr[q]
        self.dma_rr[q] = (j + 1) % N_DMA_SEMS
        key = ("d", q, j)
        waits = self._need(q, reads, writes)
        prev = self.dma_cnt[key]
        if prev > 0 and self.known[q].get(key, 0) < prev:
            waits.append((key, prev))
            self.known[q][key] = prev
        val = prev + 16
        self.dma_cnt[key] = val

        def fn(e, out=out, in_=in_, kw=kw):
            return e.dma_start(out=out, in_=in_, **kw)
        self.ops[q].append((waits, fn, key, 16))
        self._record(q, key, val, reads, writes, partial)
        self.n_ops += 1
        self.n_wait += len(waits)

    def gather(self, out, in_dram, idx_ap, reads=(), writes=()):
        q = "pool"
        j = self.dma_rr[q]
        self.dma_rr[q] = (j + 1) % N_DMA_SEMS
        key = ("d", q, j)
        waits = self._need(q, reads, writes)
        prev = self.dma_cnt[key]
        if prev > 0 and self.known[q].get(key, 0) < prev:
            waits.append((key, prev))
            self.known[q][key] = prev
        val = prev + 16
        self.dma_cnt[key] = val

        def fn(e):
            return e.indirect_dma_start(out=out, out_offset=None, in_=in_dram, in_offset=bass.IndirectOffsetOnAxis(ap=idx_ap, axis=0))
        self.ops[q].append((waits, fn, key, 16))
        self._record(q, key, val, reads, writes, False)
        self.n_ops += 1

    def collective(self, kind, alu, groups, in_ap, out_ap, reads=(), writes=(), inc=16):
        q = "pool"
        key = ("d", "cc", 0)
        if key not in self.dma_cnt:
            self.dma_cnt[key] = 0
            self.sem_names.append(key)
        waits = self._need(q, reads, writes)
        prev = self.dma_cnt[key]
        if prev > 0 and self.known[q].get(key, 0) < prev:
            waits.append((key, prev))
            self.known[q][key] = prev
        val = prev + inc
        self.dma_cnt[key] = val

        def fn(e):
            return e.collective_compute(kind, alu, replica_groups=groups, ins=[in_ap], outs=[out_ap])
        self.ops[q].append((waits, fn, key, inc))
        self._record(q, key, val, reads, writes, False)
        self.n_ops += 1

    def finish(self, bufs, eng="sp"):
        waits = self._need(eng, bufs, ())
        self.ops[eng].append((waits, None, None, 0))

    def emit(self):
        nc = self.nc
        from contextlib import ExitStack
        with ExitStack() as st:
            used = set()
            for e, lst in self.ops.items():
                for waits, fn, key, inc in lst:
                    if key is not None:
                        used.add(key)
                    for k, _ in waits:
                        used.add(k)
            for key in self.sem_names:
                if key in used:
                    self.sems[key] = nc.alloc_semaphore(name=f"s{self.pid}_" + "_".join(map(str, key)))
            block = st.enter_context(nc.Block())
            for ename in ("sp", "act", "pe", "dve", "pool"):
                lst = self.ops[ename]
                if not lst:
                    continue

                def body(e, lst=lst):
                    for waits, fn, key, inc in lst:
                        for k, v in waits:
                            e.wait_ge(self.sems[k], v)
                        if fn is not None:
                            fn(e).then_inc(self.sems[key], inc)
                reg = {"sp": block.sync, "act": block.scalar, "pe": block.tensor,
                       "dve": block.vector, "pool": block.gpsimd}[ename]
                reg(body)


class Ring:
    def __init__(self, nc, name, shape, dtype, n, psum=False):
        self.items = []
        for i in range(n):
            if psum:
                t = nc.alloc_psum_tensor(f"{name}{i}", list(shape), dtype)
            else:
                t = nc.alloc_sbuf_tensor(f"{name}{i}", list(shape), dtype)
            self.items.append((t.ap(), Buf(f"{name}{i}")))
        self.i = 0

    def next(self):
        it = self.items[self.i]
        self.i = (self.i + 1) % len(self.items)
        return it


import math
import numpy as np
import ml_dtypes
import concourse.bass as bass
import concourse.mybir as mybir

D = 2048
NLAT = 4096
NCTX = 256
NTOK = NLAT + NCTX
OWN_LAT = 2048
OWN_CTX = 128
NOWN = OWN_LAT + OWN_CTX
FFH = 5632
EPS = 1e-6
ALPHA = (2 * 2) ** 0.25
C_A, C_B, C_C, C_DZ, C_DX, C_DT, C_END = 0, 640, 1664, 3200, 3712, 4736, 4752
TMW = 2192
FMW = 2560


class Arena:
    def __init__(self, nc, limit=206 * 1024):
        self.nc = nc
        self.off = 0
        self.limit = limit
        self.n = 0
        self.base = nc.alloc_sbuf_tensor("arena", [128, limit // 4], F32).ap()

    def sb(self, shape, dtype, name=None):
        esz = mybir.dt.size(dtype)
        nel = int(np.prod(shape[1:]))
        nbytes = (nel * esz + 63) // 64 * 64
        assert self.off + nbytes <= self.limit, f"SBUF arena overflow {self.off}+{nbytes}"
        a = self.base[0:shape[0], self.off // 4:(self.off + nbytes) // 4]
        if dtype != F32:
            a = a.bitcast(dtype)
        a = a[:, 0:nel]
        if len(shape) == 3:
            a = a.rearrange("p (a b) -> p a b", a=shape[1])
        elif len(shape) == 4:
            a = a.rearrange("p (a b c) -> p a b c", a=shape[1], b=shape[2])
        self.off += nbytes
        return a

    def mark(self):
        return self.off

    def reset(self, m=0):
        self.off = m


class Ctx:
    pass


def sbuf_ring(K, name, shape, dtype, n):
    return [(K.A.sb(shape, dtype, f"{name}{i}"), Buf(f"{name}{i}")) for i in range(n)]


class RR:
    def __init__(self, items):
        self.items = items
        self.i = 0

    def next(self):
        it = self.items[self.i]
        self.i = (self.i + 1) % len(self.items)
        return it


def barrier(K):
    P = K.P
    evs = {}
    for e in P.eng_handles:
        if P.seq[e] > 0:
            evs[("c", e)] = (P.seq[e], {})
    for key, cnt in P.dma_cnt.items():
        if cnt > 0:
            evs[key] = (cnt, {})
    b = Buf("barrier")
    b.w = evs
    for e in ("sp", "act", "pe", "dve", "pool"):
        waits = P._need(e, [b], ())
        if waits:
            P.ops[e].append((waits, None, None, 0))


def evac_copy(K, i, out, in_, reads, writes, partial=False):
    if i % 2 == 0:
        K.P.op("dve", lambda e: e.tensor_copy(out=out, in_=in_), reads=reads, writes=writes, partial=partial)
    else:
        K.P.op("act", lambda e: e.copy(out=out, in_=in_), reads=reads, writes=writes, partial=partial)


def copy_on(K, eng, out, in_, reads, writes, partial=True):
    if eng == "act":
        K.P.op("act", lambda e: e.copy(out=out, in_=in_), reads=reads, writes=writes, partial=partial)
    else:
        K.P.op("dve", lambda e: e.tensor_copy(out=out, in_=in_), reads=reads, writes=writes, partial=partial)


def ln_stats(K, x_ap, bx, width, tmp):
    P = K.P
    st, b_st = tmp["st"].next()
    mv, b_mv = tmp["mv"].next()
    nch = width // 512
    for c in range(nch):
        P.op("dve", lambda e, c=c: e.bn_stats(out=st[:, c, :], in_=x_ap[:, c * 512:(c + 1) * 512]),
             reads=[bx], writes=[b_st], partial=(c > 0))
    P.op("dve", lambda e: e.bn_aggr(out=mv[:, 0:2], in_=st[:, 0:nch, :]), reads=[b_st], writes=[b_mv])
    P.op("dve", lambda e: e.tensor_scalar_add(out=mv[:, 3:4], in0=mv[:, 1:2], scalar1=EPS), reads=[b_mv], writes=[b_mv])
    P.op("act", lambda e: e.sqrt(out=mv[:, 3:4], in_=mv[:, 3:4]), reads=[b_mv], writes=[b_mv])
    P.op("dve", lambda e: e.reciprocal(out=mv[:, 2:3], in_=mv[:, 3:4]), reads=[b_mv], writes=[b_mv])
    return mv, b_mv


def load_modT(K, mod_d, r, name):
    P, A = K.P, K.A
    raw = A.sb([96, 128], F32, name + "raw")
    b_raw = Buf(name + "raw")
    P.dma("sp", raw, mod_d[r:r + 1, :].rearrange("o (j p) -> (o j) p", p=128), writes=[b_raw], partial=False)
    ps, b_ps = K.ps_misc.next()
    P.op("pe", lambda e: e.transpose(out=ps[:, 0:96], in_=raw, identity=K.ident_f[0:96, 0:96]),
         reads=[b_raw, K.b_const], writes=[b_ps])
    mt = A.sb([128, 96], F32, name)
    b_mt = Buf(name)
    P.op("dve", lambda e: e.tensor_copy(out=mt, in_=ps[:, 0:96]), reads=[b_ps], writes=[b_mt])
    P.op("dve", lambda e: e.tensor_scalar_add(out=mt[:, 16:32], in0=mt[:, 16:32], scalar1=1.0), reads=[b_mt], writes=[b_mt])
    P.op("dve", lambda e: e.tensor_scalar_add(out=mt[:, 64:80], in0=mt[:, 64:80], scalar1=1.0), reads=[b_mt], writes=[b_mt])
    return mt, b_mt


def build_hT_tile(K, loader, modT, b_modT, soff, hT_dst, b_hT, tmp):
    xt, b_xt = tmp["xt"].next()
    loader(xt, b_xt)
    hT_from_sbuf(K, xt, b_xt, modT, b_modT, soff, hT_dst, b_hT, tmp)


def hT_from_sbuf(K, xt, b_xt, modT, b_modT, soff, hT_dst, b_hT, tmp):
    P = K.P
    mv, b_mv = ln_stats(K, xt, b_xt, D, tmp)
    xs, b_xs = tmp["xs"].next()
    P.op("dve", lambda e: e.tensor_scalar(out=xs, in0=xt, scalar1=mv[:, 0:1], scalar2=mv[:, 2:3],
                                          op0=ALU.subtract, op1=ALU.mult), reads=[b_xt, b_mv], writes=[b_xs])
    for half in range(2):
        ps, b_ps = K.ps_tp.next()
        psb = ps.bitcast(BF16).rearrange("p (k t) -> p k t", k=8)
        for k in range(8):
            kk = half * 8 + k
            P.op("pe", lambda e, k=k, kk=kk, psb=psb: e.transpose(out=psb[:, k, :], in_=xs[:, kk * 128:(kk + 1) * 128], identity=K.ident),
                 reads=[b_xs, K.b_const], writes=[b_ps], partial=(k > 0))
        t1, b_t1 = tmp["t1"].next()
        sc_b = modT[:, soff + 16 + half * 8: soff + 16 + half * 8 + 8].unsqueeze(2).broadcast_to([128, 8, 128])
        sh_b = modT[:, soff + half * 8: soff + half * 8 + 8].unsqueeze(2).broadcast_to([128, 8, 128])
        P.op("dve", lambda e, psb=psb, t1=t1, sc_b=sc_b: e.tensor_tensor(out=t1, in0=psb, in1=sc_b, op=ALU.mult),
             reads=[b_ps, b_modT], writes=[b_t1])
        P.op("dve", lambda e, t1=t1, sh_b=sh_b, half=half: e.tensor_tensor(out=hT_dst[:, half * 8:(half + 1) * 8, :], in0=t1, in1=sh_b, op=ALU.add),
             reads=[b_t1, b_modT], writes=[b_hT], partial=True)


def load_w_chunk(K, ring, w_ap, k_rows, c0, ncols, nsplit=1):
    wb, b_wb = ring.next()
    kc = k_rows // 128
    step = (kc + nsplit - 1) // nsplit
    first = True
    for k0 in range(0, kc, step):
        k1 = min(kc, k0 + step)
        K.P.dma("pool", wb[:, k0:k1, 0:ncols], w_ap[k0 * 128:k1 * 128, c0:c0 + ncols].rearrange("(k p) n -> p k n", p=128),
                writes=[b_wb], partial=(not first))
        first = False
    return wb, b_wb


def phase_A(K, c2_d, w_ada, b_ada, mod_d):
    P, A = K.P, K.A
    m0 = A.mark()
    craw = A.sb([32, 128], F32, "craw")
    b_craw = Buf("craw")
    P.dma("sp", craw, c2_d.rearrange("r (k p) -> (r k) p", p=128), writes=[b_craw], partial=False)
    csil = A.sb([32, 128], F32, "csil")
    b_csil = Buf("csil")
    P.op("act", lambda e: e.activation(out=csil, in_=craw, func=AF.Silu), reads=[b_craw], writes=[b_csil])
    ps, b_ps = K.ps_misc.next()
    P.op("pe", lambda e: e.transpose(out=ps[:, 0:32], in_=csil, identity=K.ident_f[0:32, 0:32]),
         reads=[b_csil, K.b_const], writes=[b_ps])
    cT = A.sb([128, 32], F32, "cT")
    b_cT = Buf("cT")
    P.op("dve", lambda e: e.tensor_copy(out=cT, in_=ps[:, 0:32]), reads=[b_ps], writes=[b_cT])
    bias = A.sb([2, 12288], F32, "bada")
    b_bias = Buf("bada")
    P.dma("sp", bias, b_ada.broadcast_to([2, 12288]) if False else b_ada.partition_broadcast(2), writes=[b_bias], partial=False)
    modsb = A.sb([2, 12288], F32, "modsb")
    b_modsb = Buf("modsb")
    wr = RR(sbuf_ring(K, "wada", [128, 16, 512], F32, 2))
    for n in range(24):
        wch, b_wch = wr.next()
        P.dma("sp" if n % 2 == 0 else "act", wch, w_ada[:, n * 512:(n + 1) * 512].rearrange("(k p) n -> p k n", p=128),
              writes=[b_wch], partial=False)
        po, b_po = K.ps_acc.next()
        for k in range(16):
            P.op("pe", lambda e, k=k, po=po, wch=wch: e.matmul(po[0:2, :], lhsT=cT[:, k::16], rhs=wch[:, k, :], start=(k == 0), stop=(k == 15)),
                 reads=[b_cT, b_wch], writes=[b_po], partial=(k > 0))
        P.op("dve", lambda e, n=n, po=po: e.tensor_tensor(out=modsb[:, n * 512:(n + 1) * 512], in0=po[0:2, :], in1=bias[:, n * 512:(n + 1) * 512], op=ALU.add),
             reads=[b_po, b_bias], writes=[b_modsb], partial=True)
    P.dma("sp", mod_d, modsb, reads=[b_modsb], writes=[K.b_mod_d], partial=False)
    barrier(K)
    A.reset(m0)


def phase_B(K, xload, mod_d, w_in, pTM, pFM, hT_d):
    P, A = K.P, K.A
    m0 = A.mark()
    modT_l, b_ml = load_modT(K, mod_d, 0, "modTl")
    modT_c, b_mc = load_modT(K, mod_d, 1, "modTc")
    tmp = {
        "st": RR(sbuf_ring(K, "bst", [128, 4, 6], F32, 2)),
        "mv": RR(sbuf_ring(K, "bmv", [128, 4], F32, 2)),
        "xt": RR(sbuf_ring(K, "xt", [128, D], F32, 2)),
        "xs": RR(sbuf_ring(K, "xs", [128, D], BF16, 2)),
        "t1": RR(sbuf_ring(K, "t1", [128, 8, 128], F32, 2)),
    }
    hT = A.sb([128, 16, 2048], BF16, "hT")
    wring = RR(sbuf_ring(K, "wch", [128, 16, 512], BF16, 2))
    ostg = RR(sbuf_ring(K, "ostg", [128, 512], F32, 4))
    groups = [("lat", 0, 16), ("lat", 16, 16), ("ctx", 0, 2)]
    tm_chunks = [(C_A, 512, 0), (C_A + 512, 512, 512), (C_A + 1024, 512, 1024), (C_A + 1536, 128, 1536),
                 (C_DZ, 512, 1664), (C_DT, 16, 2176)]
    fm_chunks = [(C_C, 512, 0), (C_C + 512, 512, 512), (C_C + 1024, 512, 1024), (C_DX, 512, 1536), (C_DX + 512, 512, 2048)]
    ev = 0
    for kind, t0, nt in groups:
        b_hT = [Buf(f"hT{t}") for t in range(nt)]
        for t in range(nt):
            mt, bm = (modT_l, b_ml) if kind == "lat" else (modT_c, b_mc)
            build_hT_tile(K, (lambda xt, b_xt, kind=kind, tt=t0 + t: xload(kind, tt, xt, b_xt)), mt, bm, 0,
                          hT[:, :, t * 128:(t + 1) * 128], b_hT[t], tmp)
        tok0 = (t0 * 128) if kind == "lat" else (NLAT + t0 * 128)
        ntok = nt * 128
        if kind == "lat" and t0 == 0:
            P.dma("sp", hT_d[:, :, 0:OWN_LAT], hT[:, :, 0:OWN_LAT], reads=b_hT, writes=[K.b_hT_d])
        if kind == "ctx":
            P.dma("sp", hT_d[:, :, OWN_LAT:OWN_LAT + OWN_CTX], hT[:, :, 0:OWN_CTX], reads=b_hT, writes=[K.b_hT_d])
        for (c0, ncols, dcol) in tm_chunks:
            wb, b_wb = load_w_chunk(K, wring, w_in, D, c0, ncols)
            for t in range(nt):
                po, b_po = K.ps_acc.next()
                for k in range(16):
                    P.op("pe", lambda e, k=k, po=po, wb=wb, t=t, ncols=ncols: e.matmul(po[:, 0:ncols], lhsT=hT[:, k, t * 128:(t + 1) * 128], rhs=wb[:, k, 0:ncols],
                                                                                 start=(k == 0), stop=(k == 15)),
                         reads=[b_hT[t], b_wb], writes=[b_po], partial=(k > 0))
                so, b_so = ostg.next()
                evac_copy(K, ev, so[:, 0:ncols], po[:, 0:ncols], [b_po], [b_so]); ev += 1
                P.dma("sp", pTM[tok0 + t * 128: tok0 + (t + 1) * 128, dcol:dcol + ncols], so[:, 0:ncols], reads=[b_so], writes=[K.b_pTM])
        for (c0, ncols, drow) in fm_chunks:
            wb, b_wb = load_w_chunk(K, wring, w_in, D, c0, ncols)
            for j in range(ncols // 128):
                for s0 in range(0, ntok, 512):
                    sl = min(512, ntok - s0)
                    po, b_po = K.ps_acc.next()
                    tiles = list(range(s0 // 128, (s0 + sl) // 128))
                    for k in range(16):
                        P.op("pe", lambda e, k=k, po=po, wb=wb, j=j, s0=s0, sl=sl: e.matmul(po[:, 0:sl], lhsT=wb[:, k, j * 128:(j + 1) * 128], rhs=hT[:, k, s0:s0 + sl],
                                                                                   start=(k == 0), stop=(k == 15)),
                             reads=[b_hT[t] for t in tiles] + [b_wb], writes=[b_po], partial=(k > 0))
                    so, b_so = ostg.next()
                    evac_copy(K, ev, so[:, 0:sl], po[:, 0:sl], [b_po], [b_so]); ev += 1
                    P.dma("sp", pFM[drow + j * 128: drow + (j + 1) * 128, tok0 + s0: tok0 + s0 + sl], so[:, 0:sl], reads=[b_so], writes=[K.b_pFM])
    barrier(K)
    A.reset(m0)


def make_ctx(nc, prev=None):
    K = Ctx()
    K.nc = nc
    K.P = Prog(nc)
    if prev is not None:
        K.A = prev.A
        K.ident, K.ident_f, K.b_const = prev.ident, prev.ident_f, Buf("const")
        banks = [(ap, Buf(f"psb{i}")) for i, (ap, _) in enumerate(prev.banks)]
    else:
        K.A = Arena(nc)
        banks = []
        for i in range(8):
            t = nc.alloc_psum_tensor(f"psb{i}", [128, 512], F32)
            banks.append((t.ap(), Buf(f"psb{i}")))
    K.banks = banks
    K.ps_acc = RR(banks[0:4])
    K.ps_tp = RR(banks[4:6])
    K.ps_misc = RR(banks[6:8])
    return K


def load_consts(K, consts):
    P, A = K.P, K.A
    K.b_const = Buf("const")
    K.ident = A.sb([128, 128], BF16, "ident")
    K.ident_f = A.sb([128, 128], F32, "identf")
    P.dma("sp", K.ident, consts["ident_bf"], writes=[K.b_const])
    P.dma("sp", K.ident_f, consts["ident_f"], writes=[K.b_const])


def rms_rstd(K, ss, b_ss, n, width):
    P = K.P
    P.op("dve", lambda e: e.tensor_scalar(out=ss[:, 0:n], in0=ss[:, 0:n], scalar1=1.0 / width, scalar2=EPS, op0=ALU.mult, op1=ALU.add),
         reads=[b_ss], writes=[b_ss])
    P.op("act", lambda e: e.sqrt(out=ss[:, 0:n], in_=ss[:, 0:n]), reads=[b_ss], writes=[b_ss])
    P.op("dve", lambda e: e.reciprocal(out=ss[:, 0:n], in_=ss[:, 0:n]), reads=[b_ss], writes=[b_ss])


def rope_apply(K, src, dst, cs, b_src, b_dst, b_cs, nh, npair, tmpr):
    P = K.P
    s4 = src.rearrange("p h (i two) -> p h i two", two=2)
    d4 = dst.rearrange("p h (i two) -> p h i two", two=2)
    x0, x1 = s4[:, :, :, 0], s4[:, :, :, 1]
    cb = cs[:, 0:1, :].broadcast_to([128, nh, npair])
    sb_ = cs[:, 1:2, :].broadcast_to([128, nh, npair])
    (ta, b_ta) = tmpr.next()
    (tb, b_tb) = tmpr.next()
    ta = ta[:, 0:nh, 0:npair]; tb = tb[:, 0:nh, 0:npair]
    P.op("dve", lambda e: e.tensor_tensor(out=ta, in0=x0, in1=cb, op=ALU.mult), reads=[b_src, b_cs], writes=[b_ta])
    P.op("pool", lambda e: e.tensor_tensor(out=tb, in0=x1, in1=sb_, op=ALU.mult), reads=[b_src, b_cs], writes=[b_tb])
    P.op("dve", lambda e: e.tensor_tensor(out=d4[:, :, :, 0], in0=ta, in1=tb, op=ALU.subtract), reads=[b_ta, b_tb], writes=[b_dst], partial=True)
    (tc, b_tc) = tmpr.next()
    (td, b_td) = tmpr.next()
    tc = tc[:, 0:nh, 0:npair]; td = td[:, 0:nh, 0:npair]
    P.op("dve", lambda e: e.tensor_tensor(out=tc, in0=x0, in1=sb_, op=ALU.mult), reads=[b_src, b_cs], writes=[b_tc])
    P.op("pool", lambda e: e.tensor_tensor(out=td, in0=x1, in1=cb, op=ALU.mult), reads=[b_src, b_cs], writes=[b_td])
    P.op("dve", lambda e: e.tensor_tensor(out=d4[:, :, :, 1], in0=tc, in1=td, op=ALU.add), reads=[b_tc, b_td], writes=[b_dst], partial=True)


def attn_core(K, heads, V_all, b_V, dv, scale, o_tm, b_otm, need_ctx):
    P, A = K.P, K.A
    SKEW = 2
    pt_ring = RR(sbuf_ring(K, "pt", [128, 512], BF16, SKEW + 2))
    rc_ring = RR(sbuf_ring(K, "rc", [128, 1], F32, 4))
    obanks = K.banks[4:8]
    qgroups = [(q0, 512, list(range(34))) for q0 in range(0, OWN_LAT, 512)]
    if need_ctx:
        qgroups.append((OWN_LAT, 128, [32, 33]))
    units = []
    for h, (kvi, parts) in enumerate(heads):
        for (q0, nq, kbs) in qgroups:
            for bi, kb in enumerate(kbs):
                units.append({"h": h, "kvi": kvi, "parts": parts, "q0": q0, "nq": nq, "kb": kb,
                              "first": bi == 0, "last": bi == len(kbs) - 1})

    def stage1(u):
        nq, q0, kb = u["nq"], u["q0"], u["kb"]
        ps, b_ps = K.ps_acc.next()
        np_ = len(u["parts"])
        for pi, (kT, qT, bk, bq) in enumerate(u["parts"]):
            P.I("pe", "matmul", reads=[bk, bq], writes=[b_ps], partial=(pi > 0), out=ps[:, 0:nq], lhsT=kT[:, kb * 128:(kb + 1) * 128],
                rhs=qT[:, q0:q0 + nq], start=(pi == 0), stop=(pi == np_ - 1))
        pt, b_pt = pt_ring.next()
        P.I("act", "activation", reads=[b_ps], writes=[b_pt], out=pt[:, 0:nq], in_=ps[:, 0:nq], func=AF.Exp, scale=scale)
        u["pt"], u["b_pt"] = pt, b_pt

    def stage2(u):
        nq, q0, kb, h = u["nq"], u["q0"], u["kb"], u["h"]
        pt, b_pt = u["pt"], u["b_pt"]
        for qs in range(nq // 128):
            ob, b_ob = obanks[qs]
            P.I("pe", "matmul", reads=[b_pt, b_V], writes=[b_ob], partial=(not u["first"]), out=ob[:, 0:dv + 1], lhsT=pt[:, qs * 128:(qs + 1) * 128],
                rhs=V_all[:, kb, u["kvi"], 0:dv + 1], start=u["first"], stop=u["last"])
        if u["last"]:
            for qs in range(nq // 128):
                ob, b_ob = obanks[qs]
                rc, b_rc = rc_ring.next()
                ot = q0 // 128 + qs
                P.I("dve", "reciprocal", reads=[b_ob], writes=[b_rc], out=rc, in_=ob[:, dv:dv + 1])
                P.I("dve", "tensor_scalar_mul", reads=[b_ob, b_rc], writes=[b_otm], partial=True, out=o_tm[:, ot, h * dv:(h + 1) * dv], in0=ob[:, 0:dv], scalar1=rc[:, 0:1])
    n = len(units)
    for i in range(n + SKEW):
        if i < n:
            stage1(units[i])
        if i >= SKEW:
            stage2(units[i - SKEW])


def store_oT(K, o_tm, b_otm, oT_d, branch, n_own_tiles):
    P = K.P
    stg = RR(sbuf_ring(K, "ostg", [128, 4, 128], BF16, 3))
    for ot in range(n_own_tiles):
        ps, b_ps = K.ps_tp.next()
        psb = ps.bitcast(BF16).rearrange("p (k t) -> p k t", k=8)
        for h in range(4):
            P.op("pe", lambda e, psb=psb, h=h, ot=ot: e.transpose(out=psb[:, h, :], in_=o_tm[:, ot, h * 128:(h + 1) * 128], identity=K.ident),
                 reads=[b_otm, K.b_const], writes=[b_ps], partial=(h > 0))
        so, b_so = stg.next()
        evac_copy(K, ot, so, psb[:, 0:4, :], [b_ps], [b_so])
        P.dma("sp", oT_d[branch * 512:(branch + 1) * 512, ot * 128:(ot + 1) * 128].rearrange("(h p) t -> p h t", p=128), so,
              reads=[b_so], writes=[K.b_oT_d])


def own_tiles(need_ctx):
    lst = [(t, t) for t in range(16)]
    if need_ctx:
        lst.append((32, 16))
    return lst


def phase_mla(K, pTM, q_norm, kv_norm, w_uq, w_ukv, ropeA, oT_d, need_ctx):
    P, A = K.P, K.A
    m0 = A.mark()
    NOWN = OWN_LAT + OWN_CTX
    knT = A.sb([128, 4, NTOK], BF16, "knT"); b_knT = Buf("knT")
    krT = A.sb([64, NTOK], BF16, "krT"); b_krT = Buf("krT")
    V_all = A.sb([128, 34, 4, 136], BF16, "Vall"); b_V = Buf("Vall")
    qnT = A.sb([128, 4, NOWN], BF16, "qnT"); b_qnT = Buf("qnT")
    qrT = A.sb([64, 4, NOWN], BF16, "qrT"); b_qrT = Buf("qrT")
    o_tm = A.sb([128, 17, 512], BF16, "otm"); b_otm = Buf("otm")
    m1 = A.mark()
    ckvT = A.sb([128, NTOK], BF16, "ckvT"); b_ckvT = Buf("ckvT")
    cqT = A.sb([128, 4, NOWN], BF16, "cqT"); b_cqT = Buf("cqT")
    wuq = A.sb([128, 4, 768], BF16, "wuq"); b_wuq = Buf("wuq")
    wukv = A.sb([128, 1024], BF16, "wukv"); b_wukv = Buf("wukv")
    gq = A.sb([128, 448], F32, "gq"); gkv = A.sb([128, 128], F32, "gkv"); b_g = Buf("gains")
    P.dma("pool", wuq[:, 0:3, :], w_uq[0:384, :].rearrange("(k p) n -> p k n", p=128), writes=[b_wuq])
    P.dma("pool", wuq[0:64, 3, :], w_uq[384:448, :], writes=[b_wuq])
    P.dma("pool", wukv, w_ukv, writes=[b_wukv])
    P.dma("sp", gq, q_norm.partition_broadcast(128), writes=[b_g])
    P.dma("sp", gkv, kv_norm.partition_broadcast(128), writes=[b_g])
    P.op("pool", lambda e: e.memset(V_all[:, :, :, 128:129], 1.0), writes=[b_V], partial=True)
    pa_r = RR(sbuf_ring(K, "pa", [128, 640], F32, 2))
    cs_r = RR(sbuf_ring(K, "csA", [128, 2, 32], F32, 2))
    sq_r = RR(sbuf_ring(K, "sq", [128, 448], F32, 2))
    ss_r = RR(sbuf_ring(K, "ss", [128, 2], F32, 4))
    nb_r = RR(sbuf_ring(K, "nb", [128, 640], BF16, 2))
    tmpr = RR(sbuf_ring(K, "rt", [128, 4, 32], F32, 8))
    owned = dict(own_tiles(need_ctx))
    for t in range(34):
        is_lat = t < 32
        pa, b_pa = pa_r.next()
        P.dma("sp", pa, pTM[t * 128:(t + 1) * 128, 0:640], writes=[b_pa], partial=False)
        nb, b_nb = nb_r.next()
        if is_lat:
            cs, b_cs = cs_r.next()
            P.dma("sp", cs, ropeA[t * 128:(t + 1) * 128], writes=[b_cs], partial=False)
        sq, b_sq = sq_r.next()
        ss, b_ss = ss_r.next()
        P.op("pool", lambda e, sq=sq, pa=pa: e.tensor_tensor(out=sq[:, 0:128], in0=pa[:, 448:576], in1=pa[:, 448:576], op=ALU.mult), reads=[b_pa], writes=[b_sq])
        P.op("dve", lambda e, sq=sq, ss=ss: e.reduce_sum(out=ss[:, 0:1], in_=sq[:, 0:128], axis=AX.X), reads=[b_sq], writes=[b_ss])
        rms_rstd(K, ss, b_ss, 1, 128)
        P.op("dve", lambda e, nb=nb, pa=pa, ss=ss: e.scalar_tensor_tensor(out=nb[:, 448:576], in0=pa[:, 448:576], scalar=ss[:, 0:1], in1=gkv, op0=ALU.mult, op1=ALU.mult),
             reads=[b_pa, b_ss, b_g], writes=[b_nb], partial=True)
        if is_lat:
            rope_apply(K, pa[:, 576:640].rearrange("p (h d) -> p h d", h=1), nb[:, 576:640].rearrange("p (h d) -> p h d", h=1), cs, b_pa, b_nb, b_cs, 1, 32, tmpr)
        else:
            P.op("dve", lambda e, nb=nb, pa=pa: e.tensor_copy(out=nb[:, 576:640], in_=pa[:, 576:640]), reads=[b_pa], writes=[b_nb], partial=True)
        ps, b_ps = K.ps_tp.next()
        psb = ps.bitcast(BF16).rearrange("p (k t) -> p k t", k=8)
        P.op("pe", lambda e, psb=psb, nb=nb: e.transpose(out=psb[:, 0, :], in_=nb[:, 448:576], identity=K.ident), reads=[b_nb, K.b_const], writes=[b_ps])
        P.op("pe", lambda e, psb=psb, nb=nb: e.transpose(out=psb[0:64, 1, :], in_=nb[:, 576:640], identity=K.ident), reads=[b_nb, K.b_const], writes=[b_ps], partial=True)
        own = t in owned
        if own:
            ot = owned[t]
            sq2, b_sq2 = sq_r.next()
            ss2, b_ss2 = ss_r.next()
            P.op("pool", lambda e, sq2=sq2, pa=pa: e.tensor_tensor(out=sq2, in0=pa[:, 0:448], in1=pa[:, 0:448], op=ALU.mult), reads=[b_pa], writes=[b_sq2])
            P.op("dve", lambda e, sq2=sq2, ss2=ss2: e.reduce_sum(out=ss2[:, 0:1], in_=sq2, axis=AX.X), reads=[b_sq2], writes=[b_ss2])
            rms_rstd(K, ss2, b_ss2, 1, 448)
            P.op("dve", lambda e, nb=nb, pa=pa, ss2=ss2: e.scalar_tensor_tensor(out=nb[:, 0:448], in0=pa[:, 0:448], scalar=ss2[:, 0:1], in1=gq, op0=ALU.mult, op1=ALU.mult),
                 reads=[b_pa, b_ss2, b_g], writes=[b_nb], partial=True)
            for kc in range(4):
                kp = 128 if kc < 3 else 64
                P.op("pe", lambda e, psb=psb, nb=nb, kc=kc, kp=kp: e.transpose(out=psb[0:kp, 2 + kc, :], in_=nb[:, kc * 128:kc * 128 + kp], identity=K.ident),
                     reads=[b_nb, K.b_const], writes=[b_ps], partial=True)
        ce = "act" if t % 2 == 0 else "dve"
        copy_on(K, ce, ckvT[:, t * 128:(t + 1) * 128], psb[:, 0, :], [b_ps], [b_ckvT])
        copy_on(K, ce, krT[:, t * 128:(t + 1) * 128], psb[0:64, 1, :], [b_ps], [b_krT])
        if own:
            copy_on(K, ce, cqT[:, 0:3, ot * 128:(ot + 1) * 128], psb[:, 2:5, :], [b_ps], [b_cqT])
            copy_on(K, ce, cqT[0:64, 3, ot * 128:(ot + 1) * 128], psb[0:64, 5, :], [b_ps], [b_cqT])
    ev = 0
    for h in range(4):
        for s0 in range(0, NTOK, 512):
            sl = min(512, NTOK - s0)
            po, b_po = K.ps_acc.next()
            P.op("pe", lambda e, po=po, h=h, s0=s0, sl=sl: e.matmul(po[:, 0:sl], lhsT=wukv[:, h * 256:h * 256 + 128], rhs=ckvT[:, s0:s0 + sl], start=True, stop=True),
                 reads=[b_wukv, b_ckvT], writes=[b_po])
            evac_copy(K, ev, knT[:, h, s0:s0 + sl], po[:, 0:sl], [b_po], [b_knT], partial=True); ev += 1
    wv = wukv.rearrange("p (h c) -> p h c", h=4)[:, :, 128:256]
    for kt in range(34):
        po, b_po = K.ps_acc.next()
        pov = po.rearrange("p (h c) -> p h c", h=4)
        P.op("pe", lambda e, pov=pov, kt=kt: e.matmul(pov, lhsT=ckvT[:, kt * 128:(kt + 1) * 128], rhs=wv, start=True, stop=True),
             reads=[b_wukv, b_ckvT], writes=[b_po])
        evac_copy(K, ev, V_all[:, kt, :, 0:128], pov, [b_po], [b_V], partial=True); ev += 1
    qf_r = RR(sbuf_ring(K, "qf", [128, 4, 192], F32, 2))
    qb_r = RR(sbuf_ring(K, "qb", [128, 4, 192], BF16, 2))
    for (t, ot) in own_tiles(need_ctx):
        qf, b_qf = qf_r.next()
        qff = qf.rearrange("p h d -> p (h d)")
        for half in range(2):
            po, b_po = K.ps_acc.next()
            for kc in range(4):
                kp = 128 if kc < 3 else 64
                P.op("pe", lambda e, po=po, kc=kc, kp=kp, ot=ot, half=half: e.matmul(po[:, 0:384], lhsT=cqT[0:kp, kc, ot * 128:(ot + 1) * 128], rhs=wuq[0:kp, kc, half * 384:(half + 1) * 384],
                                                                           start=(kc == 0), stop=(kc == 3)),
                     reads=[b_cqT, b_wuq], writes=[b_po], partial=(kc > 0))
            evac_copy(K, half, qff[:, half * 384:(half + 1) * 384], po[:, 0:384], [b_po], [b_qf], partial=(half > 0))
        qb, b_qb = qb_r.next()
        P.op("act", lambda e, qb=qb, qf=qf: e.copy(out=qb[:, :, 0:128], in_=qf[:, :, 0:128]), reads=[b_qf], writes=[b_qb])
        if t < 32:
            cs, b_cs = cs_r.next()
            P.dma("sp", cs, ropeA[t * 128:(t + 1) * 128], writes=[b_cs], partial=False)
            rope_apply(K, qf[:, :, 128:192], qb[:, :, 128:192], cs, b_qf, b_qb, b_cs, 4, 32, tmpr)
        else:
            P.op("dve", lambda e, qb=qb, qf=qf: e.tensor_copy(out=qb[:, :, 128:192], in_=qf[:, :, 128:192]), reads=[b_qf], writes=[b_qb], partial=True)
        ps, b_ps = K.ps_tp.next()
        psb = ps.bitcast(BF16).rearrange("p (k t) -> p k t", k=8)
        for h in range(4):
            P.op("pe", lambda e, psb=psb, qb=qb, h=h: e.transpose(out=psb[:, h, :], in_=qb[:, h, 0:128], identity=K.ident), reads=[b_qb, K.b_const], writes=[b_ps], partial=(h > 0))
            P.op("pe", lambda e, psb=psb, qb=qb, h=h: e.transpose(out=psb[0:64, 4 + h, :], in_=qb[:, h, 128:192], identity=K.ident), reads=[b_qb, K.b_const], writes=[b_ps], partial=True)
        ce = "act" if ot % 2 == 0 else "dve"
        copy_on(K, ce, qnT[:, :, ot * 128:(ot + 1) * 128], psb[:, 0:4, :], [b_ps], [b_qnT])
        copy_on(K, ce, qrT[:, :, ot * 128:(ot + 1) * 128], psb[0:64, 4:8, :], [b_ps], [b_qrT])
    barrier(K)
    A.reset(m1)
    heads = [(h, [(knT[:, h, :], qnT[:, h, :], b_knT, b_qnT), (krT, qrT[:, h, :], b_krT, b_qrT)]) for h in range(4)]
    attn_core(K, heads, V_all, b_V, 128, 192 ** -0.5, o_tm, b_otm, need_ctx)
    store_oT(K, o_tm, b_otm, oT_d, 0, 17 if need_ctx else 16)
    barrier(K)
    A.reset(m0)


def phase_gqa(K, pTM, q_norm, k_norm, ropeB, oT_d, need_ctx):
    P, A = K.P, K.A
    m0 = A.mark()
    NOWN = OWN_LAT + OWN_CTX
    kT = A.sb([128, 2, NTOK], BF16, "gkT"); b_kT = Buf("gkT")
    V_all = A.sb([128, 34, 2, 136], BF16, "gV"); b_V = Buf("gV")
    qT = A.sb([128, 4, NOWN], BF16, "gqT"); b_qT = Buf("gqT")
    o_tm = A.sb([128, 17, 512], BF16, "gotm"); b_otm = Buf("gotm")
    m1 = A.mark()
    gn = A.sb([128, 6, 128], F32, "ggain"); b_g = Buf("ggain")
    for h in range(4):
        P.dma("sp", gn[:, h, :], q_norm.partition_broadcast(128), writes=[b_g])
    for h in range(2):
        P.dma("sp", gn[:, 4 + h, :], k_norm.partition_broadcast(128), writes=[b_g])
    P.op("pool", lambda e: e.memset(V_all[:, :, :, 128:129], 1.0), writes=[b_V], partial=True)
    pb_r = RR(sbuf_ring(K, "pb", [128, 1024], F32, 2))
    cs_r = RR(sbuf_ring(K, "csB", [128, 2, 64], F32, 2))
    sq_r = RR(sbuf_ring(K, "gsq", [128, 6, 128], F32, 2))
    xn_r = RR(sbuf_ring(K, "gxn", [128, 6, 128], F32, 2))
    ss_r = RR(sbuf_ring(K, "gss", [128, 6], F32, 3))
    nb_r = RR(sbuf_ring(K, "gnb", [128, 6, 128], BF16, 2))
    tmpr = RR(sbuf_ring(K, "grt", [128, 6, 64], F32, 8))
    owned = dict(own_tiles(need_ctx))
    import os
    tl = [int(v) for v in os.environ.get("GQA_T", ",".join(map(str, range(34)))).split(",")]
    for t in tl:
        is_lat = t < 32
        own = t in owned
        pb, b_pb = pb_r.next()
        P.dma("sp", pb, pTM[t * 128:(t + 1) * 128, 640:1664], writes=[b_pb], partial=False)
        h0, nh = (0, 6) if own else (4, 2)
        x = pb[:, h0 * 128:(h0 + nh) * 128].rearrange("p (h d) -> p h d", h=nh)
        sq, b_sq = sq_r.next()
        ss, b_ss = ss_r.next()
        xn, b_xn = xn_r.next()
        nb, b_nb = nb_r.next()
        P.op("pool", lambda e, sq=sq, x=x, nh=nh: e.tensor_tensor(out=sq[:, 0:nh, :], in0=x, in1=x, op=ALU.mult), reads=[b_pb], writes=[b_sq])
        P.op("dve", lambda e, sq=sq, ss=ss, nh=nh: e.reduce_sum(out=ss[:, 0:nh], in_=sq[:, 0:nh, :], axis=AX.X), reads=[b_sq], writes=[b_ss])
        rms_rstd(K, ss, b_ss, nh, 128)
        P.op("dve", lambda e, xn=xn, x=x, ss=ss, nh=nh: e.tensor_tensor(out=xn[:, 0:nh, :], in0=x, in1=ss[:, 0:nh].unsqueeze(2).broadcast_to([128, nh, 128]), op=ALU.mult),
             reads=[b_pb, b_ss], writes=[b_xn])
        if is_lat:
            P.op("pool", lambda e, xn=xn, nh=nh, h0=h0: e.tensor_tensor(out=xn[:, 0:nh, :], in0=xn[:, 0:nh, :], in1=gn[:, h0:h0 + nh, :], op=ALU.mult),
                 reads=[b_xn, b_g], writes=[b_xn])
            cs, b_cs = cs_r.next()
            P.dma("sp", cs, ropeB[t * 128:(t + 1) * 128], writes=[b_cs], partial=False)
            rope_apply(K, xn[:, 0:nh, :], nb[:, 0:nh, :], cs, b_xn, b_nb, b_cs, nh, 64, tmpr)
        else:
            P.op("pool", lambda e, xn=xn, nb=nb, nh=nh, h0=h0: e.tensor_tensor(out=nb[:, 0:nh, :], in0=xn[:, 0:nh, :], in1=gn[:, h0:h0 + nh, :], op=ALU.mult),
                 reads=[b_xn, b_g], writes=[b_nb])
        ps, b_ps = K.ps_tp.next()
        psb = ps.bitcast(BF16).rearrange("p (k t) -> p k t", k=8)
        for j in range(nh):
            P.op("pe", lambda e, psb=psb, nb=nb, j=j: e.transpose(out=psb[:, j, :], in_=nb[:, j, :], identity=K.ident), reads=[b_nb, K.b_const], writes=[b_ps], partial=(j > 0))
        ce = "act" if t % 2 == 0 else "dve"
        if own:
            ot = owned[t]
            copy_on(K, ce, qT[:, :, ot * 128:(ot + 1) * 128], psb[:, 0:4, :], [b_ps], [b_qT])
            copy_on(K, ce, kT[:, :, t * 128:(t + 1) * 128], psb[:, 4:6, :], [b_ps], [b_kT])
        else:
            copy_on(K, ce, kT[:, :, t * 128:(t + 1) * 128], psb[:, 0:2, :], [b_ps], [b_kT])
        P.op("act", lambda e, pb=pb, t=t: e.copy(out=V_all[:, t, :, 0:128], in_=pb[:, 768:1024].rearrange("p (h d) -> p h d", h=2)), reads=[b_pb], writes=[b_V], partial=True)
    barrier(K)
    A.reset(m1)
    heads = [(h // 2, [(kT[:, h // 2, :], qT[:, h, :], b_kT, b_qT)]) for h in range(4)]
    import os
    if getattr(K, "dbg", None):
        P.dma("sp", K.dbg["qT"], qT, reads=[b_qT], writes=[K.b_oT_d])
        P.dma("sp", K.dbg["kT"], kT, reads=[b_kT], writes=[K.b_oT_d])
        P.dma("sp", K.dbg["V"], V_all, reads=[b_V], writes=[K.b_oT_d])
    if os.environ.get("BISECT") == "prep":
        barrier(K); A.reset(m0); return
    attn_core(K, heads, V_all, b_V, 128, 128 ** -0.5, o_tm, b_otm, need_ctx)
    if os.environ.get("BISECT") == "core":
        barrier(K); A.reset(m0); return
    store_oT(K, o_tm, b_otm, oT_d, 1, 17 if need_ctx else 16)
    barrier(K)
    A.reset(m0)


def rope_tables(half):
    pos = np.arange(NLAT)
    if half == 1:
        pos = NLAT - 1 - pos
    row = (pos // 64).astype(np.float32)
    col = (pos % 64).astype(np.float32)
    out = []
    for rot in (64, 128):
        nf = rot // 4
        inv = (10000.0 ** (-np.arange(nf, dtype=np.float32) / nf)).astype(np.float32)
        ang = np.concatenate([row[:, None] * inv, col[:, None] * inv], -1).astype(np.float32)
        out.append(np.stack([np.cos(ang), np.sin(ang)], 1).astype(np.float32))
    return out


def load_bcast(K, src_row, width, name):
    t = K.A.sb([128, width], F32, name)
    b = Buf(name)
    K.P.dma("sp", t, src_row.partition_broadcast(128), writes=[b], partial=False)
    return t, b


def ln_affine_tile(K, r_rows_dram, g_b, b_b, bgb, tmp, out_tile, b_out):
    P = K.P
    xt, b_xt = tmp["xt"].next()
    P.dma("sp", xt, r_rows_dram, writes=[b_xt], partial=False)
    mv, b_mv = ln_stats(K, xt, b_xt, D, tmp)
    P.op("dve", lambda e: e.tensor_scalar(out=xt, in0=xt, scalar1=mv[:, 0:1], scalar2=mv[:, 2:3], op0=ALU.subtract, op1=ALU.mult),
         reads=[b_xt, b_mv], writes=[b_xt])
    P.op("pool", lambda e: e.tensor_tensor(out=xt, in0=xt, in1=g_b, op=ALU.mult), reads=[b_xt, bgb], writes=[b_xt])
    P.op("dve", lambda e: e.tensor_tensor(out=out_tile, in0=xt, in1=b_b, op=ALU.add), reads=[b_xt, bgb], writes=[b_out])


def phase_D(K, x_lat, x_ctx, hT_d, oT_d, mod_d, w_mgate, b_mgate, w_branch, w_out, ln1_g, ln1_b,
            w_ffn_in, w_ffn_out, ln2_g, ln2_b, r_d, xmid_d, out_lat, out_ctx, need_ctx):
    P, A = K.P, K.A
    m0 = A.mark()
    n_own = 17 if need_ctx else 16

    def xrows(ot, c0=0, c1=D):
        return x_lat[ot * 128:(ot + 1) * 128, c0:c1] if ot < 16 else x_ctx[0:128, c0:c1]

    def outrows(ot):
        return out_lat[ot * 128:(ot + 1) * 128, :] if ot < 16 else out_ctx[0:128, :]

    modT_l, b_ml = load_modT(K, mod_d, 0, "DmodTl")
    modT_c, b_mc = load_modT(K, mod_d, 1, "DmodTc")
    braw = A.sb([64, 128], F32, "bgraw"); b_braw = Buf("bgraw")
    P.dma("sp", braw, b_mgate.rearrange("i (k p) -> (i k) p", p=128), writes=[b_braw], partial=False)
    ps, b_ps = K.ps_misc.next()
    P.op("pe", lambda e: e.transpose(out=ps[:, 0:64], in_=braw, identity=K.ident_f[0:64, 0:64]), reads=[b_braw, K.b_const], writes=[b_ps])
    bgT = A.sb([128, 64], F32, "bgT"); b_bgT = Buf("bgT")
    P.op("dve", lambda e: e.tensor_copy(out=bgT, in_=ps[:, 0:64]), reads=[b_ps], writes=[b_bgT])
    m_grp = A.mark()
    groups = [list(range(0, 8)), list(range(8, n_own))]
    oT_v = oT_d.rearrange("(c p) t -> p c t", p=128)
    def do_group(tiles):
        A.reset(m_grp)
        g0 = tiles[0] * 128
        ng = len(tiles) * 128
        slabs = [(s0, min(512, ng - s0)) for s0 in range(0, ng, 512)]
        accT = A.sb([128, 16, ng], BF16, "accT"); b_accT = Buf("accT")
        m_s1 = A.mark()
        hTg = A.sb([128, 16, ng], BF16, "hTg"); b_hTg = Buf("hTg")
        oTg = A.sb([128, 16, ng], BF16, "oTg"); b_oTg = Buf("oTg")
        P.dma("sp", hTg, hT_d[:, :, g0:g0 + ng], writes=[b_hTg], partial=False)
        P.dma("act", oTg, oT_v[:, :, g0:g0 + ng], writes=[b_oTg], partial=False)
        wg_r = RR(sbuf_ring(K, "wg", [128, 16, 512], BF16, 2))
        wb_r = RR(sbuf_ring(K, "wbr", [128, 4, 512], BF16, 2))
        acc = A.sb([128, 4, ng], F32, "acc"); b_acc = Buf("acc")
        sg_r = RR(sbuf_ring(K, "sig", [128, 512], F32, 2))
        tm_r = RR(sbuf_ring(K, "term", [128, 512], F32, 2))
        for jc in range(4):
            for i in range(4):
                wg, b_wg = load_w_chunk(K, wg_r, w_mgate[i], D, jc * 512, 512)
                wb, b_wb = load_w_chunk(K, wb_r, w_branch[i], 512, jc * 512, 512)
                for jj in range(4):
                    j = jc * 4 + jj
                    for (s0, sl) in slabs:
                        pg, b_pg = K.ps_acc.next()
                        for k in range(16):
                            P.op("pe", lambda e, pg=pg, wg=wg, k=k, jj=jj, s0=s0, sl=sl: e.matmul(pg[:, 0:sl], lhsT=wg[:, k, jj * 128:(jj + 1) * 128], rhs=hTg[:, k, s0:s0 + sl],
                                                                                        start=(k == 0), stop=(k == 15)),
                                 reads=[b_wg, b_hTg], writes=[b_pg], partial=(k > 0))
                        pb, b_pb = K.ps_acc.next()
                        for k in range(4):
                            P.op("pe", lambda e, pb=pb, wb=wb, k=k, jj=jj, s0=s0, sl=sl, i=i: e.matmul(pb[:, 0:sl], lhsT=wb[:, k, jj * 128:(jj + 1) * 128], rhs=oTg[:, i * 4 + k, s0:s0 + sl],
                                                                                             start=(k == 0), stop=(k == 3)),
                                 reads=[b_wb, b_oTg], writes=[b_pb], partial=(k > 0))
                        sg, b_sg = sg_r.next()
                        P.op("act", lambda e, sg=sg, pg=pg, sl=sl, i=i, j=j: e.activation(out=sg[:, 0:sl], in_=pg[:, 0:sl], func=AF.Sigmoid, bias=bgT[:, i * 16 + j:i * 16 + j + 1]),
                             reads=[b_pg, b_bgT], writes=[b_sg])
                        if i == 0:
                            P.op("dve", lambda e, sg=sg, pb=pb, jj=jj, s0=s0, sl=sl: e.tensor_tensor(out=acc[:, jj, s0:s0 + sl], in0=sg[:, 0:sl], in1=pb[:, 0:sl], op=ALU.mult),
                                 reads=[b_sg, b_pb], writes=[b_acc], partial=True)
                        else:
                            tm, b_tm = tm_r.next()
                            P.op("dve", lambda e, tm=tm, sg=sg, pb=pb, sl=sl: e.tensor_tensor(out=tm[:, 0:sl], in0=sg[:, 0:sl], in1=pb[:, 0:sl], op=ALU.mult),
                                 reads=[b_sg, b_pb], writes=[b_tm])
                            P.op("dve", lambda e, tm=tm, jj=jj, s0=s0, sl=sl: e.tensor_tensor(out=acc[:, jj, s0:s0 + sl], in0=acc[:, jj, s0:s0 + sl], in1=tm[:, 0:sl], op=ALU.add),
                                 reads=[b_tm, b_acc], writes=[b_acc], partial=True)
            P.op("act", lambda e, jc=jc: e.copy(out=accT[:, jc * 4:(jc + 1) * 4, :], in_=acc), reads=[b_acc], writes=[b_accT], partial=True)
        barrier(K)
        A.reset(m_s1)
        g1l, b_g1l = load_bcast(K, mod_d[0:1, 2 * D:3 * D], D, "g1l")
        g1c, b_g1c = load_bcast(K, mod_d[1:2, 2 * D:3 * D], D, "g1c")
        wo_r = RR(sbuf_ring(K, "wo", [128, 16, 512], BF16, 2))
        xc_r = RR(sbuf_ring(K, "xc", [128, 512], F32, 3))
        rr_r = RR(sbuf_ring(K, "rr", [128, 512], F32, 3))
        for nch in range(4):
            wo, b_wo = load_w_chunk(K, wo_r, w_out, D, nch * 512, 512)
            for ti, ot in enumerate(tiles):
                po, b_po = K.ps_acc.next()
                for k in range(16):
                    P.op("pe", lambda e, po=po, wo=wo, k=k, ti=ti: e.matmul(po, lhsT=accT[:, k, ti * 128:(ti + 1) * 128], rhs=wo[:, k, :], start=(k == 0), stop=(k == 15)),
                         reads=[b_accT, b_wo], writes=[b_po], partial=(k > 0))
                xc, b_xc = xc_r.next()
                P.dma("act", xc, xrows(ot, nch * 512, (nch + 1) * 512), writes=[b_xc], partial=False)
                rr, b_rr = rr_r.next()
                gb, bgb = (g1l, b_g1l) if ot < 16 else (g1c, b_g1c)
                P.op("dve", lambda e, rr=rr, po=po, gb=gb, nch=nch: e.tensor_tensor(out=rr, in0=po, in1=gb[:, nch * 512:(nch + 1) * 512], op=ALU.mult),
                     reads=[b_po, bgb], writes=[b_rr])
                P.op("dve", lambda e, rr=rr, xc=xc: e.scalar_tensor_tensor(out=rr, in0=xc, scalar=ALPHA, in1=rr, op0=ALU.mult, op1=ALU.add),
                     reads=[b_xc, b_rr], writes=[b_rr])
                P.dma("sp", r_d[ot * 128:(ot + 1) * 128, nch * 512:(nch + 1) * 512], rr, reads=[b_rr], writes=[K.b_r_d])
        barrier(K)
        A.reset(m_grp)
        hT2 = A.sb([128, 16, ng], BF16, "hT2")
        b_hT2 = [Buf(f"hT2_{i}") for i in range(len(tiles))]
        m_s3b = A.mark()
        l1g, b_l1 = load_bcast(K, ln1_g, D, "l1g")
        l1b, _ = load_bcast(K, ln1_b, D, "l1b")
        b_l1b = _
        tmp = {
            "st": RR(sbuf_ring(K, "Dst", [128, 4, 6], F32, 2)), "mv": RR(sbuf_ring(K, "Dmv", [128, 4], F32, 2)),
            "xt": RR(sbuf_ring(K, "Dxt", [128, D], F32, 2)), "xs": RR(sbuf_ring(K, "Dxs", [128, D], BF16, 2)),
            "t1": RR(sbuf_ring(K, "Dt1", [128, 8, 128], F32, 2)),
        }
        xm_r = RR(sbuf_ring(K, "xm", [128, D], F32, 2))
        for ti, ot in enumerate(tiles):
            xm, b_xm = xm_r.next()
            xt, b_xt = tmp["xt"].next()
            P.dma("sp", xt, r_d[ot * 128:(ot + 1) * 128, :], writes=[b_xt], partial=False)
            mv, b_mv = ln_stats(K, xt, b_xt, D, tmp)
            P.op("dve", lambda e, xt=xt, mv=mv: e.tensor_scalar(out=xt, in0=xt, scalar1=mv[:, 0:1], scalar2=mv[:, 2:3], op0=ALU.subtract, op1=ALU.mult),
                 reads=[b_xt, b_mv], writes=[b_xt])
            P.op("dve", lambda e, xt=xt: e.tensor_tensor(out=xt, in0=xt, in1=l1g, op=ALU.mult), reads=[b_xt, b_l1], writes=[b_xt])
            P.op("dve", lambda e, xt=xt, xm=xm: e.tensor_tensor(out=xm, in0=xt, in1=l1b, op=ALU.add), reads=[b_xt, b_l1b], writes=[b_xm])
            P.dma("act", xmid_d[ot * 128:(ot + 1) * 128, :], xm, reads=[b_xm], writes=[K.b_xmid_d])
            mt, bm = (modT_l, b_ml) if ot < 16 else (modT_c, b_mc)
            hT_from_sbuf(K, xm, b_xm, mt, bm, 48, hT2[:, :, ti * 128:(ti + 1) * 128], b_hT2[ti], tmp)
        barrier(K)
        A.reset(m_s3b)
        aT = A.sb([128, 44, ng], BF16, "aT"); b_aT = Buf("aT")
        m_s3 = A.mark()
        wu_r = RR(sbuf_ring(K, "wu", [128, 16, 128], BF16, 2))
        wgt_r = RR(sbuf_ring(K, "wgt", [128, 16, 128], BF16, 2))
        sl_r = RR(sbuf_ring(K, "silu", [128, 512], F32, 2))
        for j in range(44):
            wu, b_wu = load_w_chunk(K, wu_r, w_ffn_in, D, j * 128, 128)
            wgt, b_wgt = load_w_chunk(K, wgt_r, w_ffn_in, D, FFH + j * 128, 128)
            for (s0, sl) in slabs:
                tl = list(range(s0 // 128, (s0 + sl) // 128))
                pu, b_pu = K.ps_acc.next()
                pgt, b_pgt = K.ps_acc.next()
                for k in range(16):
                    P.op("pe", lambda e, pu=pu, wu=wu, k=k, s0=s0, sl=sl: e.matmul(pu[:, 0:sl], lhsT=wu[:, k, :], rhs=hT2[:, k, s0:s0 + sl], start=(k == 0), stop=(k == 15)),
                         reads=[b_wu] + [b_hT2[t] for t in tl], writes=[b_pu], partial=(k > 0))
                for k in range(16):
                    P.op("pe", lambda e, pgt=pgt, wgt=wgt, k=k, s0=s0, sl=sl: e.matmul(pgt[:, 0:sl], lhsT=wgt[:, k, :], rhs=hT2[:, k, s0:s0 + sl], start=(k == 0), stop=(k == 15)),
                         reads=[b_wgt] + [b_hT2[t] for t in tl], writes=[b_pgt], partial=(k > 0))
                sv, b_sv = sl_r.next()
                P.op("act", lambda e, sv=sv, pgt=pgt, sl=sl: e.activation(out=sv[:, 0:sl], in_=pgt[:, 0:sl], func=AF.Silu), reads=[b_pgt], writes=[b_sv])
                P.op("dve", lambda e, sv=sv, pu=pu, j=j, s0=s0, sl=sl: e.tensor_tensor(out=aT[:, j, s0:s0 + sl], in0=sv[:, 0:sl], in1=pu[:, 0:sl], op=ALU.mult),
                     reads=[b_sv, b_pu], writes=[b_aT], partial=True)
        barrier(K)
        A.reset(m_s3)
        g2l, b_g2l = load_bcast(K, mod_d[0:1, 5 * D:6 * D], D, "g2l")
        g2c, b_g2c = load_bcast(K, mod_d[1:2, 5 * D:6 * D], D, "g2c")
        CW = 256
        w2_r = RR(sbuf_ring(K, "w2", [128, 44, CW], BF16, 2))
        xc2_r = RR(sbuf_ring(K, "xc2", [128, CW], F32, 3))
        rr2_r = RR(sbuf_ring(K, "rr2", [128, CW], F32, 3))
        for nch in range(D // CW):
            w2, b_w2 = load_w_chunk(K, w2_r, w_ffn_out, FFH, nch * CW, CW)
            for ti, ot in enumerate(tiles):
                po, b_po = K.ps_acc.next()
                for k in range(44):
                    P.op("pe", lambda e, po=po, w2=w2, k=k, ti=ti: e.matmul(po[:, 0:256], lhsT=aT[:, k, ti * 128:(ti + 1) * 128], rhs=w2[:, k, :], start=(k == 0), stop=(k == 43)),
                         reads=[b_aT, b_w2], writes=[b_po], partial=(k > 0))
                xc, b_xc = xc2_r.next()
                P.dma("act", xc, xmid_d[ot * 128:(ot + 1) * 128, nch * 256:(nch + 1) * 256], reads=[K.b_xmid_d], writes=[b_xc], partial=False)
                rr, b_rr = rr2_r.next()
                gb, bgb = (g2l, b_g2l) if ot < 16 else (g2c, b_g2c)
                P.op("dve", lambda e, rr=rr, po=po, gb=gb, nch=nch: e.tensor_tensor(out=rr, in0=po[:, 0:256], in1=gb[:, nch * 256:(nch + 1) * 256], op=ALU.mult),
                     reads=[b_po, bgb], writes=[b_rr])
                P.op("dve", lambda e, rr=rr, xc=xc: e.scalar_tensor_tensor(out=rr, in0=xc, scalar=ALPHA, in1=rr, op0=ALU.mult, op1=ALU.add),
                     reads=[b_xc, b_rr], writes=[b_rr])
                P.dma("sp", r_d[ot * 128:(ot + 1) * 128, nch * 256:(nch + 1) * 256], rr, reads=[b_rr], writes=[K.b_r_d])
        barrier(K)
        A.reset(m_grp)
        l2g, b_l2g = load_bcast(K, ln2_g, D, "l2g")
        l2b, b_l2b = load_bcast(K, ln2_b, D, "l2b")
        tmp = {"st": RR(sbuf_ring(K, "Est", [128, 4, 6], F32, 2)), "mv": RR(sbuf_ring(K, "Emv", [128, 4], F32, 2)),
               "xt": RR(sbuf_ring(K, "Ext", [128, D], F32, 3))}
        for ti, ot in enumerate(tiles):
            xt, b_xt = tmp["xt"].next()
            P.dma("sp", xt, r_d[ot * 128:(ot + 1) * 128, :], reads=[K.b_r_d], writes=[b_xt], partial=False)
            mv, b_mv = ln_stats(K, xt, b_xt, D, tmp)
            P.op("dve", lambda e, xt=xt, mv=mv: e.tensor_scalar(out=xt, in0=xt, scalar1=mv[:, 0:1], scalar2=mv[:, 2:3], op0=ALU.subtract, op1=ALU.mult),
                 reads=[b_xt, b_mv], writes=[b_xt])
            P.op("dve", lambda e, xt=xt: e.tensor_tensor(out=xt, in0=xt, in1=l2g, op=ALU.mult), reads=[b_xt, b_l2g], writes=[b_xt])
            P.op("dve", lambda e, xt=xt: e.tensor_tensor(out=xt, in0=xt, in1=l2b, op=ALU.add), reads=[b_xt, b_l2b], writes=[b_xt])
            P.dma("act", outrows(ot), xt, reads=[b_xt], writes=[K.b_out])
        barrier(K)
    for tiles in groups:
        do_group(tiles)
    A.reset(m0)


def mamba_consts():
    s = np.arange(128)[:, None]
    l = np.arange(128)[None, :]
    tri = np.stack([(s <= l), (s >= l), np.ones((128, 128), bool)]).astype(np.float32)
    l5 = np.arange(512)[None, :]
    masks = np.zeros((2, 4, 128, 512), np.float32)
    for j in range(4):
        masks[0, j] = (j * 128 + s) <= l5
        masks[1, j] = (j * 128 + s) >= l5
    return tri, masks.astype(ml_dtypes.bfloat16)


def store_oT_tile(K, y_bf, b_y, oT_d, branch, ot, stg):
    P = K.P
    ps, b_ps = K.ps_tp.next()
    psb = ps.bitcast(BF16).rearrange("p (k t) -> p k t", k=8)
    for h in range(4):
        P.I("pe", "transpose", reads=[b_y, K.b_const], writes=[b_ps], partial=(h > 0),
            out=psb[:, h, :], in_=y_bf[:, h * 128:(h + 1) * 128], identity=K.ident)
    so, b_so = stg.next()
    P.I("dve", "tensor_copy", reads=[b_ps], writes=[b_so], out=so, in_=psb[:, 0:4, :])
    P.dma("sp", oT_d[branch * 512:(branch + 1) * 512, ot * 128:(ot + 1) * 128].rearrange("(h p) t -> p h t", p=128), so,
          reads=[b_so], writes=[K.b_oT_d])


def phase_mamba(K, pTM, pFM, conv_w, conv_b, a_log, dt_bias, d_skip, norm_g, tri_d, masks_d, cumT_d, oT_d, need_ctx):
    P, A = K.P, K.A
    m0 = A.mark()
    BT = A.sb([128, 2, NTOK], BF16, "mBT"); b_BT = Buf("mBT")
    CT = A.sb([128, 2, NOWN], BF16, "mCT"); b_CT = Buf("mCT")
    xdt = A.sb([128, 2, 34, 512], BF16, "mxdt"); b_xdt = Buf("mxdt")
    xtm = A.sb([128, 17, 512], BF16, "mxtm"); b_xtm = Buf("mxtm")
    negcum = A.sb([128, 34, 16], F32, "mnegcum"); b_nc = Buf("mnegcum")
    m1 = A.mark()
    tri = A.sb([128, 3, 128], F32, "mtri"); b_tri = Buf("mtri")
    P.dma("sp", tri, tri_d.rearrange("a p l -> p a l"), writes=[b_tri], partial=False)
    craw = A.sb([32, 128], F32, "mcraw"); b_craw = Buf("mcraw")
    P.dma("sp", craw[0:24, :], conv_w.rearrange("k (c p) -> (k c) p", p=128), writes=[b_craw])
    P.dma("sp", craw[24:32, :], conv_b.rearrange("o (c p) -> (o c) p", p=128), writes=[b_craw])
    ps, b_ps = K.ps_misc.next()
    P.I("pe", "transpose", reads=[b_craw, K.b_const], writes=[b_ps], out=ps[:, 0:32], in_=craw, identity=K.ident_f[0:32, 0:32])
    cw = A.sb([128, 32], F32, "mcw"); b_cw = Buf("mcw")
    P.I("dve", "tensor_copy", reads=[b_ps], writes=[b_cw], out=cw, in_=ps[:, 0:32])
    alog, b_alog = load_bcast(K, a_log, 16, "malog")
    dtb, b_dtb = load_bcast(K, dt_bias, 16, "mdtb")
    P.I("act", "activation", reads=[b_alog], writes=[b_alog], out=alog, in_=alog, func=AF.Exp)
    P.I("dve", "tensor_scalar_mul", reads=[b_alog], writes=[b_alog], out=alog, in0=alog, scalar1=-1.0)
    dt = A.sb([128, 34, 16], F32, "mdt"); b_dt = Buf("mdt")
    da = A.sb([128, 34, 16], F32, "mda"); b_da = Buf("mda")
    P.dma("sp", dt, pTM[:, 2176:2192].rearrange("(t p) c -> p t c", p=128), reads=[K.b_pTM], writes=[b_dt], partial=False, allow_slow_non_contiguous=False)
    P.I("dve", "tensor_tensor", reads=[b_dt, b_dtb], writes=[b_dt], out=dt, in0=dt, in1=dtb.unsqueeze(1).broadcast_to([128, 34, 16]), op=ALU.add)
    P.I("act", "activation", reads=[b_dt], writes=[b_dt], out=dt, in_=dt, func=AF.Exp)
    P.I("act", "activation", reads=[b_dt], writes=[b_dt], out=dt, in_=dt, func=AF.Ln, bias=1.0)
    P.I("dve", "tensor_tensor", reads=[b_dt, b_alog], writes=[b_da], out=da, in0=dt, in1=alog.unsqueeze(1).broadcast_to([128, 34, 16]), op=ALU.mult)
    ntot_r = RR(sbuf_ring(K, "mntot", [128, 8], F32, 3))
    orders = [[32, 33] + list(range(32)), [33, 32] + list(range(31, -1, -1))]
    for d in range(2):
        ntot, b_nt = ntot_r.next()
        P.I("pool", "memset", writes=[b_nt], ap=ntot, constant=0.0)
        for T in orders[d]:
            pc, b_pc = K.ps_misc.next()
            P.I("pe", "matmul", reads=[b_tri, b_da], writes=[b_pc], out=pc[:, 0:8], lhsT=tri[:, d, :], rhs=da[:, T, d * 8:(d + 1) * 8], start=True, stop=True)
            P.I("pe", "matmul", reads=[b_tri, b_da], writes=[b_pc], partial=True, out=pc[:, 8:16], lhsT=tri[:, 2, :], rhs=da[:, T, d * 8:(d + 1) * 8], start=True, stop=True)
            P.I("dve", "scalar_tensor_tensor", reads=[b_pc, b_nt], writes=[b_nc], partial=True,
                out=negcum[:, T, d * 8:(d + 1) * 8], in0=pc[:, 0:8], scalar=-1.0, in1=ntot, op0=ALU.mult, op1=ALU.add)
            ntot2, b_nt2 = ntot_r.next()
            P.I("dve", "scalar_tensor_tensor", reads=[b_pc, b_nt], writes=[b_nt2],
                out=ntot2, in0=pc[:, 8:16], scalar=-1.0, in1=ntot, op0=ALU.mult, op1=ALU.add)
            ntot, b_nt = ntot2, b_nt2
    m2 = A.mark()
    cumT = A.sb([16, NTOK], F32, "mcumT"); b_cumT = Buf("mcumT")
    for T in range(34):
        pt_, b_pt_ = K.ps_misc.next()
        P.I("pe", "transpose", reads=[b_nc, K.b_const], writes=[b_pt_], out=pt_[0:16, 0:128], in_=negcum[:, T, :], identity=K.ident_f)
        P.I("dve", "tensor_scalar_mul", reads=[b_pt_], writes=[b_cumT], partial=True, out=cumT[:, T * 128:(T + 1) * 128], in0=pt_[0:16, 0:128], scalar1=-1.0)
    P.dma("sp", cumT_d, cumT, reads=[b_cumT], writes=[K.b_cumT_d], partial=False)
    barrier(K)
    A.reset(m2)
    pf_r = RR(sbuf_ring(K, "mpf", [128, NTOK], F32, 2))
    u = A.sb([128, NTOK], F32, "mu"); b_u = Buf("mu")
    xTb_r = RR(sbuf_ring(K, "mxTb", [128, NTOK], BF16, 2))
    segs = [(0, NLAT), (NLAT, NTOK)]
    for c in range(8):
        pf, b_pf = pf_r.next()
        P.dma("sp" if c % 2 == 0 else "act", pf, pFM[1536 + c * 128:1536 + (c + 1) * 128, :], reads=[K.b_pFM], writes=[b_pf], partial=False)
        P.I("act", "activation", reads=[b_pf, b_cw], writes=[b_u], out=u, in_=pf, func=AF.Identity, scale=cw[:, 8 + c:9 + c], bias=cw[:, 24 + c:25 + c])
        for (a, b) in segs:
            P.I("dve", "scalar_tensor_tensor", reads=[b_pf, b_cw, b_u], writes=[b_u], partial=True,
                out=u[:, a + 1:b], in0=pf[:, a:b - 1], scalar=cw[:, c:c + 1], in1=u[:, a + 1:b], op0=ALU.mult, op1=ALU.add)
            P.I("dve", "scalar_tensor_tensor", reads=[b_pf, b_cw, b_u], writes=[b_u], partial=True,
                out=u[:, a:b - 1], in0=pf[:, a + 1:b], scalar=cw[:, 16 + c:17 + c], in1=u[:, a:b - 1], op0=ALU.mult, op1=ALU.add)
        if c < 4:
            xTb, b_xTb = xTb_r.next()
            P.I("act", "activation", reads=[b_u], writes=[b_xTb], out=xTb, in_=u, func=AF.Silu)
            for T0 in range(0, 34, 8):
                nT = min(8, 34 - T0)
                ps2, b_ps2 = K.ps_tp.next()
                psb = ps2.bitcast(BF16).rearrange("p (k t) -> p k t", k=8)
                for i in range(nT):
                    P.I("pe", "transpose", reads=[b_xTb, K.b_const], writes=[b_ps2], partial=(i > 0),
                        out=psb[:, i, :], in_=xTb[:, (T0 + i) * 128:(T0 + i + 1) * 128], identity=K.ident)
                src = psb[:, 0:nT, :].rearrange("p t (h e) -> p t h e", h=2)
                for d in range(2):
                    P.I("dve", "tensor_tensor", reads=[b_ps2, b_dt], writes=[b_xdt], partial=True,
                        out=xdt[:, d, T0:T0 + nT, c * 128:(c + 1) * 128].rearrange("p t (h e) -> p t h e", h=2), in0=src,
                        in1=dt[:, T0:T0 + nT, d * 8 + 2 * c:d * 8 + 2 * c + 2].unsqueeze(3).broadcast_to([128, nT, 2, 64]), op=ALU.mult)
                if T0 < 16:
                    P.I("dve", "tensor_copy", reads=[b_ps2], writes=[b_xtm], partial=True, out=xtm[:, T0:T0 + 8, c * 128:(c + 1) * 128], in_=psb[:, 0:8, :])
                if T0 == 32:
                    P.I("dve", "tensor_copy", reads=[b_ps2], writes=[b_xtm], partial=True, out=xtm[:, 16, c * 128:(c + 1) * 128], in_=psb[:, 0, :])
        elif c < 6:
            P.I("act", "activation", reads=[b_u], writes=[b_BT], partial=True, out=BT[:, c - 4, :], in_=u, func=AF.Silu)
        else:
            P.I("act", "activation", reads=[b_u], writes=[b_CT], partial=True, out=CT[:, c - 6, 0:OWN_LAT], in_=u[:, 0:OWN_LAT], func=AF.Silu)
            P.I("act", "activation", reads=[b_u], writes=[b_CT], partial=True, out=CT[:, c - 6, OWN_LAT:NOWN], in_=u[:, NLAT:NLAT + OWN_CTX], func=AF.Silu)
    barrier(K)
    A.reset(m1)
    ysum = A.sb([128, 17, 512], F32, "mysum"); b_ys = Buf("mysum")
    masks = A.sb([128, 2, 4, 512], BF16, "mmask"); b_mk = Buf("mmask")
    P.dma("sp", masks, masks_d.rearrange("d j p l -> p d j l"), writes=[b_mk], partial=False)
    crow_r = RR(sbuf_ring(K, "mcrow", [128, NOWN], F32, 2))
    SKEW = 2
    dec_r = RR(sbuf_ring(K, "mdec", [128, 512], F32, SKEW + 2))
    pt_r = RR(sbuf_ring(K, "mpt", [128, 512], BF16, SKEW + 2))
    obanks = K.banks[4:8]
    qgroups = []
    for i in range(4):
        qgroups.append((i * 512, 512, i * 4, False))
    if need_ctx:
        qgroups.append((OWN_LAT, 128, 32, True))
    units = []
    for d in range(2):
        for h in range(8):
            for (q0, nq, T0, is_ctx) in qgroups:
                if not is_ctx:
                    if d == 0:
                        blocks = [(32, None), (33, None)] + [(T, None) for T in range(T0)] + [(T0 + j, j) for j in range(4)]
                    else:
                        blocks = [(32, None), (33, None)] + [(T, None) for T in range(31, T0 + 3, -1)] + [(T0 + j, j) for j in range(4)]
                else:
                    blocks = [(32, 0)] if d == 0 else [(33, None), (32, 0)]
                for bi, (kb, dj) in enumerate(blocks):
                    units.append({"d": d, "h": h, "q0": q0, "nq": nq, "kb": kb, "dj": dj, "first": bi == 0, "last": bi == len(blocks) - 1,
                                  "newrow": (bi == 0 and q0 == 0)})
    cur = {}

    def stage1(u):
        d, h, q0, nq, kb, dj = u["d"], u["h"], u["q0"], u["nq"], u["kb"], u["dj"]
        g = h // 4
        col = d * 8 + h
        if u["newrow"]:
            crow, b_crow = crow_r.next()
            P.dma("sp", crow[:, 0:OWN_LAT], cumT_d[col:col + 1, 0:OWN_LAT].partition_broadcast(128), reads=[K.b_cumT_d], writes=[b_crow], partial=False)
            P.dma("act", crow[:, OWN_LAT:NOWN], cumT_d[col:col + 1, NLAT:NLAT + OWN_CTX].partition_broadcast(128), reads=[K.b_cumT_d], writes=[b_crow])
            cur["crow"], cur["b_crow"] = crow, b_crow
        crow, b_crow = cur["crow"], cur["b_crow"]
        pc, b_pc = K.ps_acc.next()
        P.I("pe", "matmul", reads=[b_BT, b_CT], writes=[b_pc], out=pc[:, 0:nq], lhsT=BT[:, g, kb * 128:(kb + 1) * 128], rhs=CT[:, g, q0:q0 + nq], start=True, stop=True)
        dec, b_dec = dec_r.next()
        if dj is None:
            P.I("act", "activation", reads=[b_crow, b_nc], writes=[b_dec], out=dec[:, 0:nq], in_=crow[:, q0:q0 + nq], func=AF.Exp, bias=negcum[:, kb, col:col + 1])
        else:
            P.I("dve", "tensor_scalar", reads=[b_crow, b_nc], writes=[b_dec], out=dec[:, 0:nq], in0=crow[:, q0:q0 + nq], scalar1=negcum[:, kb, col:col + 1], scalar2=0.0,
                op0=ALU.add, op1=ALU.min)
            P.I("act", "activation", reads=[b_dec], writes=[b_dec], out=dec[:, 0:nq], in_=dec[:, 0:nq], func=AF.Exp)
        pt, b_pt = pt_r.next()
        P.I("dve", "tensor_tensor", reads=[b_pc, b_dec], writes=[b_pt], out=pt[:, 0:nq], in0=pc[:, 0:nq], in1=dec[:, 0:nq], op=ALU.mult)
        if dj is not None:
            P.I("pool", "tensor_tensor", reads=[b_pt, b_mk], writes=[b_pt], out=pt[:, 0:nq], in0=pt[:, 0:nq], in1=masks[:, d, dj, 0:nq], op=ALU.mult)
        u["pt"], u["b_pt"] = pt, b_pt

    def stage2(u):
        d, h, q0, nq, kb = u["d"], u["h"], u["q0"], u["nq"], u["kb"]
        pt, b_pt = u["pt"], u["b_pt"]
        for qs in range(nq // 128):
            ob, b_ob = obanks[qs]
            P.I("pe", "matmul", reads=[b_pt, b_xdt], writes=[b_ob], partial=(not u["first"]), out=ob[:, 0:64], lhsT=pt[:, qs * 128:(qs + 1) * 128],
                rhs=xdt[:, d, kb, h * 64:(h + 1) * 64], start=u["first"], stop=u["last"])
        if u["last"]:
            for qs in range(nq // 128):
                ob, b_ob = obanks[qs]
                ot = q0 // 128 + qs
                if d == 0:
                    P.I("dve", "tensor_copy", reads=[b_ob], writes=[b_ys], partial=True, out=ysum[:, ot, h * 64:(h + 1) * 64], in_=ob[:, 0:64])
                else:
                    P.I("dve", "tensor_tensor", reads=[b_ob, b_ys], writes=[b_ys], partial=True, out=ysum[:, ot, h * 64:(h + 1) * 64], in0=ob[:, 0:64],
                        in1=ysum[:, ot, h * 64:(h + 1) * 64], op=ALU.add)
    nu = len(units)
    for i in range(nu + SKEW):
        if i < nu:
            stage1(units[i])
        if i >= SKEW:
            stage2(units[i - SKEW])
    dsk, b_dsk = load_bcast(K, d_skip, 8, "mdsk")
    ng, b_ng = load_bcast(K, norm_g, 512, "mng")
    z_r = RR(sbuf_ring(K, "mz", [128, 512], F32, 2))
    t_r = RR(sbuf_ring(K, "mt", [128, 512], F32, 2))
    ss_r = RR(sbuf_ring(K, "mss", [128, 2], F32, 3))
    yb_r = RR(sbuf_ring(K, "myb", [128, 512], BF16, 2))
    stg = RR(sbuf_ring(K, "mstg", [128, 4, 128], BF16, 2))
    for (T, ot) in own_tiles(need_ctx):
        z, b_z = z_r.next()
        P.dma("sp", z, pTM[T * 128:(T + 1) * 128, 1664:2176], reads=[K.b_pTM], writes=[b_z], partial=False)
        P.I("act", "activation", reads=[b_z], writes=[b_z], out=z, in_=z, func=AF.Silu)
        t, b_t = t_r.next()
        P.I("dve", "tensor_tensor", reads=[b_xtm, b_dsk], writes=[b_t], out=t.rearrange("p (h e) -> p h e", h=8), in0=xtm[:, ot, :].rearrange("p (h e) -> p h e", h=8),
            in1=dsk.unsqueeze(2).broadcast_to([128, 8, 64]), op=ALU.mult)
        P.I("pool", "tensor_tensor", reads=[b_t, b_ys], writes=[b_t], out=t, in0=t, in1=ysum[:, ot, :], op=ALU.add)
        P.I("dve", "tensor_tensor", reads=[b_t, b_z], writes=[b_t], out=t, in0=t, in1=z, op=ALU.mult)
        P.I("pool", "tensor_tensor", reads=[b_t], writes=[b_z], out=z, in0=t, in1=t, op=ALU.mult)
        ss, b_ss = ss_r.next()
        P.I("dve", "reduce_sum", reads=[b_z], writes=[b_ss], out=ss[:, 0:1], in_=z, axis=AX.X)
        rms_rstd(K, ss, b_ss, 1, 512)
        yb, b_yb = yb_r.next()
        P.I("dve", "scalar_tensor_tensor", reads=[b_t, b_ss, b_ng], writes=[b_yb], out=yb, in0=t, scalar=ss[:, 0:1], in1=ng, op0=ALU.mult, op1=ALU.mult)
        store_oT_tile(K, yb, b_yb, oT_d, 3, ot, stg)
    barrier(K)
    A.reset(m0)


def hyena_consts(n, n_own):
    N = 2 * n
    nb = n + 1
    F = (nb + 127) // 128 * 128
    s = np.arange(n, dtype=np.float64)
    f = np.arange(F, dtype=np.float64)
    valid = (f < nb)
    th = 2 * np.pi * np.outer(s, f) / N
    CT = np.cos(th) * valid[None, :]
    ST = np.sin(th) * valid[None, :]
    t = np.arange(n_own, dtype=np.float64)
    w = np.where((f == 0) | (f == n), 1.0, 2.0) * valid
    th2 = 2 * np.pi * np.outer(f, t) / N
    Ci = (w[:, None] * np.cos(th2)) / N
    Si = (w[:, None] * np.sin(th2)) / N
    tt = np.linspace(0.0, 1.0, n, dtype=np.float32)[:, None]
    omega = (2.0 * math.pi * np.arange(n, dtype=np.float32) / n).astype(np.float32)
    bands = np.linspace(1e-4, 15, 16, dtype=np.float32)
    ang = omega[:, None] * bands[None, :]
    feats = np.concatenate([tt, np.cos(ang), -np.sin(ang)], -1).astype(np.float32)
    negt = (-tt[:, 0]).astype(np.float32).reshape(n // 128, 128).T.copy()
    bf = ml_dtypes.bfloat16
    nsc, nfb = n // 128, F // 128
    CTt = np.ascontiguousarray(CT.reshape(nsc, 128, nfb, 128).transpose(2, 1, 0, 3)).astype(bf)
    STt = np.ascontiguousarray(ST.reshape(nsc, 128, nfb, 128).transpose(2, 1, 0, 3)).astype(bf)
    return {"CT": CTt, "ST": STt, "Ci": Ci.astype(bf), "Si": Si.astype(bf),
            "featsT": np.ascontiguousarray(feats.T), "negt": negt}


def hy_sin(K, dst, src_ps, rows, w, fcol, fbcol, b_src, b_dst, b_par, tmp):
    P = K.P
    a, b_a = tmp.next()
    s2, b_s2 = tmp.next()
    s4, b_s4 = tmp.next()
    a, s2, s4 = a[0:rows, 0:w], s2[0:rows, 0:w], s4[0:rows, 0:w]
    P.I("dve", "tensor_scalar", reads=[b_src, b_par], writes=[b_a], out=a, in0=src_ps, scalar1=fcol, scalar2=fbcol, op0=ALU.mult, op1=ALU.add)
    P.I("act", "activation", reads=[b_a], writes=[b_s2], out=s2, in_=a, func=AF.Sin, scale=0.5)
    P.I("act", "activation", reads=[b_a], writes=[b_s4], out=s4, in_=a, func=AF.Sin, scale=0.25)
    P.I("dve", "tensor_tensor", reads=[b_s4], writes=[b_s4], out=s4, in0=s4, in1=s4, op=ALU.mult)
    P.I("dve", "tensor_scalar", reads=[b_s4], writes=[b_s4], out=s4, in0=s4, scalar1=-2.0, scalar2=1.0, op0=ALU.mult, op1=ALU.add)
    P.I("dve", "scalar_tensor_tensor", reads=[b_s2, b_s4], writes=[b_dst], partial=True, out=dst, in0=s2, scalar=2.0, in1=s4, op0=ALU.mult, op1=ALU.mult)


def phase_hyena(K, pFM, conv_w, conv_b, w1, b1, w2, b2, w3, freq, skip, sgn_d, deltas_d, geoms, oT_d):
    P, A = K.P, K.A
    m0 = A.mark()
    craw = A.sb([52, 128], F32, "hcraw"); b_craw = Buf("hcraw")
    P.dma("sp", craw[0:36, :], conv_w.rearrange("k (c p) -> (k c) p", p=128), writes=[b_craw])
    P.dma("sp", craw[36:48, :], conv_b.rearrange("o (c p) -> (o c) p", p=128), writes=[b_craw])
    P.dma("sp", craw[48:52, :], skip.rearrange("o (c p) -> (o c) p", p=128), writes=[b_craw])
    ps, b_ps = K.ps_misc.next()
    P.I("pe", "transpose", reads=[b_craw, K.b_const], writes=[b_ps], out=ps[:, 0:52], in_=craw, identity=K.ident_f[0:52, 0:52])
    cw = A.sb([128, 52], F32, "hcw"); b_cw = Buf("hcw")
    P.I("dve", "tensor_copy", reads=[b_ps], writes=[b_cw], out=cw, in_=ps[:, 0:52])
    w1s = A.sb([33, 64], F32, "hw1"); w2s = A.sb([64, 64], F32, "hw2"); w3s = A.sb([64, 1024], F32, "hw3"); b_w = Buf("hw")
    P.dma("sp", w1s, w1, writes=[b_w]); P.dma("sp", w2s, w2, writes=[b_w]); P.dma("sp", w3s, w3, writes=[b_w])
    par = A.sb([64, 8], F32, "hpar"); b_par = Buf("hpar")
    P.dma("sp", par[:, 0:1], freq.rearrange("o k -> k o"), writes=[b_par], allow_slow_non_contiguous=True)
    P.dma("sp", par[:, 1:2], b1.rearrange("o k -> k o"), writes=[b_par], allow_slow_non_contiguous=True)
    P.dma("sp", par[:, 2:3], b2.rearrange("o k -> k o"), writes=[b_par], allow_slow_non_contiguous=True)
    P.I("dve", "tensor_tensor", reads=[b_par], writes=[b_par], partial=True, out=par[:, 3:4], in0=par[:, 0:1], in1=par[:, 1:2], op=ALU.mult)
    P.I("dve", "tensor_tensor", reads=[b_par], writes=[b_par], partial=True, out=par[:, 4:5], in0=par[:, 0:1], in1=par[:, 2:3], op=ALU.mult)
    deltab, b_dl = load_bcast(K, deltas_d, 512, "hdelta")
    sgn, b_sgn = load_bcast(K, sgn_d, 1, "hsgn")
    m_g = A.mark()
    for G in geoms:
        A.reset(m_g)
        hyena_geom(K, G, pFM, cw, b_cw, w1s, w2s, w3s, b_w, par, b_par, deltab, b_dl, sgn, b_sgn, oT_d)
    barrier(K)
    A.reset(m0)


def hyena_geom(K, G, pFM, cw, b_cw, w1s, w2s, w3s, b_w, par, b_par, deltab, b_dl, sgn, b_sgn, oT_d):
    P, A = K.P, K.A
    n, col0, own_off, n_own = G["n"], G["col0"], G["own_off"], G["n_own"]
    nsc = n // 128
    nfb = G["CT"].shape[0]
    negt = A.sb([128, nsc], F32, "hnegt"); b_negt = Buf("hnegt")
    P.dma("sp", negt, G["negt"], writes=[b_negt], partial=False)
    hdn2T = A.sb([64, n], F32, "hhdn2"); b_h2 = Buf("hhdn2")
    m_a = A.mark()
    featsT = A.sb([33, n], F32, "hfeat"); b_ft = Buf("hfeat")
    P.dma("sp", featsT, G["featsT"], writes=[b_ft], partial=False)
    hdn1T = A.sb([64, n], F32, "hhdn1"); b_h1 = Buf("hhdn1")
    tmp = RR(sbuf_ring(K, "hsin", [64, 512], F32, 6))
    for s0 in range(0, n, 512):
        sl = min(512, n - s0)
        pz, b_pz = K.ps_misc.next()
        P.I("pe", "matmul", reads=[b_w, b_ft], writes=[b_pz], out=pz[0:64, 0:sl], lhsT=w1s, rhs=featsT[:, s0:s0 + sl], start=True, stop=True)
        hy_sin(K, hdn1T[:, s0:s0 + sl], pz[0:64, 0:sl], 64, sl, par[:, 0:1], par[:, 3:4], b_pz, b_h1, b_par, tmp)
    for s0 in range(0, n, 512):
        sl = min(512, n - s0)
        pz, b_pz = K.ps_misc.next()
        P.I("pe", "matmul", reads=[b_w, b_h1], writes=[b_pz], out=pz[0:64, 0:sl], lhsT=w2s, rhs=hdn1T[:, s0:s0 + sl], start=True, stop=True)
        hy_sin(K, hdn2T[:, s0:s0 + sl], pz[0:64, 0:sl], 64, sl, par[:, 0:1], par[:, 4:5], b_pz, b_h2, b_par, tmp)
    barrier(K)
    A.reset(m_a)
    m_h = A.mark()
    for hc in range(2):
        A.reset(m_h)
        hyena_half(K, G, hc, pFM, cw, b_cw, w3s, b_w, deltab, b_dl, sgn, b_sgn, negt, b_negt, hdn2T, b_h2, oT_d)


def hyena_half(K, G, hc, pFM, cw, b_cw, w3s, b_w, deltab, b_dl, sgn, b_sgn, negt, b_negt, hdn2T, b_h2, oT_d):
    P, A = K.P, K.A
    n, col0, own_off, n_own = G["n"], G["col0"], G["own_off"], G["n_own"]
    nsc = n // 128
    nfb = G["CT"].shape[0]
    sfilt = A.sb([128, nsc, 256], BF16, "hsf"); b_sf = Buf("hsf")
    dfilt = A.sb([128, nsc, 256], BF16, "hdf"); b_df = Buf("hdf")
    vv_tm = A.sb([128, nsc, 256], BF16, "hvv"); b_vv = Buf("hvv")
    vv_own = A.sb([128, 2, n_own], BF16, "hvvo"); b_vvo = Buf("hvvo")
    x0_own = A.sb([128, 2, n_own], BF16, "hx0o"); b_x0o = Buf("hx0o")
    Yre = A.sb([128, nfb, 256], BF16, "hYre"); b_Yre = Buf("hYre")
    Yim = A.sb([128, nfb, 256], BF16, "hYim"); b_Yim = Buf("hYim")
    m_t = A.mark()
    win_r = RR(sbuf_ring(K, "hwin", [128, 256], F32, 2))
    hw_r = RR(sbuf_ring(K, "hhw", [128, 2, 256], F32, 2))
    w3v = w3s.rearrange("k (d c) -> k d c", d=2)[:, :, hc * 256:(hc + 1) * 256]
    for T in range(nsc):
        pf_, b_pf_ = K.ps_acc.next()
        pfv = pf_.rearrange("p (d c) -> p d c", d=2)
        P.I("pe", "matmul", reads=[b_w, b_h2], writes=[b_pf_], out=pfv, lhsT=hdn2T[:, T * 128:(T + 1) * 128], rhs=w3v, start=True, stop=True)
        win, b_win = win_r.next()
        P.I("act", "activation", reads=[b_dl, b_negt], writes=[b_win], out=win, in_=deltab[:, hc * 256:(hc + 1) * 256], func=AF.Exp, scale=negt[:, T:T + 1])
        hw, b_hw = hw_r.next()
        P.I("dve", "tensor_tensor", reads=[b_pf_, b_win], writes=[b_hw], out=hw, in0=pfv, in1=win.unsqueeze(1).broadcast_to([128, 2, 256]), op=ALU.mult)
        if T == 0:
            P.I("pool", "memset", reads=[], writes=[b_hw], partial=True, ap=hw[0:1, 1, :], constant=0.0)
        P.I("pool", "tensor_tensor", reads=[b_hw], writes=[b_sf], partial=True, out=sfilt[:, T, :], in0=hw[:, 0, :], in1=hw[:, 1, :], op=ALU.add)
        P.I("dve", "tensor_tensor", reads=[b_hw], writes=[b_df], partial=True, out=dfilt[:, T, :], in0=hw[:, 0, :], in1=hw[:, 1, :], op=ALU.subtract)
    barrier(K)
    A.reset(m_t)
    pf_r = RR(sbuf_ring(K, "hpf", [128, n], F32, 2))
    u_r = RR(sbuf_ring(K, "hu", [128, n], F32, 2))
    vvb = A.sb([128, n], BF16, "hvvb"); b_vvb = Buf("hvvb")

    def conv_chunk(c):
        pf, b_pf = pf_r.next()
        u, b_u = u_r.next()
        P.dma("sp" if c % 2 == 0 else "act", pf, pFM[c * 128:(c + 1) * 128, col0:col0 + n], reads=[K.b_pFM], writes=[b_pf], partial=False)
        P.I("act", "activation", reads=[b_pf, b_cw], writes=[b_u], out=u, in_=pf, func=AF.Identity, scale=cw[:, 12 + c:13 + c], bias=cw[:, 36 + c:37 + c])
        P.I("dve", "scalar_tensor_tensor", reads=[b_pf, b_cw, b_u], writes=[b_u], partial=True,
            out=u[:, 1:n], in0=pf[:, 0:n - 1], scalar=cw[:, c:c + 1], in1=u[:, 1:n], op0=ALU.mult, op1=ALU.add)
        P.I("dve", "scalar_tensor_tensor", reads=[b_pf, b_cw, b_u], writes=[b_u], partial=True,
            out=u[:, 0:n - 1], in0=pf[:, 1:n], scalar=cw[:, 24 + c:25 + c], in1=u[:, 0:n - 1], op0=ALU.mult, op1=ALU.add)
        return u, b_u
    for j in range(2):
        cc = 2 * hc + j
        ux, b_ux = conv_chunk(4 + cc)
        uv, b_uv = conv_chunk(8 + cc)
        P.I("pool", "tensor_tensor", reads=[b_ux, b_uv], writes=[b_uv], out=uv, in0=uv, in1=ux, op=ALU.mult)
        P.I("act", "copy", reads=[b_uv], writes=[b_vvb], out=vvb, in_=uv)
        P.I("dve", "tensor_copy", reads=[b_uv], writes=[b_vvo], partial=True, out=vv_own[:, j, :], in_=uv[:, 0:n_own])
        for T0 in range(0, nsc, 8):
            nT = min(8, nsc - T0)
            ps2, b_ps2 = K.ps_tp.next()
            psb = ps2.bitcast(BF16).rearrange("p (k t) -> p k t", k=8)
            for i in range(nT):
                P.I("pe", "transpose", reads=[b_vvb, K.b_const], writes=[b_ps2], partial=(i > 0),
                    out=psb[:, i, :], in_=vvb[:, (T0 + i) * 128:(T0 + i + 1) * 128], identity=K.ident)
            P.I("dve", "tensor_copy", reads=[b_ps2], writes=[b_vv], partial=True, out=vv_tm[:, T0:T0 + nT, j * 128:(j + 1) * 128], in_=psb[:, 0:nT, :])
        u0, b_u0 = conv_chunk(cc)
        P.I("act", "copy", reads=[b_u0], writes=[b_x0o], partial=True, out=x0_own[:, j, :], in_=u0[:, 0:n_own])
    barrier(K)
    A.reset(m_t)
    cst_r = RR(sbuf_ring(K, "hcst", [128, 2, nsc, 128], BF16, 3))
    ks_r = RR(sbuf_ring(K, "hks", [128, 2, 256], F32, 2))
    tt_r = RR(sbuf_ring(K, "htt", [128, 4, 256], F32, 2))
    ps8 = RR(K.banks[0:8])
    for fb in range(nfb):
        cst, b_cst = cst_r.next()
        P.dma("sp", cst[:, 0, :, :], G["CT"][fb], writes=[b_cst], partial=False)
        P.dma("act", cst[:, 1, :, :], G["ST"][fb], writes=[b_cst])
        pa, b_pa = ps8.next(); pb, b_pb = ps8.next(); pck, b_pck = ps8.next(); pdk, b_pdk = ps8.next()
        for (po, b_po, ci, rhs, b_rhs) in ((pa, b_pa, 0, vv_tm, b_vv), (pb, b_pb, 1, vv_tm, b_vv), (pck, b_pck, 0, sfilt, b_sf), (pdk, b_pdk, 1, dfilt, b_df)):
            for k in range(nsc):
                P.I("pe", "matmul", reads=[b_cst, b_rhs], writes=[b_po], partial=(k > 0), out=po[:, 0:256], lhsT=cst[:, ci, k, :], rhs=rhs[:, k, :],
                    start=(k == 0), stop=(k == nsc - 1))
        ks, b_ks = ks_r.next()
        P.I("act", "copy", reads=[b_pck], writes=[b_ks], partial=True, out=ks[:, 0, :], in_=pck[:, 0:256])
        P.I("act", "activation", reads=[b_pdk, b_sgn], writes=[b_ks], partial=True, out=ks[:, 1, :], in_=pdk[:, 0:256], func=AF.Copy, scale=sgn[:, 0:1])
        tt, b_tt = tt_r.next()
        P.I("dve", "tensor_tensor", reads=[b_pa, b_ks], writes=[b_tt], partial=True, out=tt[:, 0, :], in0=pa[:, 0:256], in1=ks[:, 0, :], op=ALU.mult)
        P.I("dve", "tensor_tensor", reads=[b_pb, b_ks], writes=[b_tt], partial=True, out=tt[:, 1, :], in0=pb[:, 0:256], in1=ks[:, 1, :], op=ALU.mult)
        P.I("dve", "tensor_tensor", reads=[b_pa, b_ks], writes=[b_tt], partial=True, out=tt[:, 2, :], in0=pa[:, 0:256], in1=ks[:, 1, :], op=ALU.mult)
        P.I("dve", "tensor_tensor", reads=[b_pb, b_ks], writes=[b_tt], partial=True, out=tt[:, 3, :], in0=pb[:, 0:256], in1=ks[:, 0, :], op=ALU.mult)
        P.I("pool", "tensor_tensor", reads=[b_tt], writes=[b_Yre], partial=True, out=Yre[:, fb, :], in0=tt[:, 0, :], in1=tt[:, 1, :], op=ALU.subtract)
        P.I("pool", "tensor_tensor", reads=[b_tt], writes=[b_Yim], partial=True, out=Yim[:, fb, :], in0=tt[:, 2, :], in1=tt[:, 3, :], op=ALU.add)
    barrier(K)
    A.reset(m_t)
    FP = 11
    ic_r = RR(sbuf_ring(K, "hic", [128, 2, FP, 512], BF16, 3))
    yt_r = RR(sbuf_ring(K, "hyt", [128, 512], F32, 2))
    yo_r = RR(sbuf_ring(K, "hyo", [128, 512], BF16, 3))
    pieces = [(f0, min(FP, nfb - f0)) for f0 in range(0, nfb, FP)]
    for t0 in range(0, n_own, 512):
        tl = min(512, n_own - t0)
        accs = [K.ps_acc.next(), K.ps_acc.next()]
        for pi_, (f0, nf) in enumerate(pieces):
            ic, b_ic = ic_r.next()
            P.dma("sp", ic[:, 0, 0:nf, 0:tl], G["Ci"][f0 * 128:(f0 + nf) * 128, t0:t0 + tl].rearrange("(k p) t -> p k t", p=128), writes=[b_ic], partial=False)
            P.dma("act", ic[:, 1, 0:nf, 0:tl], G["Si"][f0 * 128:(f0 + nf) * 128, t0:t0 + tl].rearrange("(k p) t -> p k t", p=128), writes=[b_ic])
            for j in range(2):
                acc, b_acc = accs[j]
                for k in range(nf):
                    first = (pi_ == 0 and k == 0)
                    last = (pi_ == len(pieces) - 1 and k == nf - 1)
                    P.I("pe", "matmul", reads=[b_Yre, b_ic], writes=[b_acc], partial=(not first), out=acc[:, 0:tl], lhsT=Yre[:, f0 + k, j * 128:(j + 1) * 128],
                        rhs=ic[:, 0, k, 0:tl], start=first, stop=False)
                    P.I("pe", "matmul", reads=[b_Yim, b_ic], writes=[b_acc], partial=True, out=acc[:, 0:tl], lhsT=Yim[:, f0 + k, j * 128:(j + 1) * 128],
                        rhs=ic[:, 1, k, 0:tl], start=False, stop=last)
        for j in range(2):
            acc, b_acc = accs[j]
            cc = 2 * hc + j
            yt, b_yt = yt_r.next()
            P.I("dve", "scalar_tensor_tensor", reads=[b_vvo, b_cw, b_acc], writes=[b_yt], out=yt[:, 0:tl], in0=vv_own[:, j, t0:t0 + tl], scalar=cw[:, 48 + cc:49 + cc],
                in1=acc[:, 0:tl], op0=ALU.mult, op1=ALU.add)
            yo, b_yo = yo_r.next()
            P.I("pool", "tensor_tensor", reads=[b_yt, b_x0o], writes=[b_yo], out=yo[:, 0:tl], in0=yt[:, 0:tl], in1=x0_own[:, j, t0:t0 + tl], op=ALU.mult)
            P.dma("sp", oT_d[1024 + cc * 128:1024 + (cc + 1) * 128, own_off + t0:own_off + t0 + tl], yo[:, 0:tl], reads=[b_yo], writes=[K.b_oT_d])
    barrier(K)


from concourse.bass_utils import run_bass_kernel_spmd

I32 = mybir.dt.int32
LAYER_KEYS = [("w_ada", [D, 6 * D]), ("b_ada", [1, 6 * D]), ("w_in", [D, C_END]),
              ("mla_q_norm", [1, 448]), ("mla_kv_norm", [1, 128]), ("mla_w_uq", [448, 768]), ("mla_w_ukv", [128, 1024]),
              ("gqa_q_norm", [1, 128]), ("gqa_k_norm", [1, 128]),
              ("hy_conv_w", [3, 1536]), ("hy_conv_b", [1, 1536]), ("hy_w1", [33, 64]), ("hy_b1", [1, 64]), ("hy_w2", [64, 64]), ("hy_b2", [1, 64]),
              ("hy_w3", [64, 1024]), ("hy_freq", [1, 64]), ("hy_skip", [1, 512]),
              ("mb_conv_w", [3, 1024]), ("mb_conv_b", [1, 1024]), ("mb_a_log", [1, 16]), ("mb_dt_bias", [1, 16]), ("mb_d", [1, 8]), ("mb_norm", [1, 512]),
              ("w_mgate", [4, D, D]), ("b_mgate", [4, D]), ("w_branch", [4, 512, D]), ("w_out", [D, D]),
              ("ln1_g", [1, D]), ("ln1_b", [1, D]), ("ln2_g", [1, D]), ("ln2_b", [1, D]),
              ("w_ffn_in", [D, 2 * FFH]), ("w_ffn_out", [FFH, D])]


def run_layer(K, W, S, xload, x_own_lat, x_own_ctx, out_lat, out_ctx, need_ctx):
    phase_A(K, S["c2"], W["w_ada"], W["b_ada"], S["mod_d"])
    phase_B(K, xload, S["mod_d"], W["w_in"], S["pTM"], S["pFM"], S["hT_d"])
    phase_mla(K, S["pTM"], W["mla_q_norm"], W["mla_kv_norm"], W["mla_w_uq"], W["mla_w_ukv"], S["ropeA"], S["oT_d"], need_ctx)
    phase_gqa(K, S["pTM"], W["gqa_q_norm"], W["gqa_k_norm"], S["ropeB"], S["oT_d"], need_ctx)
    phase_hyena(K, S["pFM"], W["hy_conv_w"], W["hy_conv_b"], W["hy_w1"], W["hy_b1"], W["hy_w2"], W["hy_b2"], W["hy_w3"], W["hy_freq"], W["hy_skip"],
                S["hy_sgn"], S["hy_deltas"], S["geoms"] if need_ctx else S["geoms"][:1], S["oT_d"])
    phase_mamba(K, S["pTM"], S["pFM"], W["mb_conv_w"], W["mb_conv_b"], W["mb_a_log"], W["mb_dt_bias"], W["mb_d"], W["mb_norm"],
                S["mb_tri"], S["mb_masks"], S["cumT_d"], S["oT_d"], need_ctx)
    phase_D(K, x_own_lat, x_own_ctx, S["hT_d"], S["oT_d"], S["mod_d"], W["w_mgate"], W["b_mgate"], W["w_branch"], W["w_out"], W["ln1_g"], W["ln1_b"],
            W["w_ffn_in"], W["w_ffn_out"], W["ln2_g"], W["ln2_b"], S["r_d"], S["xmid_d"], out_lat, out_ctx, need_ctx)


def set_bufs(K):
    for nm in ("mod_d", "pTM", "pFM", "hT_d", "oT_d", "r_d", "xmid_d", "out", "cumT_d"):
        setattr(K, "b_" + nm, Buf(nm))
    K.dbg = None


def build_fused():
    nc = bass.Bass("TRN2", target_bir_lowering=False)

    def dt(n, s, d=F32, kind="ExternalInput"):
        return nc.dram_tensor(n, s, d, kind=kind).ap()
    S = {}
    x_lat = dt("x_lat", [NLAT, D]); x_ctx = dt("x_ctx", [NCTX, D]); S["c2"] = dt("c2", [2, D])
    consts = {"ident_bf": dt("ident_bf", [128, 128], BF16), "ident_f": dt("ident_f", [128, 128])}
    S["ropeA"] = dt("ropeA", [NLAT, 2, 32]); S["ropeB"] = dt("ropeB", [NLAT, 2, 64])
    S["hy_sgn"] = dt("hy_sgn", [1, 1]); S["hy_deltas"] = dt("hy_deltas", [1, 512])
    geoms = []
    for nm, n, col0, own_off, n_own in (("L", 4096, 0, 0, 2048), ("C", 256, 4096, 2048, 128)):
        Fp = (n + 1 + 127) // 128 * 128
        geoms.append({"n": n, "col0": col0, "own_off": own_off, "n_own": n_own,
                      "CT": dt(f"hy{nm}_CT", [Fp // 128, 128, n // 128, 128], BF16), "ST": dt(f"hy{nm}_ST", [Fp // 128, 128, n // 128, 128], BF16),
                      "Ci": dt(f"hy{nm}_Ci", [Fp, n_own], BF16), "Si": dt(f"hy{nm}_Si", [Fp, n_own], BF16),
                      "featsT": dt(f"hy{nm}_featsT", [33, n]), "negt": dt(f"hy{nm}_negt", [128, n // 128])})
    S["geoms"] = geoms
    S["mb_tri"] = dt("mb_tri", [3, 128, 128]); S["mb_masks"] = dt("mb_masks", [2, 4, 128, 512], BF16)
    xidx_d = dt("xidx", [128, 17], I32)
    Ws = [{k: dt(f"{k}_{l}", shp) for k, shp in LAYER_KEYS} for l in range(2)]
    S["cumT_d"] = dt("cumT_d", [16, NTOK], F32, "Internal")
    S["mod_d"] = dt("mod_d", [2, 6 * D], F32, "Internal")
    S["pTM"] = dt("pTM", [NTOK, TMW], F32, "Internal")
    S["pFM"] = dt("pFM", [FMW, NTOK], F32, "Internal")
    S["hT_d"] = dt("hT_d", [128, 16, NOWN], BF16, "Internal")
    S["oT_d"] = dt("oT_d", [2048, NOWN], BF16, "Internal")
    S["r_d"] = dt("r_d", [NOWN, D], F32, "Internal")
    S["xmid_d"] = dt("xmid_d", [NOWN, D], F32, "Internal")
    xown_d = dt("xown_d", [NOWN, D], F32, "Internal")
    xall_d = dt("xall_d", [8 * NOWN, D], F32, "Internal")
    out_lat = dt("out_lat", [OWN_LAT, D], F32, "ExternalOutput")
    K = make_ctx(nc)
    set_bufs(K)
    load_consts(K, consts)

    def xload0(kind, t, xt, b_xt):
        src = x_lat if kind == "lat" else x_ctx
        K.P.dma("sp", xt, src[t * 128:(t + 1) * 128, :], writes=[b_xt], partial=False)
    run_layer(K, Ws[0], S, xload0, x_lat, x_ctx, xown_d[0:OWN_LAT, :], xown_d[OWN_LAT:NOWN, :], True)
    b_xall = Buf("xall")
    K.P.collective("AllGather", ALU.bypass, [list(range(8))], xown_d, xall_d, reads=[K.b_out], writes=[b_xall], inc=1)
    barrier(K)
    K.P.emit()
    K1 = make_ctx(nc, prev=K)
    set_bufs(K1)
    m0 = K1.A.mark()
    xidx = K1.A.sb([128, 17], I32, "xidx"); b_xidx = Buf("xidx")
    K1.P.dma("sp", xidx, xidx_d, writes=[b_xidx], partial=False)

    def xload1(kind, t, xt, b_xt):
        if kind == "lat" and t < 16:
            K1.P.dma("sp", xt, xown_d[t * 128:(t + 1) * 128, :], writes=[b_xt], partial=False)
        elif kind == "lat":
            K1.P.gather(xt, xall_d, xidx[:, t - 16:t - 15], reads=[b_xidx], writes=[b_xt])
        elif t == 0:
            K1.P.dma("sp", xt, xown_d[OWN_LAT:NOWN, :], writes=[b_xt], partial=False)
        else:
            K1.P.gather(xt, xall_d, xidx[:, 16:17], reads=[b_xidx], writes=[b_xt])
    run_layer(K1, Ws[1], S, xload1, xown_d[0:OWN_LAT, :], xown_d[OWN_LAT:NOWN, :], out_lat, None, False)
    K1.P.finish([K1.b_out], "sp")
    K1.P.emit()
    return nc


def kernel(**inp):
    f32 = np.float32
    x = np.asarray(inp["x"], f32)
    z = np.asarray(inp["ctx"], f32)
    c = np.asarray(inp["c"], f32)
    c_ctx = np.asarray(inp["c_ctx"], f32)
    ident_bf = np.eye(128, dtype=ml_dtypes.bfloat16)
    ident_f = np.eye(128, dtype=f32)
    ropes = [rope_tables(0), rope_tables(1)]
    hyc = {"L": hyena_consts(4096, 2048), "C": hyena_consts(256, 128)}
    tri, masks = mamba_consts()
    deltas = np.abs(np.linspace(math.log(1e-2) / 1.5, math.log(1e-2) / 0.3, 512, dtype=np.float32))[None]
    nc = build_fused()
    shapes = dict(LAYER_KEYS)
    in_maps = []
    for core in range(8):
        b, half = core // 2, core % 2
        m = {"x_lat": np.ascontiguousarray(x[b] if half == 0 else x[b][::-1]),
             "x_ctx": np.ascontiguousarray(z[b] if half == 0 else z[b][::-1]),
             "c2": np.stack([c[b], c_ctx]), "ident_bf": ident_bf, "ident_f": ident_f,
             "ropeA": ropes[half][0], "ropeB": ropes[half][1],
             "hy_sgn": np.full((1, 1), 1.0 if half == 0 else -1.0, f32), "hy_deltas": deltas, "mb_tri": tri, "mb_masks": masks}
        for nm in ("L", "C"):
            for k_, v_ in hyc[nm].items():
                m[f"hy{nm}_{k_}"] = v_
        partner = core ^ 1
        p = np.arange(128)
        idx = np.empty((128, 17), np.int32)
        for j in range(16):
            idx[:, j] = partner * NOWN + (OWN_LAT - 1 - 128 * j - p)
        idx[:, 16] = partner * NOWN + OWN_LAT + (OWN_CTX - 1 - p)
        m["xidx"] = idx
        for l in range(2):
            for k, shp in LAYER_KEYS:
                a = np.asarray(inp[k][l], f32)
                if half == 1:
                    if k == "w_in":
                        a = np.concatenate([a[:, :C_DT], a[:, C_DT + 8:C_DT + 16], a[:, C_DT:C_DT + 8]], 1)
                    elif k in ("hy_conv_w", "mb_conv_w", "mb_a_log", "mb_dt_bias"):
                        a = a[::-1]
                m[f"{k}_{l}"] = np.ascontiguousarray(a).reshape(shp)
        in_maps.append(m)
    res = run_bass_kernel_spmd(nc, in_maps, core_ids=list(range(8)))
    out = np.empty_like(x)
    for core in range(8):
        b, half = core // 2, core % 2
        ol = res.results[core]["out_lat"]
        if half == 0:
            out[b, :OWN_LAT] = ol
        else:
            out[b, OWN_LAT:] = ol[::-1]
    return out.astype(np.float32)
```

```python
import numpy as np
import concourse.bass as bass
import concourse.mybir as mybir

F32 = mybir.dt.float32
BF16 = mybir.dt.bfloat16
AF = mybir.ActivationFunctionType
ALU = mybir.AluOpType
AX = mybir.AxisListType

N_DMA_SEMS = 8


class Buf:
    __slots__ = ("name", "w", "r")

    def __init__(self, name):
        self.name = name
        self.w = {}
        self.r = {}


class Prog:
    _n_prog = 0

    def __init__(self, nc):
        self.nc = nc
        self.pid = Prog._n_prog
        Prog._n_prog += 1
        self.eng_handles = {"pe": nc.tensor, "dve": nc.vector, "act": nc.scalar,
                            "pool": nc.gpsimd, "sp": nc.sync}
        self.ops = {k: [] for k in self.eng_handles}
        self.seq = {k: 0 for k in self.eng_handles}
        self.known = {k: {} for k in self.eng_handles}
        self.sem_names = []
        for k in self.eng_handles:
            self.sem_names.append(("c", k))
        self.dma_rr = {}
        self.dma_cnt = {}
        self.dma_last = {}
        for q in ("sp", "pool", "act"):
            self.dma_rr[q] = 0
            for j in range(N_DMA_SEMS):
                key = ("d", q, j)
                self.sem_names.append(key)
                self.dma_cnt[key] = 0
                self.dma_last[key] = None
        self.sems = {}
        self.n_wait = 0
        self.n_ops = 0

    def _need(self, eng, reads, writes):
        deps = {}

        def add(d, war=False):
            for key, (val, snap) in d.items():
                if key == ("c", eng) and (war or eng == "pe"):
                    continue
                if key not in deps or deps[key][0] < val:
                    deps[key] = (val, snap)
        for b in reads:
            add(b.w)
        for b in writes:
            add(b.w)
            add(b.r, war=True)
        kn = self.known[eng]
        waits = []
        for key, (val, snap) in deps.items():
            if kn.get(key, 0) >= val:
                continue
            waits.append((key, val))
            kn[key] = val
            for k2, v2 in snap.items():
                if kn.get(k2, 0) < v2:
                    kn[k2] = v2
        return waits

    def _record(self, eng, key, val, reads, writes, partial):
        ev = (val, dict(self.known[eng]))
        for b in reads:
            old = b.r.get(key)
            if old is None or old[0] < val:
                b.r[key] = ev
        for b in writes:
            if partial:
                b.w[key] = ev
            else:
                b.w = {key: ev}
                b.r = {}

    def op(self, eng, fn, reads=(), writes=(), partial=False):
        waits = self._need(eng, reads, writes)
        self.seq[eng] += 1
        val = self.seq[eng]
        key = ("c", eng)
        self.ops[eng].append((waits, fn, key, 1))
        self._record(eng, key, val, reads, writes, partial)
        self.n_ops += 1
        self.n_wait += len(waits)

    def I(self, eng, name, reads=(), writes=(), partial=False, **kw):
        self.op(eng, (lambda e, name=name, kw=kw: getattr(e, name)(**kw)), reads=reads, writes=writes, partial=partial)

    def dma(self, q, out, in_, reads=(), writes=(), partial=True, **kw):
        j = self.dma_rr[q]
        self.dma_rr[q] = (j + 1) % N_DMA_SEMS
        key = ("d", q, j)
        waits = self._need(q, reads, writes)
        prev = self.dma_cnt[key]
        if prev > 0 and self.known[q].get(key, 0) < prev:
            waits.append((key, prev))
            self.known[q][key] = prev
        val = prev + 16
        self.dma_cnt[key] = val

        def fn(e, out=out, in_=in_, kw=kw):
            return e.dma_start(out=out, in_=in_, **kw)
        self.ops[q].append((waits, fn, key, 16))
        self._record(q, key, val, reads, writes, partial)
        self.n_ops += 1
        self.n_wait += len(waits)

    def gather(self, out, in_dram, idx_ap, reads=(), writes=()):
        q = "pool"
        j = self.dma_rr[q]
        self.dma_rr[q] = (j + 1) % N_DMA_SEMS
        key = ("d", q, j)
        waits = self._need(q, reads, writes)
        prev = self.dma_cnt[key]
        if prev > 0 and self.known[q].get(key, 0) < prev:
            waits.append((key, prev))
            self.known[q][key] = prev
        val = prev + 16
        self.dma_cnt[key] = val

        def fn(e):
            return e.indirect_dma_start(out=out, out_offset=None, in_=in_dram, in_offset=bass.IndirectOffsetOnAxis(ap=idx_ap, axis=0))
        self.ops[q].append((waits, fn, key, 16))
        self._record(q, key, val, reads, writes, False)
        self.n_ops += 1

    def collective(self, kind, alu, groups, in_ap, out_ap, reads=(), writes=(), inc=16):
        q = "pool"
        key = ("d", "cc", 0)
        if key not in self.dma_cnt:
            self.dma_cnt[key] = 0
            self.sem_names.append(key)
        waits = self._need(q, reads, writes)
        prev = self.dma_cnt[key]
        if prev > 0 and self.known[q].get(key, 0) < prev:
            waits.append((key, prev))
            self.known[q][key] = prev
        val = prev + inc
        self.dma_cnt[key] = val

        def fn(e):
            return e.collective_compute(kind, alu, replica_groups=groups, ins=[in_ap], outs=[out_ap])
        self.ops[q].append((waits, fn, key, inc))
        self._record(q, key, val, reads, writes, False)
        self.n_ops += 1

    def finish(self, bufs, eng="sp"):
        waits = self._need(eng, bufs, ())
        self.ops[eng].append((waits, None, None, 0))

    def emit(self):
        nc = self.nc
        from contextlib import ExitStack
        with ExitStack() as st:
            used = set()
            for e, lst in self.ops.items():
                for waits, fn, key, inc in lst:
                    if key is not None:
                        used.add(key)
                    for k, _ in waits:
                        used.add(k)
            for key in self.sem_names:
                if key in used:
                    self.sems[key] = nc.alloc_semaphore(name=f"s{self.pid}_" + "_".join(map(str, key)))
            block = st.enter_context(nc.Block())
            for ename in ("sp", "act", "pe", "dve", "pool"):
                lst = self.ops[ename]
                if not lst:
                    continue

                def body(e, lst=lst):
                    for waits, fn, key, inc in lst:
                        for k, v in waits:
                            e.wait_ge(self.sems[k], v)
                        if fn is not None:
                            fn(e).then_inc(self.sems[key], inc)
                reg = {"sp": block.sync, "act": block.scalar, "pe": block.tensor,
                       "dve": block.vector, "pool": block.gpsimd}[ename]
                reg(body)


class Ring:
    def __init__(self, nc, name, shape, dtype, n, psum=False):
        self.items = []
        for i in range(n):
            if psum:
                t = nc.alloc_psum_tensor(f"{name}{i}", list(shape), dtype)
            else:
                t = nc.alloc_sbuf_tensor(f"{name}{i}", list(shape), dtype)
            self.items.append((t.ap(), Buf(f"{name}{i}")))
        self.i = 0

    def next(self):
        it = self.items[self.i]
        self.i = (self.i + 1) % len(self.items)
        return it


import math
import numpy as np
import ml_dtypes
import concourse.bass as bass
import concourse.mybir as mybir

D = 2048
NLAT = 4096
NCTX = 256
NTOK = NLAT + NCTX
OWN_LAT = 2048
OWN_CTX = 128
NOWN = OWN_LAT + OWN_CTX
FFH = 5632
EPS = 1e-6
ALPHA = (2 * 2) ** 0.25
C_A, C_B, C_C, C_DZ, C_DX, C_DT, C_END = 0, 640, 1664, 3200, 3712, 4736, 4752
TMW = 2192
FMW = 2560


class Arena:
    def __init__(self, nc, limit=206 * 1024):
        self.nc = nc
        self.off = 0
        self.limit = limit
        self.n = 0
        self.base = nc.alloc_sbuf_tensor("arena", [128, limit // 4], F32).ap()

    def sb(self, shape, dtype, name=None):
        esz = mybir.dt.size(dtype)
        nel = int(np.prod(shape[1:]))
        nbytes = (nel * esz + 63) // 64 * 64
        assert self.off + nbytes <= self.limit, f"SBUF arena overflow {self.off}+{nbytes}"
        a = self.base[0:shape[0], self.off // 4:(self.off + nbytes) // 4]
        if dtype != F32:
            a = a.bitcast(dtype)
        a = a[:, 0:nel]
        if len(shape) == 3:
            a = a.rearrange("p (a b) -> p a b", a=shape[1])
        elif len(shape) == 4:
            a = a.rearrange("p (a b c) -> p a b c", a=shape[1], b=shape[2])
        self.off += nbytes
        return a

    def mark(self):
        return self.off

    def reset(self, m=0):
        self.off = m


class Ctx:
    pass


def sbuf_ring(K, name, shape, dtype, n):
    return [(K.A.sb(shape, dtype, f"{name}{i}"), Buf(f"{name}{i}")) for i in range(n)]


class RR:
    def __init__(self, items):
        self.items = items
        self.i = 0

    def next(self):
        it = self.items[self.i]
        self.i = (self.i + 1) % len(self.items)
        return it


def barrier(K):
    P = K.P
    evs = {}
    for e in P.eng_handles:
        if P.seq[e] > 0:
            evs[("c", e)] = (P.seq[e], {})
    for key, cnt in P.dma_cnt.items():
        if cnt > 0:
            evs[key] = (cnt, {})
    b = Buf("barrier")
    b.w = evs
    for e in ("sp", "act", "pe", "dve", "pool"):
        waits = P._need(e, [b], ())
        if waits:
            P.ops[e].append((waits, None, None, 0))


def evac_copy(K, i, out, in_, reads, writes, partial=False):
    if i % 2 == 0:
        K.P.op("dve", lambda e: e.tensor_copy(out=out, in_=in_), reads=reads, writes=writes, partial=partial)
    else:
        K.P.op("act", lambda e: e.copy(out=out, in_=in_), reads=reads, writes=writes, partial=partial)


def copy_on(K, eng, out, in_, reads, writes, partial=True):
    if eng == "act":
        K.P.op("act", lambda e: e.copy(out=out, in_=in_), reads=reads, writes=writes, partial=partial)
    else:
        K.P.op("dve", lambda e: e.tensor_copy(out=out, in_=in_), reads=reads, writes=writes, partial=partial)


def ln_stats(K, x_ap, bx, width, tmp):
    P = K.P
    st, b_st = tmp["st"].next()
    mv, b_mv = tmp["mv"].next()
    nch = width // 512
    for c in range(nch):
        P.op("dve", lambda e, c=c: e.bn_stats(out=st[:, c, :], in_=x_ap[:, c * 512:(c + 1) * 512]),
             reads=[bx], writes=[b_st], partial=(c > 0))
    P.op("dve", lambda e: e.bn_aggr(out=mv[:, 0:2], in_=st[:, 0:nch, :]), reads=[b_st], writes=[b_mv])
    P.op("dve", lambda e: e.tensor_scalar_add(out=mv[:, 3:4], in0=mv[:, 1:2], scalar1=EPS), reads=[b_mv], writes=[b_mv])
    P.op("act", lambda e: e.sqrt(out=mv[:, 3:4], in_=mv[:, 3:4]), reads=[b_mv], writes=[b_mv])
    P.op("dve", lambda e: e.reciprocal(out=mv[:, 2:3], in_=mv[:, 3:4]), reads=[b_mv], writes=[b_mv])
    return mv, b_mv


def load_modT(K, mod_d, r, name):
    P, A = K.P, K.A
    raw = A.sb([96, 128], F32, name + "raw")
    b_raw = Buf(name + "raw")
    P.dma("sp", raw, mod_d[r:r + 1, :].rearrange("o (j p) -> (o j) p", p=128), writes=[b_raw], partial=False)
    ps, b_ps = K.ps_misc.next()
    P.op("pe", lambda e: e.transpose(out=ps[:, 0:96], in_=raw, identity=K.ident_f[0:96, 0:96]),
         reads=[b_raw, K.b_const], writes=[b_ps])
    mt = A.sb([128, 96], F32, name)
    b_mt = Buf(name)
    P.op("dve", lambda e: e.tensor_copy(out=mt, in_=ps[:, 0:96]), reads=[b_ps], writes=[b_mt])
    P.op("dve", lambda e: e.tensor_scalar_add(out=mt[:, 16:32], in0=mt[:, 16:32], scalar1=1.0), reads=[b_mt], writes=[b_mt])
    P.op("dve", lambda e: e.tensor_scalar_add(out=mt[:, 64:80], in0=mt[:, 64:80], scalar1=1.0), reads=[b_mt], writes=[b_mt])
    return mt, b_mt


def build_hT_tile(K, loader, modT, b_modT, soff, hT_dst, b_hT, tmp):
    xt, b_xt = tmp["xt"].next()
    loader(xt, b_xt)
    hT_from_sbuf(K, xt, b_xt, modT, b_modT, soff, hT_dst, b_hT, tmp)


def hT_from_sbuf(K, xt, b_xt, modT, b_modT, soff, hT_dst, b_hT, tmp):
    P = K.P
    mv, b_mv = ln_stats(K, xt, b_xt, D, tmp)
    xs, b_xs = tmp["xs"].next()
    P.op("dve", lambda e: e.tensor_scalar(out=xs, in0=xt, scalar1=mv[:, 0:1], scalar2=mv[:, 2:3],
                                          op0=ALU.subtract, op1=ALU.mult), reads=[b_xt, b_mv], writes=[b_xs])
    for half in range(2):
        ps, b_ps = K.ps_tp.next()
        psb = ps.bitcast(BF16).rearrange("p (k t) -> p k t", k=8)
        for k in range(8):
            kk = half * 8 + k
            P.op("pe", lambda e, k=k, kk=kk, psb=psb: e.transpose(out=psb[:, k, :], in_=xs[:, kk * 128:(kk + 1) * 128], identity=K.ident),
                 reads=[b_xs, K.b_const], writes=[b_ps], partial=(k > 0))
        t1, b_t1 = tmp["t1"].next()
        sc_b = modT[:, soff + 16 + half * 8: soff + 16 + half * 8 + 8].unsqueeze(2).broadcast_to([128, 8, 128])
        sh_b = modT[:, soff + half * 8: soff + half * 8 + 8].unsqueeze(2).broadcast_to([128, 8, 128])
        P.op("dve", lambda e, psb=psb, t1=t1, sc_b=sc_b: e.tensor_tensor(out=t1, in0=psb, in1=sc_b, op=ALU.mult),
             reads=[b_ps, b_modT], writes=[b_t1])
        P.op("dve", lambda e, t1=t1, sh_b=sh_b, half=half: e.tensor_tensor(out=hT_dst[:, half * 8:(half + 1) * 8, :], in0=t1, in1=sh_b, op=ALU.add),
             reads=[b_t1, b_modT], writes=[b_hT], partial=True)


def load_w_chunk(K, ring, w_ap, k_rows, c0, ncols, nsplit=1):
    wb, b_wb = ring.next()
    kc = k_rows // 128
    step = (kc + nsplit - 1) // nsplit
    first = True
    for k0 in range(0, kc, step):
        k1 = min(kc, k0 + step)
        K.P.dma("pool", wb[:, k0:k1, 0:ncols], w_ap[k0 * 128:k1 * 128, c0:c0 + ncols].rearrange("(k p) n -> p k n", p=128),
                writes=[b_wb], partial=(not first))
        first = False
    return wb, b_wb


def phase_A(K, c2_d, w_ada, b_ada, mod_d):
    P, A = K.P, K.A
    m0 = A.mark()
    craw = A.sb([32, 128], F32, "craw")
    b_craw = Buf("craw")
    P.dma("sp", craw, c2_d.rearrange("r (k p) -> (r k) p", p=128), writes=[b_craw], partial=False)
    csil = A.sb([32, 128], F32, "csil")
    b_csil = Buf("csil")
    P.op("act", lambda e: e.activation(out=csil, in_=craw, func=AF.Silu), reads=[b_craw], writes=[b_csil])
    ps, b_ps = K.ps_misc.next()
    P.op("pe", lambda e: e.transpose(out=ps[:, 0:32], in_=csil, identity=K.ident_f[0:32, 0:32]),
         reads=[b_csil, K.b_const], writes=[b_ps])
    cT = A.sb([128, 32], F32, "cT")
    b_cT = Buf("cT")
    P.op("dve", lambda e: e.tensor_copy(out=cT, in_=ps[:, 0:32]), reads=[b_ps], writes=[b_cT])
    bias = A.sb([2, 12288], F32, "bada")
    b_bias = Buf("bada")
    P.dma("sp", bias, b_ada.broadcast_to([2, 12288]) if False else b_ada.partition_broadcast(2), writes=[b_bias], partial=False)
    modsb = A.sb([2, 12288], F32, "modsb")
    b_modsb = Buf("modsb")
    wr = RR(sbuf_ring(K, "wada", [128, 16, 512], F32, 2))
    for n in range(24):
        wch, b_wch = wr.next()
        P.dma("sp" if n % 2 == 0 else "act", wch, w_ada[:, n * 512:(n + 1) * 512].rearrange("(k p) n -> p k n", p=128),
              writes=[b_wch], partial=False)
        po, b_po = K.ps_acc.next()
        for k in range(16):
            P.op("pe", lambda e, k=k, po=po, wch=wch: e.matmul(po[0:2, :], lhsT=cT[:, k::16], rhs=wch[:, k, :], start=(k == 0), stop=(k == 15)),
                 reads=[b_cT, b_wch], writes=[b_po], partial=(k > 0))
        P.op("dve", lambda e, n=n, po=po: e.tensor_tensor(out=modsb[:, n * 512:(n + 1) * 512], in0=po[0:2, :], in1=bias[:, n * 512:(n + 1) * 512], op=ALU.add),
             reads=[b_po, b_bias], writes=[b_modsb], partial=True)
    P.dma("sp", mod_d, modsb, reads=[b_modsb], writes=[K.b_mod_d], partial=False)
    barrier(K)
    A.reset(m0)


def phase_B(K, xload, mod_d, w_in, pTM, pFM, hT_d):
    P, A = K.P, K.A
    m0 = A.mark()
    modT_l, b_ml = load_modT(K, mod_d, 0, "modTl")
    modT_c, b_mc = load_modT(K, mod_d, 1, "modTc")
    tmp = {
        "st": RR(sbuf_ring(K, "bst", [128, 4, 6], F32, 2)),
        "mv": RR(sbuf_ring(K, "bmv", [128, 4], F32, 2)),
        "xt": RR(sbuf_ring(K, "xt", [128, D], F32, 2)),
        "xs": RR(sbuf_ring(K, "xs", [128, D], BF16, 2)),
        "t1": RR(sbuf_ring(K, "t1", [128, 8, 128], F32, 2)),
    }
    hT = A.sb([128, 16, 2048], BF16, "hT")
    wring = RR(sbuf_ring(K, "wch", [128, 16, 512], BF16, 2))
    ostg = RR(sbuf_ring(K, "ostg", [128, 512], F32, 4))
    groups = [("lat", 0, 16), ("lat", 16, 16), ("ctx", 0, 2)]
    tm_chunks = [(C_A, 512, 0), (C_A + 512, 512, 512), (C_A + 1024, 512, 1024), (C_A + 1536, 128, 1536),
                 (C_DZ, 512, 1664), (C_DT, 16, 2176)]
    fm_chunks = [(C_C, 512, 0), (C_C + 512, 512, 512), (C_C + 1024, 512, 1024), (C_DX, 512, 1536), (C_DX + 512, 512, 2048)]
    ev = 0
    for kind, t0, nt in groups:
        b_hT = [Buf(f"hT{t}") for t in range(nt)]
        for t in range(nt):
            mt, bm = (modT_l, b_ml) if kind == "lat" else (modT_c, b_mc)
            build_hT_tile(K, (lambda xt, b_xt, kind=kind, tt=t0 + t: xload(kind, tt, xt, b_xt)), mt, bm, 0,
                          hT[:, :, t * 128:(t + 1) * 128], b_hT[t], tmp)
        tok0 = (t0 * 128) if kind == "lat" else (NLAT + t0 * 128)
        ntok = nt * 128
        if kind == "lat" and t0 == 0:
            P.dma("sp", hT_d[:, :, 0:OWN_LAT], hT[:, :, 0:OWN_LAT], reads=b_hT, writes=[K.b_hT_d])
        if kind == "ctx":
            P.dma("sp", hT_d[:, :, OWN_LAT:OWN_LAT + OWN_CTX], hT[:, :, 0:OWN_CTX], reads=b_hT, writes=[K.b_hT_d])
        tm_list = tm_chunks if not (kind == "lat" and t0 == 16) else [(C_A + 448, 192, 448), (C_B + 512, 512, 1152), (C_DT, 16, 2176)]
        for (c0, ncols, dcol) in tm_list:
            wb, b_wb = load_w_chunk(K, wring, w_in, D, c0, ncols)
            for t in range(nt):
                po, b_po = K.ps_acc.next()
                for k in range(16):
                    P.op("pe", lambda e, k=k, po=po, wb=wb, t=t, ncols=ncols: e.matmul(po[:, 0:ncols], lhsT=hT[:, k, t * 128:(t + 1) * 128], rhs=wb[:, k, 0:ncols],
                                                                                 start=(k == 0), stop=(k == 15)),
                         reads=[b_hT[t], b_wb], writes=[b_po], partial=(k > 0))
                so, b_so = ostg.next()
                evac_copy(K, ev, so[:, 0:ncols], po[:, 0:ncols], [b_po], [b_so]); ev += 1
                P.dma("sp", pTM[tok0 + t * 128: tok0 + (t + 1) * 128, dcol:dcol + ncols], so[:, 0:ncols], reads=[b_so], writes=[K.b_pTM])
        for (c0, ncols, drow) in fm_chunks:
            wb, b_wb = load_w_chunk(K, wring, w_in, D, c0, ncols)
            for j in range(ncols // 128):
                for s0 in range(0, ntok, 512):
                    sl = min(512, ntok - s0)
                    po, b_po = K.ps_acc.next()
                    tiles = list(range(s0 // 128, (s0 + sl) // 128))
                    for k in range(16):
                        P.op("pe", lambda e, k=k, po=po, wb=wb, j=j, s0=s0, sl=sl: e.matmul(po[:, 0:sl], lhsT=wb[:, k, j * 128:(j + 1) * 128], rhs=hT[:, k, s0:s0 + sl],
                                                                                   start=(k == 0), stop=(k == 15)),
                             reads=[b_hT[t] for t in tiles] + [b_wb], writes=[b_po], partial=(k > 0))
                    so, b_so = ostg.next()
                    evac_copy(K, ev, so[:, 0:sl], po[:, 0:sl], [b_po], [b_so]); ev += 1
                    P.dma("sp", pFM[drow + j * 128: drow + (j + 1) * 128, tok0 + s0: tok0 + s0 + sl], so[:, 0:sl], reads=[b_so], writes=[K.b_pFM])
    barrier(K)
    A.reset(m0)


def make_ctx(nc, prev=None):
    K = Ctx()
    K.nc = nc
    K.P = Prog(nc)
    if prev is not None:
        K.A = prev.A
        K.ident, K.ident_f, K.b_const = prev.ident, prev.ident_f, Buf("const")
        banks = [(ap, Buf(f"psb{i}")) for i, (ap, _) in enumerate(prev.banks)]
    else:
        K.A = Arena(nc)
        banks = []
        for i in range(8):
            t = nc.alloc_psum_tensor(f"psb{i}", [128, 512], F32)
            banks.append((t.ap(), Buf(f"psb{i}")))
    K.banks = banks
    K.ps_acc = RR(banks[0:4])
    K.ps_tp = RR(banks[4:6])
    K.ps_misc = RR(banks[6:8])
    return K


def load_consts(K, consts):
    P, A = K.P, K.A
    K.b_const = Buf("const")
    K.ident = A.sb([128, 128], BF16, "ident")
    K.ident_f = A.sb([128, 128], F32, "identf")
    P.dma("sp", K.ident, consts["ident_bf"], writes=[K.b_const])
    P.dma("sp", K.ident_f, consts["ident_f"], writes=[K.b_const])


def rms_rstd(K, ss, b_ss, n, width):
    P = K.P
    P.op("dve", lambda e: e.tensor_scalar(out=ss[:, 0:n], in0=ss[:, 0:n], scalar1=1.0 / width, scalar2=EPS, op0=ALU.mult, op1=ALU.add),
         reads=[b_ss], writes=[b_ss])
    P.op("act", lambda e: e.sqrt(out=ss[:, 0:n], in_=ss[:, 0:n]), reads=[b_ss], writes=[b_ss])
    P.op("dve", lambda e: e.reciprocal(out=ss[:, 0:n], in_=ss[:, 0:n]), reads=[b_ss], writes=[b_ss])


def rope_apply(K, src, dst, cs, b_src, b_dst, b_cs, nh, npair, tmpr):
    P = K.P
    s4 = src.rearrange("p h (i two) -> p h i two", two=2)
    d4 = dst.rearrange("p h (i two) -> p h i two", two=2)
    x0, x1 = s4[:, :, :, 0], s4[:, :, :, 1]
    cb = cs[:, 0:1, :].broadcast_to([128, nh, npair])
    sb_ = cs[:, 1:2, :].broadcast_to([128, nh, npair])
    (ta, b_ta) = tmpr.next()
    (tb, b_tb) = tmpr.next()
    ta = ta[:, 0:nh, 0:npair]; tb = tb[:, 0:nh, 0:npair]
    P.op("dve", lambda e: e.tensor_tensor(out=ta, in0=x0, in1=cb, op=ALU.mult), reads=[b_src, b_cs], writes=[b_ta])
    P.op("pool", lambda e: e.tensor_tensor(out=tb, in0=x1, in1=sb_, op=ALU.mult), reads=[b_src, b_cs], writes=[b_tb])
    P.op("dve", lambda e: e.tensor_tensor(out=d4[:, :, :, 0], in0=ta, in1=tb, op=ALU.subtract), reads=[b_ta, b_tb], writes=[b_dst], partial=True)
    (tc, b_tc) = tmpr.next()
    (td, b_td) = tmpr.next()
    tc = tc[:, 0:nh, 0:npair]; td = td[:, 0:nh, 0:npair]
    P.op("dve", lambda e: e.tensor_tensor(out=tc, in0=x0, in1=sb_, op=ALU.mult), reads=[b_src, b_cs], writes=[b_tc])
    P.op("pool", lambda e: e.tensor_tensor(out=td, in0=x1, in1=cb, op=ALU.mult), reads=[b_src, b_cs], writes=[b_td])
    P.op("dve", lambda e: e.tensor_tensor(out=d4[:, :, :, 1], in0=tc, in1=td, op=ALU.add), reads=[b_tc, b_td], writes=[b_dst], partial=True)


def attn_core(K, heads, V_all, b_V, dv, scale, o_tm, b_otm, need_ctx):
    P, A = K.P, K.A
    SKEW = 2
    pt_ring = RR(sbuf_ring(K, "pt", [128, 512], BF16, SKEW + 2))
    rc_ring = RR(sbuf_ring(K, "rc", [128, 1], F32, 4))
    obanks = K.banks[4:8]
    qgroups = [(q0, 512, list(range(34))) for q0 in range(0, OWN_LAT, 512)]
    if need_ctx:
        qgroups.append((OWN_LAT, 128, [32, 33]))
    units = []
    for h, (kvi, parts) in enumerate(heads):
        for (q0, nq, kbs) in qgroups:
            for bi, kb in enumerate(kbs):
                units.append({"h": h, "kvi": kvi, "parts": parts, "q0": q0, "nq": nq, "kb": kb,
                              "first": bi == 0, "last": bi == len(kbs) - 1})

    def stage1(u):
        nq, q0, kb = u["nq"], u["q0"], u["kb"]
        ps, b_ps = K.ps_acc.next()
        np_ = len(u["parts"])
        for pi, (kT, qT, bk, bq) in enumerate(u["parts"]):
            P.I("pe", "matmul", reads=[bk, bq], writes=[b_ps], partial=(pi > 0), out=ps[:, 0:nq], lhsT=kT[:, kb * 128:(kb + 1) * 128],
                rhs=qT[:, q0:q0 + nq], start=(pi == 0), stop=(pi == np_ - 1))
        pt, b_pt = pt_ring.next()
        P.I("act", "activation", reads=[b_ps], writes=[b_pt], out=pt[:, 0:nq], in_=ps[:, 0:nq], func=AF.Exp, scale=scale)
        u["pt"], u["b_pt"] = pt, b_pt

    def stage2(u):
        nq, q0, kb, h = u["nq"], u["q0"], u["kb"], u["h"]
        pt, b_pt = u["pt"], u["b_pt"]
        for qs in range(nq // 128):
            ob, b_ob = obanks[qs]
            P.I("pe", "matmul", reads=[b_pt, b_V], writes=[b_ob], partial=(not u["first"]), out=ob[:, 0:dv + 1], lhsT=pt[:, qs * 128:(qs + 1) * 128],
                rhs=V_all[:, kb, u["kvi"], 0:dv + 1], start=u["first"], stop=u["last"])
        if u["last"]:
            for qs in range(nq // 128):
                ob, b_ob = obanks[qs]
                rc, b_rc = rc_ring.next()
                ot = q0 // 128 + qs
                P.I("dve", "reciprocal", reads=[b_ob], writes=[b_rc], out=rc, in_=ob[:, dv:dv + 1])
                P.I("dve", "tensor_scalar_mul", reads=[b_ob, b_rc], writes=[b_otm], partial=True, out=o_tm[:, ot, h * dv:(h + 1) * dv], in0=ob[:, 0:dv], scalar1=rc[:, 0:1])
    n = len(units)
    for i in range(n + SKEW):
        if i < n:
            stage1(units[i])
        if i >= SKEW:
            stage2(units[i - SKEW])


def store_oT(K, o_tm, b_otm, oT_d, branch, n_own_tiles):
    P = K.P
    stg = RR(sbuf_ring(K, "ostg", [128, 4, 128], BF16, 3))
    for ot in range(n_own_tiles):
        ps, b_ps = K.ps_tp.next()
        psb = ps.bitcast(BF16).rearrange("p (k t) -> p k t", k=8)
        for h in range(4):
            P.op("pe", lambda e, psb=psb, h=h, ot=ot: e.transpose(out=psb[:, h, :], in_=o_tm[:, ot, h * 128:(h + 1) * 128], identity=K.ident),
                 reads=[b_otm, K.b_const], writes=[b_ps], partial=(h > 0))
        so, b_so = stg.next()
        evac_copy(K, ot, so, psb[:, 0:4, :], [b_ps], [b_so])
        P.dma("sp", oT_d[branch * 512:(branch + 1) * 512, ot * 128:(ot + 1) * 128].rearrange("(h p) t -> p h t", p=128), so,
              reads=[b_so], writes=[K.b_oT_d])


def own_tiles(need_ctx):
    lst = [(t, t) for t in range(16)]
    if need_ctx:
        lst.append((32, 16))
    return lst


def phase_mla(K, pTM, q_norm, kv_norm, w_uq, w_ukv, ropeA, oT_d, need_ctx):
    P, A = K.P, K.A
    m0 = A.mark()
    NOWN = OWN_LAT + OWN_CTX
    knT = A.sb([128, 4, NTOK], BF16, "knT"); b_knT = Buf("knT")
    krT = A.sb([64, NTOK], BF16, "krT"); b_krT = Buf("krT")
    V_all = A.sb([128, 34, 4, 136], BF16, "Vall"); b_V = Buf("Vall")
    qnT = A.sb([128, 4, NOWN], BF16, "qnT"); b_qnT = Buf("qnT")
    qrT = A.sb([64, 4, NOWN], BF16, "qrT"); b_qrT = Buf("qrT")
    o_tm = A.sb([128, 17, 512], BF16, "otm"); b_otm = Buf("otm")
    m1 = A.mark()
    ckvT = A.sb([128, NTOK], BF16, "ckvT"); b_ckvT = Buf("ckvT")
    cqT = A.sb([128, 4, NOWN], BF16, "cqT"); b_cqT = Buf("cqT")
    wuq = A.sb([128, 4, 768], BF16, "wuq"); b_wuq = Buf("wuq")
    wukv = A.sb([128, 1024], BF16, "wukv"); b_wukv = Buf("wukv")
    gq = A.sb([128, 448], F32, "gq"); gkv = A.sb([128, 128], F32, "gkv"); b_g = Buf("gains")
    P.dma("pool", wuq[:, 0:3, :], w_uq[0:384, :].rearrange("(k p) n -> p k n", p=128), writes=[b_wuq])
    P.dma("pool", wuq[0:64, 3, :], w_uq[384:448, :], writes=[b_wuq])
    P.dma("pool", wukv, w_ukv, writes=[b_wukv])
    P.dma("sp", gq, q_norm.partition_broadcast(128), writes=[b_g])
    P.dma("sp", gkv, kv_norm.partition_broadcast(128), writes=[b_g])
    P.op("pool", lambda e: e.memset(V_all[:, :, :, 128:129], 1.0), writes=[b_V], partial=True)
    pa_r = RR(sbuf_ring(K, "pa", [128, 640], F32, 2))
    cs_r = RR(sbuf_ring(K, "csA", [128, 2, 32], F32, 2))
    sq_r = RR(sbuf_ring(K, "sq", [128, 448], F32, 2))
    ss_r = RR(sbuf_ring(K, "ss", [128, 2], F32, 4))
    nb_r = RR(sbuf_ring(K, "nb", [128, 640], BF16, 2))
    tmpr = RR(sbuf_ring(K, "rt", [128, 4, 32], F32, 8))
    owned = dict(own_tiles(need_ctx))
    for t in range(34):
        is_lat = t < 32
        pa, b_pa = pa_r.next()
        P.dma("sp", pa, pTM[t * 128:(t + 1) * 128, 0:640], writes=[b_pa], partial=False)
        nb, b_nb = nb_r.next()
        if is_lat:
            cs, b_cs = cs_r.next()
            P.dma("sp", cs, ropeA[t * 128:(t + 1) * 128], writes=[b_cs], partial=False)
        sq, b_sq = sq_r.next()
        ss, b_ss = ss_r.next()
        P.op("pool", lambda e, sq=sq, pa=pa: e.tensor_tensor(out=sq[:, 0:128], in0=pa[:, 448:576], in1=pa[:, 448:576], op=ALU.mult), reads=[b_pa], writes=[b_sq])
        P.op("dve", lambda e, sq=sq, ss=ss: e.reduce_sum(out=ss[:, 0:1], in_=sq[:, 0:128], axis=AX.X), reads=[b_sq], writes=[b_ss])
        rms_rstd(K, ss, b_ss, 1, 128)
        P.op("dve", lambda e, nb=nb, pa=pa, ss=ss: e.scalar_tensor_tensor(out=nb[:, 448:576], in0=pa[:, 448:576], scalar=ss[:, 0:1], in1=gkv, op0=ALU.mult, op1=ALU.mult),
             reads=[b_pa, b_ss, b_g], writes=[b_nb], partial=True)
        if is_lat:
            rope_apply(K, pa[:, 576:640].rearrange("p (h d) -> p h d", h=1), nb[:, 576:640].rearrange("p (h d) -> p h d", h=1), cs, b_pa, b_nb, b_cs, 1, 32, tmpr)
        else:
            P.op("dve", lambda e, nb=nb, pa=pa: e.tensor_copy(out=nb[:, 576:640], in_=pa[:, 576:640]), reads=[b_pa], writes=[b_nb], partial=True)
        ps, b_ps = K.ps_tp.next()
        psb = ps.bitcast(BF16).rearrange("p (k t) -> p k t", k=8)
        P.op("pe", lambda e, psb=psb, nb=nb: e.transpose(out=psb[:, 0, :], in_=nb[:, 448:576], identity=K.ident), reads=[b_nb, K.b_const], writes=[b_ps])
        P.op("pe", lambda e, psb=psb, nb=nb: e.transpose(out=psb[0:64, 1, :], in_=nb[:, 576:640], identity=K.ident), reads=[b_nb, K.b_const], writes=[b_ps], partial=True)
        own = t in owned
        if own:
            ot = owned[t]
            sq2, b_sq2 = sq_r.next()
            ss2, b_ss2 = ss_r.next()
            P.op("pool", lambda e, sq2=sq2, pa=pa: e.tensor_tensor(out=sq2, in0=pa[:, 0:448], in1=pa[:, 0:448], op=ALU.mult), reads=[b_pa], writes=[b_sq2])
            P.op("dve", lambda e, sq2=sq2, ss2=ss2: e.reduce_sum(out=ss2[:, 0:1], in_=sq2, axis=AX.X), reads=[b_sq2], writes=[b_ss2])
            rms_rstd(K, ss2, b_ss2, 1, 448)
            P.op("dve", lambda e, nb=nb, pa=pa, ss2=ss2: e.scalar_tensor_tensor(out=nb[:, 0:448], in0=pa[:, 0:448], scalar=ss2[:, 0:1], in1=gq, op0=ALU.mult, op1=ALU.mult),
                 reads=[b_pa, b_ss2, b_g], writes=[b_nb], partial=True)
            for kc in range(4):
                kp = 128 if kc < 3 else 64
                P.op("pe", lambda e, psb=psb, nb=nb, kc=kc, kp=kp: e.transpose(out=psb[0:kp, 2 + kc, :], in_=nb[:, kc * 128:kc * 128 + kp], identity=K.ident),
                     reads=[b_nb, K.b_const], writes=[b_ps], partial=True)
        ce = "act" if t % 2 == 0 else "dve"
        copy_on(K, ce, ckvT[:, t * 128:(t + 1) * 128], psb[:, 0, :], [b_ps], [b_ckvT])
        copy_on(K, ce, krT[:, t * 128:(t + 1) * 128], psb[0:64, 1, :], [b_ps], [b_krT])
        if own:
            copy_on(K, ce, cqT[:, 0:3, ot * 128:(ot + 1) * 128], psb[:, 2:5, :], [b_ps], [b_cqT])
            copy_on(K, ce, cqT[0:64, 3, ot * 128:(ot + 1) * 128], psb[0:64, 5, :], [b_ps], [b_cqT])
    ev = 0
    for h in range(4):
        for s0 in range(0, NTOK, 512):
            sl = min(512, NTOK - s0)
            po, b_po = K.ps_acc.next()
            P.op("pe", lambda e, po=po, h=h, s0=s0, sl=sl: e.matmul(po[:, 0:sl], lhsT=wukv[:, h * 256:h * 256 + 128], rhs=ckvT[:, s0:s0 + sl], start=True, stop=True),
                 reads=[b_wukv, b_ckvT], writes=[b_po])
            evac_copy(K, ev, knT[:, h, s0:s0 + sl], po[:, 0:sl], [b_po], [b_knT], partial=True); ev += 1
    wv = wukv.rearrange("p (h c) -> p h c", h=4)[:, :, 128:256]
    for kt in range(34):
        po, b_po = K.ps_acc.next()
        pov = po.rearrange("p (h c) -> p h c", h=4)
        P.op("pe", lambda e, pov=pov, kt=kt: e.matmul(pov, lhsT=ckvT[:, kt * 128:(kt + 1) * 128], rhs=wv, start=True, stop=True),
             reads=[b_wukv, b_ckvT], writes=[b_po])
        evac_copy(K, ev, V_all[:, kt, :, 0:128], pov, [b_po], [b_V], partial=True); ev += 1
    qf_r = RR(sbuf_ring(K, "qf", [128, 4, 192], F32, 2))
    qb_r = RR(sbuf_ring(K, "qb", [128, 4, 192], BF16, 2))
    for (t, ot) in own_tiles(need_ctx):
        qf, b_qf = qf_r.next()
        qff = qf.rearrange("p h d -> p (h d)")
        for half in range(2):
            po, b_po = K.ps_acc.next()
            for kc in range(4):
                kp = 128 if kc < 3 else 64
                P.op("pe", lambda e, po=po, kc=kc, kp=kp, ot=ot, half=half: e.matmul(po[:, 0:384], lhsT=cqT[0:kp, kc, ot * 128:(ot + 1) * 128], rhs=wuq[0:kp, kc, half * 384:(half + 1) * 384],
                                                                           start=(kc == 0), stop=(kc == 3)),
                     reads=[b_cqT, b_wuq], writes=[b_po], partial=(kc > 0))
            evac_copy(K, half, qff[:, half * 384:(half + 1) * 384], po[:, 0:384], [b_po], [b_qf], partial=(half > 0))
        qb, b_qb = qb_r.next()
        P.op("act", lambda e, qb=qb, qf=qf: e.copy(out=qb[:, :, 0:128], in_=qf[:, :, 0:128]), reads=[b_qf], writes=[b_qb])
        if t < 32:
            cs, b_cs = cs_r.next()
            P.dma("sp", cs, ropeA[t * 128:(t + 1) * 128], writes=[b_cs], partial=False)
            rope_apply(K, qf[:, :, 128:192], qb[:, :, 128:192], cs, b_qf, b_qb, b_cs, 4, 32, tmpr)
        else:
            P.op("dve", lambda e, qb=qb, qf=qf: e.tensor_copy(out=qb[:, :, 128:192], in_=qf[:, :, 128:192]), reads=[b_qf], writes=[b_qb], partial=True)
        ps, b_ps = K.ps_tp.next()
        psb = ps.bitcast(BF16).rearrange("p (k t) -> p k t", k=8)
        for h in range(4):
            P.op("pe", lambda e, psb=psb, qb=qb, h=h: e.transpose(out=psb[:, h, :], in_=qb[:, h, 0:128], identity=K.ident), reads=[b_qb, K.b_const], writes=[b_ps], partial=(h > 0))
            P.op("pe", lambda e, psb=psb, qb=qb, h=h: e.transpose(out=psb[0:64, 4 + h, :], in_=qb[:, h, 128:192], identity=K.ident), reads=[b_qb, K.b_const], writes=[b_ps], partial=True)
        ce = "act" if ot % 2 == 0 else "dve"
        copy_on(K, ce, qnT[:, :, ot * 128:(ot + 1) * 128], psb[:, 0:4, :], [b_ps], [b_qnT])
        copy_on(K, ce, qrT[:, :, ot * 128:(ot + 1) * 128], psb[0:64, 4:8, :], [b_ps], [b_qrT])
    barrier(K)
    A.reset(m1)
    heads = [(h, [(knT[:, h, :], qnT[:, h, :], b_knT, b_qnT), (krT, qrT[:, h, :], b_krT, b_qrT)]) for h in range(4)]
    attn_core(K, heads, V_all, b_V, 128, 192 ** -0.5, o_tm, b_otm, need_ctx)
    store_oT(K, o_tm, b_otm, oT_d, 0, 17 if need_ctx else 16)
    barrier(K)
    A.reset(m0)


def phase_gqa(K, pTM, q_norm, k_norm, ropeB, oT_d, need_ctx):
    P, A = K.P, K.A
    m0 = A.mark()
    NOWN = OWN_LAT + OWN_CTX
    kT = A.sb([128, 2, NTOK], BF16, "gkT"); b_kT = Buf("gkT")
    V_all = A.sb([128, 34, 2, 136], BF16, "gV"); b_V = Buf("gV")
    qT = A.sb([128, 4, NOWN], BF16, "gqT"); b_qT = Buf("gqT")
    o_tm = A.sb([128, 17, 512], BF16, "gotm"); b_otm = Buf("gotm")
    m1 = A.mark()
    gn = A.sb([128, 6, 128], F32, "ggain"); b_g = Buf("ggain")
    for h in range(4):
        P.dma("sp", gn[:, h, :], q_norm.partition_broadcast(128), writes=[b_g])
    for h in range(2):
        P.dma("sp", gn[:, 4 + h, :], k_norm.partition_broadcast(128), writes=[b_g])
    P.op("pool", lambda e: e.memset(V_all[:, :, :, 128:129], 1.0), writes=[b_V], partial=True)
    pb_r = RR(sbuf_ring(K, "pb", [128, 1024], F32, 2))
    cs_r = RR(sbuf_ring(K, "csB", [128, 2, 64], F32, 2))
    sq_r = RR(sbuf_ring(K, "gsq", [128, 6, 128], F32, 2))
    xn_r = RR(sbuf_ring(K, "gxn", [128, 6, 128], F32, 2))
    ss_r = RR(sbuf_ring(K, "gss", [128, 6], F32, 3))
    nb_r = RR(sbuf_ring(K, "gnb", [128, 6, 128], BF16, 2))
    tmpr = RR(sbuf_ring(K, "grt", [128, 6, 64], F32, 8))
    owned = dict(own_tiles(need_ctx))
    import os
    tl = [int(v) for v in os.environ.get("GQA_T", ",".join(map(str, range(34)))).split(",")]
    for t in tl:
        is_lat = t < 32
        own = t in owned
        pb, b_pb = pb_r.next()
        P.dma("sp", pb, pTM[t * 128:(t + 1) * 128, 640:1664], writes=[b_pb], partial=False)
        h0, nh = (0, 6) if own else (4, 2)
        x = pb[:, h0 * 128:(h0 + nh) * 128].rearrange("p (h d) -> p h d", h=nh)
        sq, b_sq = sq_r.next()
        ss, b_ss = ss_r.next()
        xn, b_xn = xn_r.next()
        nb, b_nb = nb_r.next()
        P.op("pool", lambda e, sq=sq, x=x, nh=nh: e.tensor_tensor(out=sq[:, 0:nh, :], in0=x, in1=x, op=ALU.mult), reads=[b_pb], writes=[b_sq])
        P.op("dve", lambda e, sq=sq, ss=ss, nh=nh: e.reduce_sum(out=ss[:, 0:nh], in_=sq[:, 0:nh, :], axis=AX.X), reads=[b_sq], writes=[b_ss])
        rms_rstd(K, ss, b_ss, nh, 128)
        P.op("dve", lambda e, xn=xn, x=x, ss=ss, nh=nh: e.tensor_tensor(out=xn[:, 0:nh, :], in0=x, in1=ss[:, 0:nh].unsqueeze(2).broadcast_to([128, nh, 128]), op=ALU.mult),
             reads=[b_pb, b_ss], writes=[b_xn])
        if is_lat:
            P.op("pool", lambda e, xn=xn, nh=nh, h0=h0: e.tensor_tensor(out=xn[:, 0:nh, :], in0=xn[:, 0:nh, :], in1=gn[:, h0:h0 + nh, :], op=ALU.mult),
                 reads=[b_xn, b_g], writes=[b_xn])
            cs, b_cs = cs_r.next()
            P.dma("sp", cs, ropeB[t * 128:(t + 1) * 128], writes=[b_cs], partial=False)
            rope_apply(K, xn[:, 0:nh, :], nb[:, 0:nh, :], cs, b_xn, b_nb, b_cs, nh, 64, tmpr)
        else:
            P.op("pool", lambda e, xn=xn, nb=nb, nh=nh, h0=h0: e.tensor_tensor(out=nb[:, 0:nh, :], in0=xn[:, 0:nh, :], in1=gn[:, h0:h0 + nh, :], op=ALU.mult),
                 reads=[b_xn, b_g], writes=[b_nb])
        ps, b_ps = K.ps_tp.next()
        psb = ps.bitcast(BF16).rearrange("p (k t) -> p k t", k=8)
        for j in range(nh):
            P.op("pe", lambda e, psb=psb, nb=nb, j=j: e.transpose(out=psb[:, j, :], in_=nb[:, j, :], identity=K.ident), reads=[b_nb, K.b_const], writes=[b_ps], partial=(j > 0))
        ce = "act" if t % 2 == 0 else "dve"
        if own:
            ot = owned[t]
            copy_on(K, ce, qT[:, :, ot * 128:(ot + 1) * 128], psb[:, 0:4, :], [b_ps], [b_qT])
            copy_on(K, ce, kT[:, :, t * 128:(t + 1) * 128], psb[:, 4:6, :], [b_ps], [b_kT])
        else:
            copy_on(K, ce, kT[:, :, t * 128:(t + 1) * 128], psb[:, 0:2, :], [b_ps], [b_kT])
        P.op("act", lambda e, pb=pb, t=t: e.copy(out=V_all[:, t, :, 0:128], in_=pb[:, 768:1024].rearrange("p (h d) -> p h d", h=2)), reads=[b_pb], writes=[b_V], partial=True)
    barrier(K)
    A.reset(m1)
    heads = [(h // 2, [(kT[:, h // 2, :], qT[:, h, :], b_kT, b_qT)]) for h in range(4)]
    import os
    if getattr(K, "dbg", None):
        P.dma("sp", K.dbg["qT"], qT, reads=[b_qT], writes=[K.b_oT_d])
        P.dma("sp", K.dbg["kT"], kT, reads=[b_kT], writes=[K.b_oT_d])
        P.dma("sp", K.dbg["V"], V_all, reads=[b_V], writes=[K.b_oT_d])
    if os.environ.get("BISECT") == "prep":
        barrier(K); A.reset(m0); return
    attn_core(K, heads, V_all, b_V, 128, 128 ** -0.5, o_tm, b_otm, need_ctx)
    if os.environ.get("BISECT") == "core":
        barrier(K); A.reset(m0); return
    store_oT(K, o_tm, b_otm, oT_d, 1, 17 if need_ctx else 16)
    barrier(K)
    A.reset(m0)


def rope_tables(half):
    pos = np.arange(NLAT)
    if half == 1:
        pos = NLAT - 1 - pos
    row = (pos // 64).astype(np.float32)
    col = (pos % 64).astype(np.float32)
    out = []
    for rot in (64, 128):
        nf = rot // 4
        inv = (10000.0 ** (-np.arange(nf, dtype=np.float32) / nf)).astype(np.float32)
        ang = np.concatenate([row[:, None] * inv, col[:, None] * inv], -1).astype(np.float32)
        out.append(np.stack([np.cos(ang), np.sin(ang)], 1).astype(np.float32))
    return out


def load_bcast(K, src_row, width, name):
    t = K.A.sb([128, width], F32, name)
    b = Buf(name)
    K.P.dma("sp", t, src_row.partition_broadcast(128), writes=[b], partial=False)
    return t, b


def ln_affine_tile(K, r_rows_dram, g_b, b_b, bgb, tmp, out_tile, b_out):
    P = K.P
    xt, b_xt = tmp["xt"].next()
    P.dma("sp", xt, r_rows_dram, writes=[b_xt], partial=False)
    mv, b_mv = ln_stats(K, xt, b_xt, D, tmp)
    P.op("dve", lambda e: e.tensor_scalar(out=xt, in0=xt, scalar1=mv[:, 0:1], scalar2=mv[:, 2:3], op0=ALU.subtract, op1=ALU.mult),
         reads=[b_xt, b_mv], writes=[b_xt])
    P.op("pool", lambda e: e.tensor_tensor(out=xt, in0=xt, in1=g_b, op=ALU.mult), reads=[b_xt, bgb], writes=[b_xt])
    P.op("dve", lambda e: e.tensor_tensor(out=out_tile, in0=xt, in1=b_b, op=ALU.add), reads=[b_xt, bgb], writes=[b_out])


def phase_D(K, x_lat, x_ctx, hT_d, oT_d, mod_d, w_mgate, b_mgate, w_branch, w_out, ln1_g, ln1_b,
            w_ffn_in, w_ffn_out, ln2_g, ln2_b, r_d, xmid_d, out_lat, out_ctx, need_ctx):
    P, A = K.P, K.A
    m0 = A.mark()
    n_own = 17 if need_ctx else 16

    def xrows(ot, c0=0, c1=D):
        return x_lat[ot * 128:(ot + 1) * 128, c0:c1] if ot < 16 else x_ctx[0:128, c0:c1]

    def outrows(ot):
        return out_lat[ot * 128:(ot + 1) * 128, :] if ot < 16 else out_ctx[0:128, :]

    modT_l, b_ml = load_modT(K, mod_d, 0, "DmodTl")
    modT_c, b_mc = load_modT(K, mod_d, 1, "DmodTc")
    braw = A.sb([64, 128], F32, "bgraw"); b_braw = Buf("bgraw")
    P.dma("sp", braw, b_mgate.rearrange("i (k p) -> (i k) p", p=128), writes=[b_braw], partial=False)
    ps, b_ps = K.ps_misc.next()
    P.op("pe", lambda e: e.transpose(out=ps[:, 0:64], in_=braw, identity=K.ident_f[0:64, 0:64]), reads=[b_braw, K.b_const], writes=[b_ps])
    bgT = A.sb([128, 64], F32, "bgT"); b_bgT = Buf("bgT")
    P.op("dve", lambda e: e.tensor_copy(out=bgT, in_=ps[:, 0:64]), reads=[b_ps], writes=[b_bgT])
    m_grp = A.mark()
    groups = [list(range(0, 8)), list(range(8, n_own))]
    oT_v = oT_d.rearrange("(c p) t -> p c t", p=128)
    def do_group(tiles):
        A.reset(m_grp)
        g0 = tiles[0] * 128
        ng = len(tiles) * 128
        slabs = [(s0, min(512, ng - s0)) for s0 in range(0, ng, 512)]
        accT = A.sb([128, 16, ng], BF16, "accT"); b_accT = Buf("accT")
        m_s1 = A.mark()
        hTg = A.sb([128, 16, ng], BF16, "hTg"); b_hTg = Buf("hTg")
        oTg = A.sb([128, 16, ng], BF16, "oTg"); b_oTg = Buf("oTg")
        P.dma("sp", hTg, hT_d[:, :, g0:g0 + ng], writes=[b_hTg], partial=False)
        P.dma("act", oTg, oT_v[:, :, g0:g0 + ng], writes=[b_oTg], partial=False)
        wg_r = RR(sbuf_ring(K, "wg", [128, 16, 512], BF16, 2))
        wb_r = RR(sbuf_ring(K, "wbr", [128, 4, 512], BF16, 2))
        acc = A.sb([128, 4, ng], F32, "acc"); b_acc = Buf("acc")
        sg_r = RR(sbuf_ring(K, "sig", [128, 512], F32, 2))
        tm_r = RR(sbuf_ring(K, "term", [128, 512], F32, 2))
        for jc in range(4):
            for i in range(4):
                wg, b_wg = load_w_chunk(K, wg_r, w_mgate[i], D, jc * 512, 512)
                wb, b_wb = load_w_chunk(K, wb_r, w_branch[i], 512, jc * 512, 512)
                for jj in range(4):
                    j = jc * 4 + jj
                    for (s0, sl) in slabs:
                        pg, b_pg = K.ps_acc.next()
                        for k in range(16):
                            P.op("pe", lambda e, pg=pg, wg=wg, k=k, jj=jj, s0=s0, sl=sl: e.matmul(pg[:, 0:sl], lhsT=wg[:, k, jj * 128:(jj + 1) * 128], rhs=hTg[:, k, s0:s0 + sl],
                                                                                        start=(k == 0), stop=(k == 15)),
                                 reads=[b_wg, b_hTg], writes=[b_pg], partial=(k > 0))
                        pb, b_pb = K.ps_acc.next()
                        for k in range(4):
                            P.op("pe", lambda e, pb=pb, wb=wb, k=k, jj=jj, s0=s0, sl=sl, i=i: e.matmul(pb[:, 0:sl], lhsT=wb[:, k, jj * 128:(jj + 1) * 128], rhs=oTg[:, i * 4 + k, s0:s0 + sl],
                                                                                             start=(k == 0), stop=(k == 3)),
                                 reads=[b_wb, b_oTg], writes=[b_pb], partial=(k > 0))
                        sg, b_sg = sg_r.next()
                        P.op("act", lambda e, sg=sg, pg=pg, sl=sl, i=i, j=j: e.activation(out=sg[:, 0:sl], in_=pg[:, 0:sl], func=AF.Sigmoid, bias=bgT[:, i * 16 + j:i * 16 + j + 1]),
                             reads=[b_pg, b_bgT], writes=[b_sg])
                        if i == 0:
                            P.op("dve", lambda e, sg=sg, pb=pb, jj=jj, s0=s0, sl=sl: e.tensor_tensor(out=acc[:, jj, s0:s0 + sl], in0=sg[:, 0:sl], in1=pb[:, 0:sl], op=ALU.mult),
                                 reads=[b_sg, b_pb], writes=[b_acc], partial=True)
                        else:
                            tm, b_tm = tm_r.next()
                            P.op("dve", lambda e, tm=tm, sg=sg, pb=pb, sl=sl: e.tensor_tensor(out=tm[:, 0:sl], in0=sg[:, 0:sl], in1=pb[:, 0:sl], op=ALU.mult),
                                 reads=[b_sg, b_pb], writes=[b_tm])
                            P.op("dve", lambda e, tm=tm, jj=jj, s0=s0, sl=sl: e.tensor_tensor(out=acc[:, jj, s0:s0 + sl], in0=acc[:, jj, s0:s0 + sl], in1=tm[:, 0:sl], op=ALU.add),
                                 reads=[b_tm, b_acc], writes=[b_acc], partial=True)
            P.op("act", lambda e, jc=jc: e.copy(out=accT[:, jc * 4:(jc + 1) * 4, :], in_=acc), reads=[b_acc], writes=[b_accT], partial=True)
        barrier(K)
        A.reset(m_s1)
        g1l, b_g1l = load_bcast(K, mod_d[0:1, 2 * D:3 * D], D, "g1l")
        g1c, b_g1c = load_bcast(K, mod_d[1:2, 2 * D:3 * D], D, "g1c")
        wo_r = RR(sbuf_ring(K, "wo", [128, 16, 512], BF16, 2))
        xc_r = RR(sbuf_ring(K, "xc", [128, 512], F32, 3))
        rr_r = RR(sbuf_ring(K, "rr", [128, 512], F32, 3))
        for nch in range(4):
            wo, b_wo = load_w_chunk(K, wo_r, w_out, D, nch * 512, 512)
            for ti, ot in enumerate(tiles):
                po, b_po = K.ps_acc.next()
                for k in range(16):
                    P.op("pe", lambda e, po=po, wo=wo, k=k, ti=ti: e.matmul(po, lhsT=accT[:, k, ti * 128:(ti + 1) * 128], rhs=wo[:, k, :], start=(k == 0), stop=(k == 15)),
                         reads=[b_accT, b_wo], writes=[b_po], partial=(k > 0))
                xc, b_xc = xc_r.next()
                P.dma("act", xc, xrows(ot, nch * 512, (nch + 1) * 512), writes=[b_xc], partial=False)
                rr, b_rr = rr_r.next()
                gb, bgb = (g1l, b_g1l) if ot < 16 else (g1c, b_g1c)
                P.op("dve", lambda e, rr=rr, po=po, gb=gb, nch=nch: e.tensor_tensor(out=rr, in0=po, in1=gb[:, nch * 512:(nch + 1) * 512], op=ALU.mult),
                     reads=[b_po, bgb], writes=[b_rr])
                P.op("dve", lambda e, rr=rr, xc=xc: e.scalar_tensor_tensor(out=rr, in0=xc, scalar=ALPHA, in1=rr, op0=ALU.mult, op1=ALU.add),
                     reads=[b_xc, b_rr], writes=[b_rr])
                P.dma("sp", r_d[ot * 128:(ot + 1) * 128, nch * 512:(nch + 1) * 512], rr, reads=[b_rr], writes=[K.b_r_d])
        barrier(K)
        A.reset(m_grp)
        hT2 = A.sb([128, 16, ng], BF16, "hT2")
        b_hT2 = [Buf(f"hT2_{i}") for i in range(len(tiles))]
        m_s3b = A.mark()
        l1g, b_l1 = load_bcast(K, ln1_g, D, "l1g")
        l1b, _ = load_bcast(K, ln1_b, D, "l1b")
        b_l1b = _
        tmp = {
            "st": RR(sbuf_ring(K, "Dst", [128, 4, 6], F32, 2)), "mv": RR(sbuf_ring(K, "Dmv", [128, 4], F32, 2)),
            "xt": RR(sbuf_ring(K, "Dxt", [128, D], F32, 2)), "xs": RR(sbuf_ring(K, "Dxs", [128, D], BF16, 2)),
            "t1": RR(sbuf_ring(K, "Dt1", [128, 8, 128], F32, 2)),
        }
        xm_r = RR(sbuf_ring(K, "xm", [128, D], F32, 2))
        for ti, ot in enumerate(tiles):
            xm, b_xm = xm_r.next()
            xt, b_xt = tmp["xt"].next()
            P.dma("sp", xt, r_d[ot * 128:(ot + 1) * 128, :], writes=[b_xt], partial=False)
            mv, b_mv = ln_stats(K, xt, b_xt, D, tmp)
            P.op("dve", lambda e, xt=xt, mv=mv: e.tensor_scalar(out=xt, in0=xt, scalar1=mv[:, 0:1], scalar2=mv[:, 2:3], op0=ALU.subtract, op1=ALU.mult),
                 reads=[b_xt, b_mv], writes=[b_xt])
            P.op("dve", lambda e, xt=xt: e.tensor_tensor(out=xt, in0=xt, in1=l1g, op=ALU.mult), reads=[b_xt, b_l1], writes=[b_xt])
            P.op("dve", lambda e, xt=xt, xm=xm: e.tensor_tensor(out=xm, in0=xt, in1=l1b, op=ALU.add), reads=[b_xt, b_l1b], writes=[b_xm])
            P.dma("act", xmid_d[ot * 128:(ot + 1) * 128, :], xm, reads=[b_xm], writes=[K.b_xmid_d])
            mt, bm = (modT_l, b_ml) if ot < 16 else (modT_c, b_mc)
            hT_from_sbuf(K, xm, b_xm, mt, bm, 48, hT2[:, :, ti * 128:(ti + 1) * 128], b_hT2[ti], tmp)
        barrier(K)
        A.reset(m_s3b)
        aT = A.sb([128, 44, ng], BF16, "aT"); b_aT = Buf("aT")
        m_s3 = A.mark()
        wu_r = RR(sbuf_ring(K, "wu", [128, 16, 128], BF16, 2))
        wgt_r = RR(sbuf_ring(K, "wgt", [128, 16, 128], BF16, 2))
        sl_r = RR(sbuf_ring(K, "silu", [128, 512], F32, 2))
        for j in range(44):
            wu, b_wu = load_w_chunk(K, wu_r, w_ffn_in, D, j * 128, 128)
            wgt, b_wgt = load_w_chunk(K, wgt_r, w_ffn_in, D, FFH + j * 128, 128)
            for (s0, sl) in slabs:
                tl = list(range(s0 // 128, (s0 + sl) // 128))
                pu, b_pu = K.ps_acc.next()
                pgt, b_pgt = K.ps_acc.next()
                for k in range(16):
                    P.op("pe", lambda e, pu=pu, wu=wu, k=k, s0=s0, sl=sl: e.matmul(pu[:, 0:sl], lhsT=wu[:, k, :], rhs=hT2[:, k, s0:s0 + sl], start=(k == 0), stop=(k == 15)),
                         reads=[b_wu] + [b_hT2[t] for t in tl], writes=[b_pu], partial=(k > 0))
                for k in range(16):
                    P.op("pe", lambda e, pgt=pgt, wgt=wgt, k=k, s0=s0, sl=sl: e.matmul(pgt[:, 0:sl], lhsT=wgt[:, k, :], rhs=hT2[:, k, s0:s0 + sl], start=(k == 0), stop=(k == 15)),
                         reads=[b_wgt] + [b_hT2[t] for t in tl], writes=[b_pgt], partial=(k > 0))
                sv, b_sv = sl_r.next()
                P.op("act", lambda e, sv=sv, pgt=pgt, sl=sl: e.activation(out=sv[:, 0:sl], in_=pgt[:, 0:sl], func=AF.Silu), reads=[b_pgt], writes=[b_sv])
                P.op("dve", lambda e, sv=sv, pu=pu, j=j, s0=s0, sl=sl: e.tensor_tensor(out=aT[:, j, s0:s0 + sl], in0=sv[:, 0:sl], in1=pu[:, 0:sl], op=ALU.mult),
                     reads=[b_sv, b_pu], writes=[b_aT], partial=True)
        barrier(K)
        A.reset(m_s3)
        g2l, b_g2l = load_bcast(K, mod_d[0:1, 5 * D:6 * D], D, "g2l")
        g2c, b_g2c = load_bcast(K, mod_d[1:2, 5 * D:6 * D], D, "g2c")
        CW = 256
        w2_r = RR(sbuf_ring(K, "w2", [128, 44, CW], BF16, 2))
        xc2_r = RR(sbuf_ring(K, "xc2", [128, CW], F32, 3))
        rr2_r = RR(sbuf_ring(K, "rr2", [128, CW], F32, 3))
        for nch in range(D // CW):
            w2, b_w2 = load_w_chunk(K, w2_r, w_ffn_out, FFH, nch * CW, CW)
            for ti, ot in enumerate(tiles):
                po, b_po = K.ps_acc.next()
                for k in range(44):
                    P.op("pe", lambda e, po=po, w2=w2, k=k, ti=ti: e.matmul(po[:, 0:256], lhsT=aT[:, k, ti * 128:(ti + 1) * 128], rhs=w2[:, k, :], start=(k == 0), stop=(k == 43)),
                         reads=[b_aT, b_w2], writes=[b_po], partial=(k > 0))
                xc, b_xc = xc2_r.next()
                P.dma("act", xc, xmid_d[ot * 128:(ot + 1) * 128, nch * 256:(nch + 1) * 256], reads=[K.b_xmid_d], writes=[b_xc], partial=False)
                rr, b_rr = rr2_r.next()
                gb, bgb = (g2l, b_g2l) if ot < 16 else (g2c, b_g2c)
                P.op("dve", lambda e, rr=rr, po=po, gb=gb, nch=nch: e.tensor_tensor(out=rr, in0=po[:, 0:256], in1=gb[:, nch * 256:(nch + 1) * 256], op=ALU.mult),
                     reads=[b_po, bgb], writes=[b_rr])
                P.op("dve", lambda e, rr=rr, xc=xc: e.scalar_tensor_tensor(out=rr, in0=xc, scalar=ALPHA, in1=rr, op0=ALU.mult, op1=ALU.add),
                     reads=[b_xc, b_rr], writes=[b_rr])
                P.dma("sp", r_d[ot * 128:(ot + 1) * 128, nch * 256:(nch + 1) * 256], rr, reads=[b_rr], writes=[K.b_r_d])
        barrier(K)
        A.reset(m_grp)
        l2g, b_l2g = load_bcast(K, ln2_g, D, "l2g")
        l2b, b_l2b = load_bcast(K, ln2_b, D, "l2b")
        tmp = {"st": RR(sbuf_ring(K, "Est", [128, 4, 6], F32, 2)), "mv": RR(sbuf_ring(K, "Emv", [128, 4], F32, 2)),
               "xt": RR(sbuf_ring(K, "Ext", [128, D], F32, 3))}
        for ti, ot in enumerate(tiles):
            xt, b_xt = tmp["xt"].next()
            P.dma("sp", xt, r_d[ot * 128:(ot + 1) * 128, :], reads=[K.b_r_d], writes=[b_xt], partial=False)
            mv, b_mv = ln_stats(K, xt, b_xt, D, tmp)
            P.op("dve", lambda e, xt=xt, mv=mv: e.tensor_scalar(out=xt, in0=xt, scalar1=mv[:, 0:1], scalar2=mv[:, 2:3], op0=ALU.subtract, op1=ALU.mult),
                 reads=[b_xt, b_mv], writes=[b_xt])
            P.op("dve", lambda e, xt=xt: e.tensor_tensor(out=xt, in0=xt, in1=l2g, op=ALU.mult), reads=[b_xt, b_l2g], writes=[b_xt])
            P.op("dve", lambda e, xt=xt: e.tensor_tensor(out=xt, in0=xt, in1=l2b, op=ALU.add), reads=[b_xt, b_l2b], writes=[b_xt])
            P.dma("act", outrows(ot), xt, reads=[b_xt], writes=[K.b_out])
        barrier(K)
    for tiles in groups:
        do_group(tiles)
    A.reset(m0)


def mamba_consts():
    s = np.arange(128)[:, None]
    l = np.arange(128)[None, :]
    tri = np.stack([(s <= l), (s >= l), np.ones((128, 128), bool)]).astype(np.float32)
    l5 = np.arange(512)[None, :]
    masks = np.zeros((2, 4, 128, 512), np.float32)
    for j in range(4):
        masks[0, j] = (j * 128 + s) <= l5
        masks[1, j] = (j * 128 + s) >= l5
    return tri, masks.astype(ml_dtypes.bfloat16)


def store_oT_tile(K, y_bf, b_y, oT_d, branch, ot, stg):
    P = K.P
    ps, b_ps = K.ps_tp.next()
    psb = ps.bitcast(BF16).rearrange("p (k t) -> p k t", k=8)
    for h in range(4):
        P.I("pe", "transpose", reads=[b_y, K.b_const], writes=[b_ps], partial=(h > 0),
            out=psb[:, h, :], in_=y_bf[:, h * 128:(h + 1) * 128], identity=K.ident)
    so, b_so = stg.next()
    P.I("dve", "tensor_copy", reads=[b_ps], writes=[b_so], out=so, in_=psb[:, 0:4, :])
    P.dma("sp", oT_d[branch * 512:(branch + 1) * 512, ot * 128:(ot + 1) * 128].rearrange("(h p) t -> p h t", p=128), so,
          reads=[b_so], writes=[K.b_oT_d])


def phase_mamba(K, pTM, pFM, conv_w, conv_b, a_log, dt_bias, d_skip, norm_g, tri_d, masks_d, cumT_d, oT_d, need_ctx):
    P, A = K.P, K.A
    m0 = A.mark()
    BT = A.sb([128, 2, NTOK], BF16, "mBT"); b_BT = Buf("mBT")
    CT = A.sb([128, 2, NOWN], BF16, "mCT"); b_CT = Buf("mCT")
    xdt = A.sb([128, 2, 34, 512], BF16, "mxdt"); b_xdt = Buf("mxdt")
    xtm = A.sb([128, 17, 512], BF16, "mxtm"); b_xtm = Buf("mxtm")
    negcum = A.sb([128, 34, 16], F32, "mnegcum"); b_nc = Buf("mnegcum")
    m1 = A.mark()
    tri = A.sb([128, 3, 128], F32, "mtri"); b_tri = Buf("mtri")
    P.dma("sp", tri, tri_d.rearrange("a p l -> p a l"), writes=[b_tri], partial=False)
    craw = A.sb([32, 128], F32, "mcraw"); b_craw = Buf("mcraw")
    P.dma("sp", craw[0:24, :], conv_w.rearrange("k (c p) -> (k c) p", p=128), writes=[b_craw])
    P.dma("sp", craw[24:32, :], conv_b.rearrange("o (c p) -> (o c) p", p=128), writes=[b_craw])
    ps, b_ps = K.ps_misc.next()
    P.I("pe", "transpose", reads=[b_craw, K.b_const], writes=[b_ps], out=ps[:, 0:32], in_=craw, identity=K.ident_f[0:32, 0:32])
    cw = A.sb([128, 32], F32, "mcw"); b_cw = Buf("mcw")
    P.I("dve", "tensor_copy", reads=[b_ps], writes=[b_cw], out=cw, in_=ps[:, 0:32])
    alog, b_alog = load_bcast(K, a_log, 16, "malog")
    dtb, b_dtb = load_bcast(K, dt_bias, 16, "mdtb")
    P.I("act", "activation", reads=[b_alog], writes=[b_alog], out=alog, in_=alog, func=AF.Exp)
    P.I("dve", "tensor_scalar_mul", reads=[b_alog], writes=[b_alog], out=alog, in0=alog, scalar1=-1.0)
    dt = A.sb([128, 34, 16], F32, "mdt"); b_dt = Buf("mdt")
    da = A.sb([128, 34, 16], F32, "mda"); b_da = Buf("mda")
    P.dma("sp", dt, pTM[:, 2176:2192].rearrange("(t p) c -> p t c", p=128), reads=[K.b_pTM], writes=[b_dt], partial=False, allow_slow_non_contiguous=False)
    P.I("dve", "tensor_tensor", reads=[b_dt, b_dtb], writes=[b_dt], out=dt, in0=dt, in1=dtb.unsqueeze(1).broadcast_to([128, 34, 16]), op=ALU.add)
    P.I("act", "activation", reads=[b_dt], writes=[b_dt], out=dt, in_=dt, func=AF.Exp)
    P.I("act", "activation", reads=[b_dt], writes=[b_dt], out=dt, in_=dt, func=AF.Ln, bias=1.0)
    P.I("dve", "tensor_tensor", reads=[b_dt, b_alog], writes=[b_da], out=da, in0=dt, in1=alog.unsqueeze(1).broadcast_to([128, 34, 16]), op=ALU.mult)
    ntot_r = RR(sbuf_ring(K, "mntot", [128, 8], F32, 3))
    orders = [[32, 33] + list(range(32)), [33, 32] + list(range(31, -1, -1))]
    for d in range(2):
        ntot, b_nt = ntot_r.next()
        P.I("pool", "memset", writes=[b_nt], ap=ntot, constant=0.0)
        for T in orders[d]:
            pc, b_pc = K.ps_misc.next()
            P.I("pe", "matmul", reads=[b_tri, b_da], writes=[b_pc], out=pc[:, 0:8], lhsT=tri[:, d, :], rhs=da[:, T, d * 8:(d + 1) * 8], start=True, stop=True)
            P.I("pe", "matmul", reads=[b_tri, b_da], writes=[b_pc], partial=True, out=pc[:, 8:16], lhsT=tri[:, 2, :], rhs=da[:, T, d * 8:(d + 1) * 8], start=True, stop=True)
            P.I("dve", "scalar_tensor_tensor", reads=[b_pc, b_nt], writes=[b_nc], partial=True,
                out=negcum[:, T, d * 8:(d + 1) * 8], in0=pc[:, 0:8], scalar=-1.0, in1=ntot, op0=ALU.mult, op1=ALU.add)
            ntot2, b_nt2 = ntot_r.next()
            P.I("dve", "scalar_tensor_tensor", reads=[b_pc, b_nt], writes=[b_nt2],
                out=ntot2, in0=pc[:, 8:16], scalar=-1.0, in1=ntot, op0=ALU.mult, op1=ALU.add)
            ntot, b_nt = ntot2, b_nt2
    m2 = A.mark()
    cumT = A.sb([16, NTOK], F32, "mcumT"); b_cumT = Buf("mcumT")
    for T in range(34):
        pt_, b_pt_ = K.ps_misc.next()
        P.I("pe", "transpose", reads=[b_nc, K.b_const], writes=[b_pt_], out=pt_[0:16, 0:128], in_=negcum[:, T, :], identity=K.ident_f)
        P.I("dve", "tensor_scalar_mul", reads=[b_pt_], writes=[b_cumT], partial=True, out=cumT[:, T * 128:(T + 1) * 128], in0=pt_[0:16, 0:128], scalar1=-1.0)
    P.dma("sp", cumT_d, cumT, reads=[b_cumT], writes=[K.b_cumT_d], partial=False)
    barrier(K)
    A.reset(m2)
    pf_r = RR(sbuf_ring(K, "mpf", [128, NTOK], F32, 2))
    u = A.sb([128, NTOK], F32, "mu"); b_u = Buf("mu")
    xTb_r = RR(sbuf_ring(K, "mxTb", [128, NTOK], BF16, 2))
    segs = [(0, NLAT), (NLAT, NTOK)]
    for c in range(8):
        pf, b_pf = pf_r.next()
        P.dma("sp", pf, pFM[1536 + c * 128:1536 + (c + 1) * 128, :], reads=[K.b_pFM], writes=[b_pf], partial=False)
        P.I("act", "activation", reads=[b_pf, b_cw], writes=[b_u], out=u, in_=pf, func=AF.Identity, scale=cw[:, 8 + c:9 + c], bias=cw[:, 24 + c:25 + c])
        for (a, b) in segs:
            P.I("dve", "scalar_tensor_tensor", reads=[b_pf, b_cw, b_u], writes=[b_u], partial=True,
                out=u[:, a + 1:b], in0=pf[:, a:b - 1], scalar=cw[:, c:c + 1], in1=u[:, a + 1:b], op0=ALU.mult, op1=ALU.add)
            P.I("dve", "scalar_tensor_tensor", reads=[b_pf, b_cw, b_u], writes=[b_u], partial=True,
                out=u[:, a:b - 1], in0=pf[:, a + 1:b], scalar=cw[:, 16 + c:17 + c], in1=u[:, a:b - 1], op0=ALU.mult, op1=ALU.add)
        if c < 4:
            xTb, b_xTb = xTb_r.next()
            P.I("act", "activation", reads=[b_u], writes=[b_xTb], out=xTb, in_=u, func=AF.Silu)
            for T0 in range(0, 34, 8):
                nT = min(8, 34 - T0)
                ps2, b_ps2 = K.ps_tp.next()
                psb = ps2.bitcast(BF16).rearrange("p (k t) -> p k t", k=8)
                for i in range(nT):
                    P.I("pe", "transpose", reads=[b_xTb, K.b_const], writes=[b_ps2], partial=(i > 0),
                        out=psb[:, i, :], in_=xTb[:, (T0 + i) * 128:(T0 + i + 1) * 128], identity=K.ident)
                src = psb[:, 0:nT, :].rearrange("p t (h e) -> p t h e", h=2)
                for d in range(2):
                    P.I("dve", "tensor_tensor", reads=[b_ps2, b_dt], writes=[b_xdt], partial=True,
                        out=xdt[:, d, T0:T0 + nT, c * 128:(c + 1) * 128].rearrange("p t (h e) -> p t h e", h=2), in0=src,
                        in1=dt[:, T0:T0 + nT, d * 8 + 2 * c:d * 8 + 2 * c + 2].unsqueeze(3).broadcast_to([128, nT, 2, 64]), op=ALU.mult)
                if T0 < 16:
                    P.I("dve", "tensor_copy", reads=[b_ps2], writes=[b_xtm], partial=True, out=xtm[:, T0:T0 + 8, c * 128:(c + 1) * 128], in_=psb[:, 0:8, :])
                if T0 == 32:
                    P.I("dve", "tensor_copy", reads=[b_ps2], writes=[b_xtm], partial=True, out=xtm[:, 16, c * 128:(c + 1) * 128], in_=psb[:, 0, :])
        elif c < 6:
            P.I("act", "activation", reads=[b_u], writes=[b_BT], partial=True, out=BT[:, c - 4, :], in_=u, func=AF.Silu)
        else:
            P.I("act", "activation", reads=[b_u], writes=[b_CT], partial=True, out=CT[:, c - 6, 0:OWN_LAT], in_=u[:, 0:OWN_LAT], func=AF.Silu)
            P.I("act", "activation", reads=[b_u], writes=[b_CT], partial=True, out=CT[:, c - 6, OWN_LAT:NOWN], in_=u[:, NLAT:NLAT + OWN_CTX], func=AF.Silu)
    barrier(K)
    A.reset(m1)
    ysum = A.sb([128, 17, 512], F32, "mysum"); b_ys = Buf("mysum")
    masks = A.sb([128, 2, 4, 512], BF16, "mmask"); b_mk = Buf("mmask")
    P.dma("sp", masks, masks_d.rearrange("d j p l -> p d j l"), writes=[b_mk], partial=False)
    crow_r = RR(sbuf_ring(K, "mcrow", [128, NOWN], F32, 2))
    SKEW = 2
    dec_r = RR(sbuf_ring(K, "mdec", [128, 512], F32, SKEW + 2))
    pt_r = RR(sbuf_ring(K, "mpt", [128, 512], BF16, SKEW + 2))
    obanks = K.banks[4:8]
    qgroups = []
    for i in range(4):
        qgroups.append((i * 512, 512, i * 4, False))
    if need_ctx:
        qgroups.append((OWN_LAT, 128, 32, True))
    units = []
    for d in range(2):
        for h in range(8):
            for (q0, nq, T0, is_ctx) in qgroups:
                if not is_ctx:
                    if d == 0:
                        blocks = [(32, None), (33, None)] + [(T, None) for T in range(T0)] + [(T0 + j, j) for j in range(4)]
                    else:
                        blocks = [(32, None), (33, None)] + [(T, None) for T in range(31, T0 + 3, -1)] + [(T0 + j, j) for j in range(4)]
                else:
                    blocks = [(32, 0)] if d == 0 else [(33, None), (32, 0)]
                for bi, (kb, dj) in enumerate(blocks):
                    units.append({"d": d, "h": h, "q0": q0, "nq": nq, "kb": kb, "dj": dj, "first": bi == 0, "last": bi == len(blocks) - 1,
                                  "newrow": (bi == 0 and q0 == 0)})
    cur = {}

    def stage1(u):
        d, h, q0, nq, kb, dj = u["d"], u["h"], u["q0"], u["nq"], u["kb"], u["dj"]
        g = h // 4
        col = d * 8 + h
        if u["newrow"]:
            crow, b_crow = crow_r.next()
            P.dma("sp", crow[:, 0:OWN_LAT], cumT_d[col:col + 1, 0:OWN_LAT].partition_broadcast(128), reads=[K.b_cumT_d], writes=[b_crow], partial=False)
            P.dma("sp", crow[:, OWN_LAT:NOWN], cumT_d[col:col + 1, NLAT:NLAT + OWN_CTX].partition_broadcast(128), reads=[K.b_cumT_d], writes=[b_crow])
            cur["crow"], cur["b_crow"] = crow, b_crow
        crow, b_crow = cur["crow"], cur["b_crow"]
        pc, b_pc = K.ps_acc.next()
        P.I("pe", "matmul", reads=[b_BT, b_CT], writes=[b_pc], out=pc[:, 0:nq], lhsT=BT[:, g, kb * 128:(kb + 1) * 128], rhs=CT[:, g, q0:q0 + nq], start=True, stop=True)
        dec, b_dec = dec_r.next()
        if dj is None:
            P.I("act", "activation", reads=[b_crow, b_nc], writes=[b_dec], out=dec[:, 0:nq], in_=crow[:, q0:q0 + nq], func=AF.Exp, bias=negcum[:, kb, col:col + 1])
        else:
            P.I("dve", "tensor_scalar", reads=[b_crow, b_nc], writes=[b_dec], out=dec[:, 0:nq], in0=crow[:, q0:q0 + nq], scalar1=negcum[:, kb, col:col + 1], scalar2=0.0,
                op0=ALU.add, op1=ALU.min)
            P.I("act", "activation", reads=[b_dec], writes=[b_dec], out=dec[:, 0:nq], in_=dec[:, 0:nq], func=AF.Exp)
        pt, b_pt = pt_r.next()
        P.I("dve", "tensor_tensor", reads=[b_pc, b_dec], writes=[b_pt], out=pt[:, 0:nq], in0=pc[:, 0:nq], in1=dec[:, 0:nq], op=ALU.mult)
        if dj is not None:
            P.I("pool", "tensor_tensor", reads=[b_pt, b_mk], writes=[b_pt], out=pt[:, 0:nq], in0=pt[:, 0:nq], in1=masks[:, d, dj, 0:nq], op=ALU.mult)
        u["pt"], u["b_pt"] = pt, b_pt

    def stage2(u):
        d, h, q0, nq, kb = u["d"], u["h"], u["q0"], u["nq"], u["kb"]
        pt, b_pt = u["pt"], u["b_pt"]
        for qs in range(nq // 128):
            ob, b_ob = obanks[qs]
            P.I("pe", "matmul", reads=[b_pt, b_xdt], writes=[b_ob], partial=(not u["first"]), out=ob[:, 0:64], lhsT=pt[:, qs * 128:(qs + 1) * 128],
                rhs=xdt[:, d, kb, h * 64:(h + 1) * 64], start=u["first"], stop=u["last"])
        if u["last"]:
            for qs in range(nq // 128):
                ob, b_ob = obanks[qs]
                ot = q0 // 128 + qs
                if d == 0:
                    P.I("dve", "tensor_copy", reads=[b_ob], writes=[b_ys], partial=True, out=ysum[:, ot, h * 64:(h + 1) * 64], in_=ob[:, 0:64])
                else:
                    P.I("dve", "tensor_tensor", reads=[b_ob, b_ys], writes=[b_ys], partial=True, out=ysum[:, ot, h * 64:(h + 1) * 64], in0=ob[:, 0:64],
                        in1=ysum[:, ot, h * 64:(h + 1) * 64], op=ALU.add)
    nu = len(units)
    for i in range(nu + SKEW):
        if i < nu:
            stage1(units[i])
        if i >= SKEW:
            stage2(units[i - SKEW])
    dsk, b_dsk = load_bcast(K, d_skip, 8, "mdsk")
    ng, b_ng = load_bcast(K, norm_g, 512, "mng")
    z_r = RR(sbuf_ring(K, "mz", [128, 512], F32, 2))
    t_r = RR(sbuf_ring(K, "mt", [128, 512], F32, 2))
    ss_r = RR(sbuf_ring(K, "mss", [128, 2], F32, 3))
    yb_r = RR(sbuf_ring(K, "myb", [128, 512], BF16, 2))
    stg = RR(sbuf_ring(K, "mstg", [128, 4, 128], BF16, 2))
    for (T, ot) in own_tiles(need_ctx):
        z, b_z = z_r.next()
        P.dma("sp", z, pTM[T * 128:(T + 1) * 128, 1664:2176], reads=[K.b_pTM], writes=[b_z], partial=False)
        P.I("act", "activation", reads=[b_z], writes=[b_z], out=z, in_=z, func=AF.Silu)
        t, b_t = t_r.next()
        P.I("dve", "tensor_tensor", reads=[b_xtm, b_dsk], writes=[b_t], out=t.rearrange("p (h e) -> p h e", h=8), in0=xtm[:, ot, :].rearrange("p (h e) -> p h e", h=8),
            in1=dsk.unsqueeze(2).broadcast_to([128, 8, 64]), op=ALU.mult)
        P.I("pool", "tensor_tensor", reads=[b_t, b_ys], writes=[b_t], out=t, in0=t, in1=ysum[:, ot, :], op=ALU.add)
        P.I("dve", "tensor_tensor", reads=[b_t, b_z], writes=[b_t], out=t, in0=t, in1=z, op=ALU.mult)
        P.I("pool", "tensor_tensor", reads=[b_t], writes=[b_z], out=z, in0=t, in1=t, op=ALU.mult)
        ss, b_ss = ss_r.next()
        P.I("dve", "reduce_sum", reads=[b_z], writes=[b_ss], out=ss[:, 0:1], in_=z, axis=AX.X)
        rms_rstd(K, ss, b_ss, 1, 512)
        yb, b_yb = yb_r.next()
        P.I("dve", "scalar_tensor_tensor", reads=[b_t, b_ss, b_ng], writes=[b_yb], out=yb, in0=t, scalar=ss[:, 0:1], in1=ng, op0=ALU.mult, op1=ALU.mult)
        store_oT_tile(K, yb, b_yb, oT_d, 3, ot, stg)
    barrier(K)
    A.reset(m0)


def hyena_consts(n, n_own):
    N = 2 * n
    nb = n + 1
    F = (nb + 127) // 128 * 128
    s = np.arange(n, dtype=np.float64)
    f = np.arange(F, dtype=np.float64)
    valid = (f < nb)
    th = 2 * np.pi * np.outer(s, f) / N
    CT = np.cos(th) * valid[None, :]
    ST = np.sin(th) * valid[None, :]
    t = np.arange(n_own, dtype=np.float64)
    w = np.where((f == 0) | (f == n), 1.0, 2.0) * valid
    th2 = 2 * np.pi * np.outer(f, t) / N
    Ci = (w[:, None] * np.cos(th2)) / N
    Si = (w[:, None] * np.sin(th2)) / N
    tt = np.linspace(0.0, 1.0, n, dtype=np.float32)[:, None]
    omega = (2.0 * math.pi * np.arange(n, dtype=np.float32) / n).astype(np.float32)
    bands = np.linspace(1e-4, 15, 16, dtype=np.float32)
    ang = omega[:, None] * bands[None, :]
    feats = np.concatenate([tt, np.cos(ang), -np.sin(ang)], -1).astype(np.float32)
    negt = (-tt[:, 0]).astype(np.float32).reshape(n // 128, 128).T.copy()
    bf = ml_dtypes.bfloat16
    nsc, nfb = n // 128, F // 128
    CTt = np.ascontiguousarray(CT.reshape(nsc, 128, nfb, 128).transpose(2, 1, 0, 3)).astype(bf)
    STt = np.ascontiguousarray(ST.reshape(nsc, 128, nfb, 128).transpose(2, 1, 0, 3)).astype(bf)
    return {"CT": CTt, "ST": STt, "Ci": Ci.astype(bf), "Si": Si.astype(bf),
            "featsT": np.ascontiguousarray(feats.T), "negt": negt}


def hy_sin(K, dst, src_ps, rows, w, fcol, fbcol, b_src, b_dst, b_par, tmp):
    P = K.P
    a, b_a = tmp.next()
    s2, b_s2 = tmp.next()
    s4, b_s4 = tmp.next()
    a, s2, s4 = a[0:rows, 0:w], s2[0:rows, 0:w], s4[0:rows, 0:w]
    P.I("dve", "tensor_scalar", reads=[b_src, b_par], writes=[b_a], out=a, in0=src_ps, scalar1=fcol, scalar2=fbcol, op0=ALU.mult, op1=ALU.add)
    P.I("act", "activation", reads=[b_a], writes=[b_s2], out=s2, in_=a, func=AF.Sin, scale=0.5)
    P.I("act", "activation", reads=[b_a], writes=[b_s4], out=s4, in_=a, func=AF.Sin, scale=0.25)
    P.I("dve", "tensor_tensor", reads=[b_s4], writes=[b_s4], out=s4, in0=s4, in1=s4, op=ALU.mult)
    P.I("dve", "tensor_scalar", reads=[b_s4], writes=[b_s4], out=s4, in0=s4, scalar1=-2.0, scalar2=1.0, op0=ALU.mult, op1=ALU.add)
    P.I("dve", "scalar_tensor_tensor", reads=[b_s2, b_s4], writes=[b_dst], partial=True, out=dst, in0=s2, scalar=2.0, in1=s4, op0=ALU.mult, op1=ALU.mult)


def phase_hyena(K, pFM, conv_w, conv_b, w1, b1, w2, b2, w3, freq, skip, sgn_d, deltas_d, geoms, oT_d):
    P, A = K.P, K.A
    m0 = A.mark()
    craw = A.sb([52, 128], F32, "hcraw"); b_craw = Buf("hcraw")
    P.dma("sp", craw[0:36, :], conv_w.rearrange("k (c p) -> (k c) p", p=128), writes=[b_craw])
    P.dma("sp", craw[36:48, :], conv_b.rearrange("o (c p) -> (o c) p", p=128), writes=[b_craw])
    P.dma("sp", craw[48:52, :], skip.rearrange("o (c p) -> (o c) p", p=128), writes=[b_craw])
    ps, b_ps = K.ps_misc.next()
    P.I("pe", "transpose", reads=[b_craw, K.b_const], writes=[b_ps], out=ps[:, 0:52], in_=craw, identity=K.ident_f[0:52, 0:52])
    cw = A.sb([128, 52], F32, "hcw"); b_cw = Buf("hcw")
    P.I("dve", "tensor_copy", reads=[b_ps], writes=[b_cw], out=cw, in_=ps[:, 0:52])
    w1s = A.sb([33, 64], F32, "hw1"); w2s = A.sb([64, 64], F32, "hw2"); w3s = A.sb([64, 1024], F32, "hw3"); b_w = Buf("hw")
    P.dma("sp", w1s, w1, writes=[b_w]); P.dma("sp", w2s, w2, writes=[b_w]); P.dma("sp", w3s, w3, writes=[b_w])
    par = A.sb([64, 8], F32, "hpar"); b_par = Buf("hpar")
    P.dma("sp", par[:, 0:1], freq.rearrange("o k -> k o"), writes=[b_par], allow_slow_non_contiguous=True)
    P.dma("sp", par[:, 1:2], b1.rearrange("o k -> k o"), writes=[b_par], allow_slow_non_contiguous=True)
    P.dma("sp", par[:, 2:3], b2.rearrange("o k -> k o"), writes=[b_par], allow_slow_non_contiguous=True)
    P.I("dve", "tensor_tensor", reads=[b_par], writes=[b_par], partial=True, out=par[:, 3:4], in0=par[:, 0:1], in1=par[:, 1:2], op=ALU.mult)
    P.I("dve", "tensor_tensor", reads=[b_par], writes=[b_par], partial=True, out=par[:, 4:5], in0=par[:, 0:1], in1=par[:, 2:3], op=ALU.mult)
    deltab, b_dl = load_bcast(K, deltas_d, 512, "hdelta")
    sgn, b_sgn = load_bcast(K, sgn_d, 1, "hsgn")
    m_g = A.mark()
    for G in geoms:
        A.reset(m_g)
        hyena_geom(K, G, pFM, cw, b_cw, w1s, w2s, w3s, b_w, par, b_par, deltab, b_dl, sgn, b_sgn, oT_d)
    barrier(K)
    A.reset(m0)


def hyena_geom(K, G, pFM, cw, b_cw, w1s, w2s, w3s, b_w, par, b_par, deltab, b_dl, sgn, b_sgn, oT_d):
    P, A = K.P, K.A
    n, col0, own_off, n_own = G["n"], G["col0"], G["own_off"], G["n_own"]
    nsc = n // 128
    nfb = G["CT"].shape[0]
    negt = A.sb([128, nsc], F32, "hnegt"); b_negt = Buf("hnegt")
    P.dma("sp", negt, G["negt"], writes=[b_negt], partial=False)
    hdn2T = A.sb([64, n], F32, "hhdn2"); b_h2 = Buf("hhdn2")
    m_a = A.mark()
    featsT = A.sb([33, n], F32, "hfeat"); b_ft = Buf("hfeat")
    P.dma("sp", featsT, G["featsT"], writes=[b_ft], partial=False)
    hdn1T = A.sb([64, n], F32, "hhdn1"); b_h1 = Buf("hhdn1")
    tmp = RR(sbuf_ring(K, "hsin", [64, 512], F32, 6))
    for s0 in range(0, n, 512):
        sl = min(512, n - s0)
        pz, b_pz = K.ps_misc.next()
        P.I("pe", "matmul", reads=[b_w, b_ft], writes=[b_pz], out=pz[0:64, 0:sl], lhsT=w1s, rhs=featsT[:, s0:s0 + sl], start=True, stop=True)
        hy_sin(K, hdn1T[:, s0:s0 + sl], pz[0:64, 0:sl], 64, sl, par[:, 0:1], par[:, 3:4], b_pz, b_h1, b_par, tmp)
    for s0 in range(0, n, 512):
        sl = min(512, n - s0)
        pz, b_pz = K.ps_misc.next()
        P.I("pe", "matmul", reads=[b_w, b_h1], writes=[b_pz], out=pz[0:64, 0:sl], lhsT=w2s, rhs=hdn1T[:, s0:s0 + sl], start=True, stop=True)
        hy_sin(K, hdn2T[:, s0:s0 + sl], pz[0:64, 0:sl], 64, sl, par[:, 0:1], par[:, 4:5], b_pz, b_h2, b_par, tmp)
    barrier(K)
    A.reset(m_a)
    m_h = A.mark()
    for hc in range(2):
        A.reset(m_h)
        hyena_half(K, G, hc, pFM, cw, b_cw, w3s, b_w, deltab, b_dl, sgn, b_sgn, negt, b_negt, hdn2T, b_h2, oT_d)


def hyena_half(K, G, hc, pFM, cw, b_cw, w3s, b_w, deltab, b_dl, sgn, b_sgn, negt, b_negt, hdn2T, b_h2, oT_d):
    P, A = K.P, K.A
    n, col0, own_off, n_own = G["n"], G["col0"], G["own_off"], G["n_own"]
    nsc = n // 128
    nfb = G["CT"].shape[0]
    sfilt = A.sb([128, nsc, 256], BF16, "hsf"); b_sf = Buf("hsf")
    dfilt = A.sb([128, nsc, 256], BF16, "hdf"); b_df = Buf("hdf")
    vv_tm = A.sb([128, nsc, 256], BF16, "hvv"); b_vv = Buf("hvv")
    vv_own = A.sb([128, 2, n_own], BF16, "hvvo"); b_vvo = Buf("hvvo")
    x0_own = A.sb([128, 2, n_own], BF16, "hx0o"); b_x0o = Buf("hx0o")
    Yre = A.sb([128, nfb, 256], BF16, "hYre"); b_Yre = Buf("hYre")
    Yim = A.sb([128, nfb, 256], BF16, "hYim"); b_Yim = Buf("hYim")
    m_t = A.mark()
    win_r = RR(sbuf_ring(K, "hwin", [128, 256], F32, 2))
    hw_r = RR(sbuf_ring(K, "hhw", [128, 2, 256], F32, 2))
    w3v = w3s.rearrange("k (d c) -> k d c", d=2)[:, :, hc * 256:(hc + 1) * 256]
    for T in range(nsc):
        pf_, b_pf_ = K.ps_acc.next()
        pfv = pf_.rearrange("p (d c) -> p d c", d=2)
        P.I("pe", "matmul", reads=[b_w, b_h2], writes=[b_pf_], out=pfv, lhsT=hdn2T[:, T * 128:(T + 1) * 128], rhs=w3v, start=True, stop=True)
        win, b_win = win_r.next()
        P.I("act", "activation", reads=[b_dl, b_negt], writes=[b_win], out=win, in_=deltab[:, hc * 256:(hc + 1) * 256], func=AF.Exp, scale=negt[:, T:T + 1])
        hw, b_hw = hw_r.next()
        P.I("dve", "tensor_tensor", reads=[b_pf_, b_win], writes=[b_hw], out=hw, in0=pfv, in1=win.unsqueeze(1).broadcast_to([128, 2, 256]), op=ALU.mult)
        if T == 0:
            P.I("pool", "memset", reads=[], writes=[b_hw], partial=True, ap=hw[0:1, 1, :], constant=0.0)
        P.I("pool", "tensor_tensor", reads=[b_hw], writes=[b_sf], partial=True, out=sfilt[:, T, :], in0=hw[:, 0, :], in1=hw[:, 1, :], op=ALU.add)
        P.I("dve", "tensor_tensor", reads=[b_hw], writes=[b_df], partial=True, out=dfilt[:, T, :], in0=hw[:, 0, :], in1=hw[:, 1, :], op=ALU.subtract)
    barrier(K)
    A.reset(m_t)
    pf_r = RR(sbuf_ring(K, "hpf", [128, n], F32, 2))
    u_r = RR(sbuf_ring(K, "hu", [128, n], F32, 2))
    vvb = A.sb([128, n], BF16, "hvvb"); b_vvb = Buf("hvvb")

    def conv_chunk(c):
        pf, b_pf = pf_r.next()
        u, b_u = u_r.next()
        P.dma("sp", pf, pFM[c * 128:(c + 1) * 128, col0:col0 + n], reads=[K.b_pFM], writes=[b_pf], partial=False)
        P.I("act", "activation", reads=[b_pf, b_cw], writes=[b_u], out=u, in_=pf, func=AF.Identity, scale=cw[:, 12 + c:13 + c], bias=cw[:, 36 + c:37 + c])
        P.I("dve", "scalar_tensor_tensor", reads=[b_pf, b_cw, b_u], writes=[b_u], partial=True,
            out=u[:, 1:n], in0=pf[:, 0:n - 1], scalar=cw[:, c:c + 1], in1=u[:, 1:n], op0=ALU.mult, op1=ALU.add)
        P.I("dve", "scalar_tensor_tensor", reads=[b_pf, b_cw, b_u], writes=[b_u], partial=True,
            out=u[:, 0:n - 1], in0=pf[:, 1:n], scalar=cw[:, 24 + c:25 + c], in1=u[:, 0:n - 1], op0=ALU.mult, op1=ALU.add)
        return u, b_u
    for j in range(2):
        cc = 2 * hc + j
        ux, b_ux = conv_chunk(4 + cc)
        uv, b_uv = conv_chunk(8 + cc)
        P.I("pool", "tensor_tensor", reads=[b_ux, b_uv], writes=[b_uv], out=uv, in0=uv, in1=ux, op=ALU.mult)
        P.I("act", "copy", reads=[b_uv], writes=[b_vvb], out=vvb, in_=uv)
        P.I("dve", "tensor_copy", reads=[b_uv], writes=[b_vvo], partial=True, out=vv_own[:, j, :], in_=uv[:, 0:n_own])
        for T0 in range(0, nsc, 8):
            nT = min(8, nsc - T0)
            ps2, b_ps2 = K.ps_tp.next()
            psb = ps2.bitcast(BF16).rearrange("p (k t) -> p k t", k=8)
            for i in range(nT):
                P.I("pe", "transpose", reads=[b_vvb, K.b_const], writes=[b_ps2], partial=(i > 0),
                    out=psb[:, i, :], in_=vvb[:, (T0 + i) * 128:(T0 + i + 1) * 128], identity=K.ident)
            P.I("dve", "tensor_copy", reads=[b_ps2], writes=[b_vv], partial=True, out=vv_tm[:, T0:T0 + nT, j * 128:(j + 1) * 128], in_=psb[:, 0:nT, :])
        u0, b_u0 = conv_chunk(cc)
        P.I("act", "copy", reads=[b_u0], writes=[b_x0o], partial=True, out=x0_own[:, j, :], in_=u0[:, 0:n_own])
    barrier(K)
    A.reset(m_t)
    cst_r = RR(sbuf_ring(K, "hcst", [128, 2, nsc, 128], BF16, 3))
    ks_r = RR(sbuf_ring(K, "hks", [128, 2, 256], F32, 2))
    tt_r = RR(sbuf_ring(K, "htt", [128, 4, 256], F32, 2))
    ps8 = RR(K.banks[0:8])
    for fb in range(nfb):
        cst, b_cst = cst_r.next()
        P.dma("sp", cst[:, 0, :, :], G["CT"][fb], writes=[b_cst], partial=False)
        P.dma("sp", cst[:, 1, :, :], G["ST"][fb], writes=[b_cst])
        pa, b_pa = ps8.next(); pb, b_pb = ps8.next(); pck, b_pck = ps8.next(); pdk, b_pdk = ps8.next()
        for (po, b_po, ci, rhs, b_rhs) in ((pa, b_pa, 0, vv_tm, b_vv), (pb, b_pb, 1, vv_tm, b_vv), (pck, b_pck, 0, sfilt, b_sf), (pdk, b_pdk, 1, dfilt, b_df)):
            for k in range(nsc):
                P.I("pe", "matmul", reads=[b_cst, b_rhs], writes=[b_po], partial=(k > 0), out=po[:, 0:256], lhsT=cst[:, ci, k, :], rhs=rhs[:, k, :],
                    start=(k == 0), stop=(k == nsc - 1))
        ks, b_ks = ks_r.next()
        P.I("act", "copy", reads=[b_pck], writes=[b_ks], partial=True, out=ks[:, 0, :], in_=pck[:, 0:256])
        P.I("act", "activation", reads=[b_pdk, b_sgn], writes=[b_ks], partial=True, out=ks[:, 1, :], in_=pdk[:, 0:256], func=AF.Copy, scale=sgn[:, 0:1])
        tt, b_tt = tt_r.next()
        P.I("dve", "tensor_tensor", reads=[b_pa, b_ks], writes=[b_tt], partial=True, out=tt[:, 0, :], in0=pa[:, 0:256], in1=ks[:, 0, :], op=ALU.mult)
        P.I("dve", "tensor_tensor", reads=[b_pb, b_ks], writes=[b_tt], partial=True, out=tt[:, 1, :], in0=pb[:, 0:256], in1=ks[:, 1, :], op=ALU.mult)
        P.I("dve", "tensor_tensor", reads=[b_pa, b_ks], writes=[b_tt], partial=True, out=tt[:, 2, :], in0=pa[:, 0:256], in1=ks[:, 1, :], op=ALU.mult)
        P.I("dve", "tensor_tensor", reads=[b_pb, b_ks], writes=[b_tt], partial=True, out=tt[:, 3, :], in0=pb[:, 0:256], in1=ks[:, 0, :], op=ALU.mult)
        P.I("pool", "tensor_tensor", reads=[b_tt], writes=[b_Yre], partial=True, out=Yre[:, fb, :], in0=tt[:, 0, :], in1=tt[:, 1, :], op=ALU.subtract)
        P.I("pool", "tensor_tensor", reads=[b_tt], writes=[b_Yim], partial=True, out=Yim[:, fb, :], in0=tt[:, 2, :], in1=tt[:, 3, :], op=ALU.add)
    barrier(K)
    A.reset(m_t)
    FP = 11
    ic_r = RR(sbuf_ring(K, "hic", [128, 2, FP, 512], BF16, 3))
    yt_r = RR(sbuf_ring(K, "hyt", [128, 512], F32, 2))
    yo_r = RR(sbuf_ring(K, "hyo", [128, 512], BF16, 3))
    pieces = [(f0, min(FP, nfb - f0)) for f0 in range(0, nfb, FP)]
    for t0 in range(0, n_own, 512):
        tl = min(512, n_own - t0)
        accs = [K.ps_acc.next(), K.ps_acc.next()]
        for pi_, (f0, nf) in enumerate(pieces):
            ic, b_ic = ic_r.next()
            P.dma("sp", ic[:, 0, 0:nf, 0:tl], G["Ci"][f0 * 128:(f0 + nf) * 128, t0:t0 + tl].rearrange("(k p) t -> p k t", p=128), writes=[b_ic], partial=False)
            P.dma("sp", ic[:, 1, 0:nf, 0:tl], G["Si"][f0 * 128:(f0 + nf) * 128, t0:t0 + tl].rearrange("(k p) t -> p k t", p=128), writes=[b_ic])
            for j in range(2):
                acc, b_acc = accs[j]
                for k in range(nf):
                    first = (pi_ == 0 and k == 0)
                    last = (pi_ == len(pieces) - 1 and k == nf - 1)
                    P.I("pe", "matmul", reads=[b_Yre, b_ic], writes=[b_acc], partial=(not first), out=acc[:, 0:tl], lhsT=Yre[:, f0 + k, j * 128:(j + 1) * 128],
                        rhs=ic[:, 0, k, 0:tl], start=first, stop=False)
                    P.I("pe", "matmul", reads=[b_Yim, b_ic], writes=[b_acc], partial=True, out=acc[:, 0:tl], lhsT=Yim[:, f0 + k, j * 128:(j + 1) * 128],
                        rhs=ic[:, 1, k, 0:tl], start=False, stop=last)
        for j in range(2):
            acc, b_acc = accs[j]
            cc = 2 * hc + j
            yt, b_yt = yt_r.next()
            P.I("dve", "scalar_tensor_tensor", reads=[b_vvo, b_cw, b_acc], writes=[b_yt], out=yt[:, 0:tl], in0=vv_own[:, j, t0:t0 + tl], scalar=cw[:, 48 + cc:49 + cc],
                in1=acc[:, 0:tl], op0=ALU.mult, op1=ALU.add)
            yo, b_yo = yo_r.next()
            P.I("pool", "tensor_tensor", reads=[b_yt, b_x0o], writes=[b_yo], out=yo[:, 0:tl], in0=yt[:, 0:tl], in1=x0_own[:, j, t0:t0 + tl], op=ALU.mult)
            P.dma("sp", oT_d[1024 + cc * 128:1024 + (cc + 1) * 128, own_off + t0:own_off + t0 + tl], yo[:, 0:tl], reads=[b_yo], writes=[K.b_oT_d])
    barrier(K)


from concourse.bass_utils import run_bass_kernel_spmd

I32 = mybir.dt.int32
LAYER_KEYS = [("w_ada", [D, 6 * D]), ("b_ada", [1, 6 * D]), ("w_in", [D, C_END]),
              ("mla_q_norm", [1, 448]), ("mla_kv_norm", [1, 128]), ("mla_w_uq", [448, 768]), ("mla_w_ukv", [128, 1024]),
              ("gqa_q_norm", [1, 128]), ("gqa_k_norm", [1, 128]),
              ("hy_conv_w", [3, 1536]), ("hy_conv_b", [1, 1536]), ("hy_w1", [33, 64]), ("hy_b1", [1, 64]), ("hy_w2", [64, 64]), ("hy_b2", [1, 64]),
              ("hy_w3", [64, 1024]), ("hy_freq", [1, 64]), ("hy_skip", [1, 512]),
              ("mb_conv_w", [3, 1024]), ("mb_conv_b", [1, 1024]), ("mb_a_log", [1, 16]), ("mb_dt_bias", [1, 16]), ("mb_d", [1, 8]), ("mb_norm", [1, 512]),
              ("w_mgate", [4, D, D]), ("b_mgate", [4, D]), ("w_branch", [4, 512, D]), ("w_out", [D, D]),
              ("ln1_g", [1, D]), ("ln1_b", [1, D]), ("ln2_g", [1, D]), ("ln2_b", [1, D]),
              ("w_ffn_in", [D, 2 * FFH]), ("w_ffn_out", [FFH, D])]


def run_layer(K, W, S, xload, x_own_lat, x_own_ctx, out_lat, out_ctx, need_ctx):
    phase_A(K, S["c2"], W["w_ada"], W["b_ada"], S["mod_d"])
    phase_B(K, xload, S["mod_d"], W["w_in"], S["pTM"], S["pFM"], S["hT_d"])
    phase_mla(K, S["pTM"], W["mla_q_norm"], W["mla_kv_norm"], W["mla_w_uq"], W["mla_w_ukv"], S["ropeA"], S["oT_d"], need_ctx)
    phase_gqa(K, S["pTM"], W["gqa_q_norm"], W["gqa_k_norm"], S["ropeB"], S["oT_d"], need_ctx)
    phase_hyena(K, S["pFM"], W["hy_conv_w"], W["hy_conv_b"], W["hy_w1"], W["hy_b1"], W["hy_w2"], W["hy_b2"], W["hy_w3"], W["hy_freq"], W["hy_skip"],
                S["hy_sgn"], S["hy_deltas"], S["geoms"] if need_ctx else S["geoms"][:1], S["oT_d"])
    phase_mamba(K, S["pTM"], S["pFM"], W["mb_conv_w"], W["mb_conv_b"], W["mb_a_log"], W["mb_dt_bias"], W["mb_d"], W["mb_norm"],
                S["mb_tri"], S["mb_masks"], S["cumT_d"], S["oT_d"], need_ctx)
    phase_D(K, x_own_lat, x_own_ctx, S["hT_d"], S["oT_d"], S["mod_d"], W["w_mgate"], W["b_mgate"], W["w_branch"], W["w_out"], W["ln1_g"], W["ln1_b"],
            W["w_ffn_in"], W["w_ffn_out"], W["ln2_g"], W["ln2_b"], S["r_d"], S["xmid_d"], out_lat, out_ctx, need_ctx)


def set_bufs(K):
    for nm in ("mod_d", "pTM", "pFM", "hT_d", "oT_d", "r_d", "xmid_d", "out", "cumT_d"):
        setattr(K, "b_" + nm, Buf(nm))
    K.dbg = None


def build_fused():
    nc = bass.Bass("TRN2", target_bir_lowering=False)

    def dt(n, s, d=F32, kind="ExternalInput"):
        return nc.dram_tensor(n, s, d, kind=kind).ap()
    S = {}
    x_lat = dt("x_lat", [NLAT, D]); x_ctx = dt("x_ctx", [NCTX, D]); S["c2"] = dt("c2", [2, D])
    consts = {"ident_bf": dt("ident_bf", [128, 128], BF16), "ident_f": dt("ident_f", [128, 128])}
    S["ropeA"] = dt("ropeA", [NLAT, 2, 32]); S["ropeB"] = dt("ropeB", [NLAT, 2, 64])
    S["hy_sgn"] = dt("hy_sgn", [1, 1]); S["hy_deltas"] = dt("hy_deltas", [1, 512])
    geoms = []
    for nm, n, col0, own_off, n_own in (("L", 4096, 0, 0, 2048), ("C", 256, 4096, 2048, 128)):
        Fp = (n + 1 + 127) // 128 * 128
        geoms.append({"n": n, "col0": col0, "own_off": own_off, "n_own": n_own,
                      "CT": dt(f"hy{nm}_CT", [Fp // 128, 128, n // 128, 128], BF16), "ST": dt(f"hy{nm}_ST", [Fp // 128, 128, n // 128, 128], BF16),
                      "Ci": dt(f"hy{nm}_Ci", [Fp, n_own], BF16), "Si": dt(f"hy{nm}_Si", [Fp, n_own], BF16),
                      "featsT": dt(f"hy{nm}_featsT", [33, n]), "negt": dt(f"hy{nm}_negt", [128, n // 128])})
    S["geoms"] = geoms
    S["mb_tri"] = dt("mb_tri", [3, 128, 128]); S["mb_masks"] = dt("mb_masks", [2, 4, 128, 512], BF16)
    xidx_d = dt("xidx", [128, 17], I32)
    Ws = [{k: dt(f"{k}_{l}", shp) for k, shp in LAYER_KEYS} for l in range(2)]
    S["cumT_d"] = dt("cumT_d", [16, NTOK], F32, "Internal")
    S["mod_d"] = dt("mod_d", [2, 6 * D], F32, "Internal")
    S["pTM"] = dt("pTM", [NTOK, TMW], F32, "Internal")
    S["pFM"] = dt("pFM", [FMW, NTOK], F32, "Internal")
    S["hT_d"] = dt("hT_d", [128, 16, NOWN], BF16, "Internal")
    S["oT_d"] = dt("oT_d", [2048, NOWN], BF16, "Internal")
    S["r_d"] = dt("r_d", [NOWN, D], F32, "Internal")
    S["xmid_d"] = dt("xmid_d", [NOWN, D], F32, "Internal")
    xown_d = dt("xown_d", [NOWN, D], F32, "Internal")
    xall_d = dt("xall_d", [8 * NOWN, D], F32, "Internal")
    out_lat = dt("out_lat", [OWN_LAT, D], F32, "ExternalOutput")
    K = make_ctx(nc)
    set_bufs(K)
    load_consts(K, consts)

    def xload0(kind, t, xt, b_xt):
        src = x_lat if kind == "lat" else x_ctx
        K.P.dma("sp", xt, src[t * 128:(t + 1) * 128, :], writes=[b_xt], partial=False)
    run_layer(K, Ws[0], S, xload0, x_lat, x_ctx, xown_d[0:OWN_LAT, :], xown_d[OWN_LAT:NOWN, :], True)
    b_xall = Buf("xall")
    K.P.collective("AllGather", ALU.bypass, [list(range(8))], xown_d, xall_d, reads=[K.b_out], writes=[b_xall], inc=1)
    barrier(K)
    K.P.emit()
    K1 = make_ctx(nc, prev=K)
    set_bufs(K1)
    m0 = K1.A.mark()
    xidx = K1.A.sb([128, 17], I32, "xidx"); b_xidx = Buf("xidx")
    K1.P.dma("sp", xidx, xidx_d, writes=[b_xidx], partial=False)

    def xload1(kind, t, xt, b_xt):
        if kind == "lat" and t < 16:
            K1.P.dma("sp", xt, xown_d[t * 128:(t + 1) * 128, :], writes=[b_xt], partial=False)
        elif kind == "lat":
            K1.P.gather(xt, xall_d, xidx[:, t - 16:t - 15], reads=[b_xidx], writes=[b_xt])
        elif t == 0:
            K1.P.dma("sp", xt, xown_d[OWN_LAT:NOWN, :], writes=[b_xt], partial=False)
        else:
            K1.P.gather(xt, xall_d, xidx[:, 16:17], reads=[b_xidx], writes=[b_xt])
    run_layer(K1, Ws[1], S, xload1, xown_d[0:OWN_LAT, :], xown_d[OWN_LAT:NOWN, :], out_lat, None, False)
    K1.P.finish([K1.b_out], "sp")
    K1.P.emit()
    return nc


def kernel(**inp):
    f32 = np.float32
    x = np.asarray(inp["x"], f32)
    z = np.asarray(inp["ctx"], f32)
    c = np.asarray(inp["c"], f32)
    c_ctx = np.asarray(inp["c_ctx"], f32)
    ident_bf = np.eye(128, dtype=ml_dtypes.bfloat16)
    ident_f = np.eye(128, dtype=f32)
    ropes = [rope_tables(0), rope_tables(1)]
    hyc = {"L": hyena_consts(4096, 2048), "C": hyena_consts(256, 128)}
    tri, masks = mamba_consts()
    deltas = np.abs(np.linspace(math.log(1e-2) / 1.5, math.log(1e-2) / 0.3, 512, dtype=np.float32))[None]
    nc = build_fused()
    shapes = dict(LAYER_KEYS)
    in_maps = []
    for core in range(8):
        b, half = core // 2, core % 2
        m = {"x_lat": np.ascontiguousarray(x[b] if half == 0 else x[b][::-1]),
             "x_ctx": np.ascontiguousarray(z[b] if half == 0 else z[b][::-1]),
             "c2": np.stack([c[b], c_ctx]), "ident_bf": ident_bf, "ident_f": ident_f,
             "ropeA": ropes[half][0], "ropeB": ropes[half][1],
             "hy_sgn": np.full((1, 1), 1.0 if half == 0 else -1.0, f32), "hy_deltas": deltas, "mb_tri": tri, "mb_masks": masks}
        for nm in ("L", "C"):
            for k_, v_ in hyc[nm].items():
                m[f"hy{nm}_{k_}"] = v_
        partner = core ^ 1
        p = np.arange(128)
        idx = np.empty((128, 17), np.int32)
        for j in range(16):
            idx[:, j] = partner * NOWN + (OWN_LAT - 1 - 128 * j - p)
        idx[:, 16] = partner * NOWN + OWN_LAT + (OWN_CTX - 1 - p)
        m["xidx"] = idx
        for l in range(2):
            for k, shp in LAYER_KEYS:
                a = np.asarray(inp[k][l], f32)
                if half == 1:
                    if k == "w_in":
                        a = np.concatenate([a[:, :C_DT], a[:, C_DT + 8:C_DT + 16], a[:, C_DT:C_DT + 8]], 1)
                    elif k in ("hy_conv_w", "mb_conv_w", "mb_a_log", "mb_dt_bias"):
                        a = a[::-1]
                m[f"{k}_{l}"] = np.ascontiguousarray(a).reshape(shp)
        in_maps.append(m)
    res = run_bass_kernel_spmd(nc, in_maps, core_ids=list(range(8)))
    out = np.empty_like(x)
    for core in range(8):
        b, half = core // 2, core % 2
        ol = res.results[core]["out_lat"]
        if half == 0:
            out[b, :OWN_LAT] = ol
        else:
            out[b, OWN_LAT:] = ol[::-1]
    return out.astype(np.float32)
```
